# Optimizing a Trainium2 kernel written in Bass

```python
import jax, jax.numpy as jnp
from jax import lax
import numpy as np

D_MODEL = 1024
BATCH = 4
SEQ = 4096
DEPTH = 1

N_META = 16
LRU_WIDTH = 1024
LRU_HEADS = 8
LRU_BLOCK = LRU_WIDTH // LRU_HEADS
CONV_WIDTH = 4
LRU_C = 8.0
RET_HEADS = 8
RET_QK_DIM = 64
RET_V_DIM = 128
RET_QK_WIDTH = RET_HEADS * RET_QK_DIM
RET_WIDTH = RET_HEADS * RET_V_DIM
CHUNK = 128
ROPE_BASE = 10000.0
MIX_WIDTH = LRU_WIDTH + RET_WIDTH
SPLIT_SIZES = (LRU_WIDTH, LRU_WIDTH, RET_QK_WIDTH, RET_QK_WIDTH, RET_WIDTH, RET_WIDTH)
IN_WIDTH = sum(SPLIT_SIZES)
EPS = 1e-6

kernel_name = 'hymba_style_rglru_retention_hybrid'


def _rmsnorm(x, g):
    xf = x.astype(jnp.float32)
    y = xf * lax.rsqrt(jnp.mean(xf * xf, axis=-1, keepdims=True) + EPS)
    return (y * g.astype(jnp.float32)).astype(x.dtype)


def _causal_conv(x, w, b):
    T = x.shape[1]
    xp = jnp.pad(x, ((0, 0), (CONV_WIDTH - 1, 0), (0, 0)))
    y = b
    for k in range(CONV_WIDTH):
        y = y + xp[:, k:k + T] * w[k]
    return y


def _block_diag(x, w, b):
    B, T, _ = x.shape
    xh = x.reshape(B, T, LRU_HEADS, LRU_BLOCK)
    return jnp.einsum('bthi,hij->bthj', xh, w).reshape(B, T, LRU_WIDTH) + b


def _rg_lru(x, w_rg, b_rg, w_ig, b_ig, lam):
    r = jax.nn.sigmoid(_block_diag(x, w_rg, b_rg).astype(jnp.float32))
    i = jax.nn.sigmoid(_block_diag(x, w_ig, b_ig).astype(jnp.float32))
    log_a = -LRU_C * r * jax.nn.softplus(-lam.astype(jnp.float32))
    a = jnp.exp(log_a)
    beta = jnp.sqrt(-jnp.expm1(2.0 * log_a))
    u = beta * i * x.astype(jnp.float32)

    def combine(lhs, rhs):
        a1, b1 = lhs
        a2, b2 = rhs
        return a1 * a2, a2 * b1 + b2

    _, h = lax.associative_scan(combine, (a, u), axis=1)
    return h.astype(x.dtype)


def _rotary(t, pos):
    half = RET_QK_DIM // 2
    inv = ROPE_BASE ** (-jnp.arange(half, dtype=jnp.float32) / half)
    ang = pos.astype(jnp.float32)[:, None] * inv[None, :]
    cos = jnp.cos(ang)[None, :, None, :]
    sin = jnp.sin(ang)[None, :, None, :]
    t1, t2 = t[..., :half], t[..., half:]
    return jnp.concatenate([t1 * cos - t2 * sin, t1 * sin + t2 * cos], axis=-1)


def _retention(q, k, v):
    B, T, H, _ = q.shape
    pad = (-N_META) % CHUNK
    widths = ((0, 0), (pad, 0), (0, 0), (0, 0))
    q, k, v = jnp.pad(q, widths), jnp.pad(k, widths), jnp.pad(v, widths)
    n = (T + pad) // CHUNK
    q = q.reshape(B, n, CHUNK, H, RET_QK_DIM)
    k = k.reshape(B, n, CHUNK, H, RET_QK_DIM)
    v = v.reshape(B, n, CHUNK, H, RET_V_DIM)
    log_g = jnp.log1p(-jnp.exp2(-5.0 - jnp.arange(RET_HEADS, dtype=jnp.float32)))
    idx = jnp.arange(CHUNK, dtype=jnp.float32)
    diff = idx[:, None] - idx[None, :]
    dmask = jnp.where(diff[None] >= 0.0,
                      jnp.exp(jnp.maximum(diff, 0.0)[None] * log_g[:, None, None]), 0.0)
    s = jnp.einsum('bnchd,bnmhd->bnhcm', q, k) * dmask
    inner = jnp.einsum('bnhcm,bnmhe->bnche', s, v)
    k_dec = k * jnp.exp((CHUNK - 1.0 - idx)[:, None] * log_g[None, :])[:, :, None]
    kv = jnp.einsum('bnchd,bnche->bnhde', k_dec, v)
    g_chunk = jnp.exp(CHUNK * log_g)[None, :, None, None]

    def step(state, kv_n):
        return g_chunk * state + kv_n, state

    init = jnp.zeros((B, H, RET_QK_DIM, RET_V_DIM), jnp.float32)
    _, r_prev = lax.scan(step, init, jnp.moveaxis(kv, 1, 0))
    r_prev = jnp.moveaxis(r_prev, 0, 1)
    q_dec = q * jnp.exp((idx + 1.0)[:, None] * log_g[None, :])[:, :, None]
    cross = jnp.einsum('bnchd,bnhde->bnche', q_dec, r_prev)
    o = (inner + cross).reshape(B, n * CHUNK, H, RET_V_DIM)
    return o[:, pad:]


def _head_norm(o, g):
    mu = jnp.mean(o, axis=-1, keepdims=True)
    oc = o - mu
    var = jnp.mean(oc * oc, axis=-1, keepdims=True)
    return oc * lax.rsqrt(var + EPS) * g.astype(jnp.float32).reshape(RET_HEADS, RET_V_DIM)


def setup_inputs(seed: int = 0) -> dict:
    key = jax.random.key(seed)
    ks = jax.random.split(key, 16)
    f32 = jnp.float32
    x = jax.random.normal(ks[0], (BATCH, SEQ, D_MODEL), f32)
    meta_tokens = jax.random.normal(ks[1], (N_META, D_MODEL), f32)
    norm_gain = 1.0 + 0.01 * jax.random.normal(ks[2], (DEPTH, D_MODEL), f32)
    w_in = jax.random.normal(ks[3], (DEPTH, D_MODEL, IN_WIDTH), f32) * D_MODEL ** -0.5
    conv_w = jax.random.normal(ks[4], (DEPTH, CONV_WIDTH, LRU_WIDTH), f32) * CONV_WIDTH ** -0.5
    conv_b = 0.01 * jax.random.normal(ks[5], (DEPTH, LRU_WIDTH), f32)
    w_rg = jax.random.normal(ks[6], (DEPTH, LRU_HEADS, LRU_BLOCK, LRU_BLOCK), f32) * LRU_BLOCK ** -0.5
    b_rg = 0.01 * jax.random.normal(ks[7], (DEPTH, LRU_WIDTH), f32)
    w_ig = jax.random.normal(ks[8], (DEPTH, LRU_HEADS, LRU_BLOCK, LRU_BLOCK), f32) * LRU_BLOCK ** -0.5
    b_ig = 0.01 * jax.random.normal(ks[9], (DEPTH, LRU_WIDTH), f32)
    ac = jax.random.uniform(ks[10], (DEPTH, LRU_WIDTH), f32, minval=0.9, maxval=0.999)
    a = ac ** (1.0 / LRU_C)
    lru_lambda = jnp.log(a) - jnp.log1p(-a)
    ret_norm_gain = 1.0 + 0.01 * jax.random.normal(ks[11], (DEPTH, RET_WIDTH), f32)
    w_out = jax.random.normal(ks[12], (DEPTH, MIX_WIDTH, D_MODEL), f32) * MIX_WIDTH ** -0.5
    final_norm_gain = 1.0 + 0.01 * jax.random.normal(ks[13], (D_MODEL,), f32)
    return {'x': x, 'meta_tokens': meta_tokens, 'norm_gain': norm_gain, 'w_in': w_in,
            'conv_w': conv_w, 'conv_b': conv_b, 'w_rg': w_rg, 'b_rg': b_rg,
            'w_ig': w_ig, 'b_ig': b_ig, 'lru_lambda': lru_lambda,
            'ret_norm_gain': ret_norm_gain, 'w_out': w_out, 'final_norm_gain': final_norm_gain}


def reference(x, meta_tokens, norm_gain, w_in, conv_w, conv_b, w_rg, b_rg, w_ig, b_ig,
              lru_lambda, ret_norm_gain, w_out, final_norm_gain):
    B = x.shape[0]
    meta = jnp.broadcast_to(meta_tokens.astype(x.dtype)[None], (B, N_META, D_MODEL))
    h = jnp.concatenate([meta, x], axis=1)
    T = h.shape[1]
    pos = jnp.arange(T)
    split_idx = np.cumsum(SPLIT_SIZES)[:-1].tolist()
    for l in range(DEPTH):
        u = _rmsnorm(h, norm_gain[l])
        proj = jnp.einsum('btd,de->bte', u, w_in[l])
        lru_x, lru_gate, q, k, v, ret_gate = jnp.split(proj, split_idx, axis=-1)
        xc = _causal_conv(lru_x, conv_w[l], conv_b[l])
        y_lru = _rg_lru(xc, w_rg[l], b_rg[l], w_ig[l], b_ig[l], lru_lambda[l]) * jax.nn.silu(lru_gate)
        qh = _rotary(q.reshape(B, T, RET_HEADS, RET_QK_DIM).astype(jnp.float32), pos)
        kh = _rotary(k.reshape(B, T, RET_HEADS, RET_QK_DIM).astype(jnp.float32), pos) * RET_QK_DIM ** -0.5
        vh = v.reshape(B, T, RET_HEADS, RET_V_DIM).astype(jnp.float32)
        o = _head_norm(_retention(qh, kh, vh), ret_norm_gain[l])
        y_ret = o.reshape(B, T, RET_WIDTH).astype(x.dtype) * jax.nn.silu(ret_gate)
        y = jnp.concatenate([y_lru, y_ret], axis=-1)
        h = h + jnp.einsum('bte,ed->btd', y, w_out[l])
    return _rmsnorm(h, final_norm_gain)[:, N_META:]
```

```python
import numpy as np
import ml_dtypes
import concourse.bass as bass
import concourse.mybir as mybir
from concourse.bass_utils import run_bass_kernel_spmd

F32 = mybir.dt.float32
BF16 = mybir.dt.bfloat16
ALU = mybir.AluOpType
AF = mybir.ActivationFunctionType

D_MODEL = 1024
BATCH = 4
SEQ = 4096
N_META = 16
CH = 128
NCHUNK = 33
EPS = 1e-6
N_SCHED_TRIALS = 300
TILE_NCH = [1, 4, 4, 4, 4, 4, 4, 4, 2, 2]
TILE_CHUNKS = []
_c = 0
for _n in TILE_NCH:
    TILE_CHUNKS.append(list(range(_c, _c + _n)))
    _c += _n
assert _c == NCHUNK
NTILES = len(TILE_CHUNKS)
TILE_T0 = [None] + [(TILE_CHUNKS[s][0] - 1) * 128 for s in range(1, NTILES)]
TILE_NT = [128 * n for n in TILE_NCH]

V_CW = 0
V_CB = 16
V_BRG = 20
V_BIG = 24
V_LAM = 28
V_NG = 32
V_RG = 40
NV = 48
DC_Q = 0
DC_KI = 4
DC_KD = 8
DC_G128 = 12
NDEC = 16


class Src:
    def __init__(self, name, sem, step):
        self.name, self.sem, self.step = name, sem, step
        self.count = 0
        self.snap = {}


class Eng(Src):
    def __init__(self, name, sem, handle):
        super().__init__(name, sem, 1)
        self.h = handle
        self.know = {}
        self.nwaits = 0
        self.nins = 0


class T:
    def __init__(self, ap, name=""):
        self.ap = ap
        self.name = name
        self.w = {}
        self.r = {}
        self.lw = None
        self.lr = []

    def __getitem__(self, k):
        return self.ap[k]


class _Proxy:
    def __init__(self):
        self.calls = []

    def __getattr__(self, name):
        def f(*a, **k):
            self.calls.append((name, a, k))
            return None
        return f


def _free_size(ap):
    n = 1
    for d in ap.shape[1:]:
        n *= int(d)
    return n


class Op:
    __slots__ = ("prio", "tbl", "eng", "calls", "reads", "writes", "kind", "ch", "dur", "lat", "idx", "deps", "nsucc", "succ", "ndeps", "start", "fin", "closed", "bytes")


class FW:
    DMA_BW = 150e9
    TBL_SWITCH = 2.7e-6

    def __init__(self, nc):
        self.nc = nc
        self.pe = self._eng("pe", nc.tensor)
        self.act = self._eng("act", nc.scalar)
        self.dve = self._eng("dve", nc.vector)
        self.pool = self._eng("pool", nc.gpsimd)
        self.sp = self._eng("sp", nc.sync)
        self.ops = []
        self.open = None
        self.cur_prio = 0

    def _eng(self, name, handle):
        return Eng(name, self.nc.alloc_semaphore("s_" + name), handle)

    def chan(self, name, step=16):
        return Src(name, self.nc.alloc_semaphore("c_" + name), step)

    def _est(self, eng, name, a, k):
        out = k.get("out", a[0] if a else None)
        n = _free_size(out) if out is not None and hasattr(out, "shape") else 64
        if eng is self.pe:
            return max(64, n) / 2.0e9 + 0.02e-6
        if eng is self.act:
            return 0.30e-6 + n * 0.84e-9
        if eng is self.dve:
            return 0.17e-6 + n * 1.05e-9
        return 0.32e-6 + n * 1.5e-9

    def _new(self, eng, kind):
        o = Op()
        o.eng, o.kind, o.calls, o.reads, o.writes = eng, kind, [], [], []
        o.ch, o.dur, o.lat, o.idx, o.closed, o.bytes = None, 0.0, 0.0, len(self.ops), True, 0
        o.tbl = None
        o.prio = self.cur_prio
        self.ops.append(o)
        return o

    def op(self, eng, fn, reads=(), writes=(), inc=True):
        p = _Proxy()
        fn(p)
        if self.open is not None and self.open.eng is eng:
            o = self.open
        else:
            assert self.open is None, "unterminated inc=False group"
            o = self._new(eng, "op")
        for (name, a, k) in p.calls:
            o.calls.append((name, a, k))
            o.dur += self._est(eng, name, a, k)
            if name == "activation":
                f_ = k.get("func")
                if f_ == AF.Sqrt:
                    o.tbl = "sqrt"
                elif f_ in (AF.Exp, AF.Tanh):
                    o.tbl = "exp"
                elif f_ == AF.Ln:
                    o.tbl = "ln"
        for t in reads:
            if t not in o.reads:
                o.reads.append(t)
        for t in writes:
            if t not in o.writes:
                o.writes.append(t)
        self.open = None if inc else o

    def dma(self, q, ch, out, in_, reads=(), writes=(), **kw):
        assert self.open is None
        o = self._new(q, "dma")
        o.calls.append(("dma_start", (), dict(out=out, in_=in_, **kw)))
        o.ch = ch
        o.reads, o.writes = list(reads), list(writes)
        nbytes = 1
        for d in out.shape:
            nbytes *= int(d)
        o.bytes = nbytes * (2 if out.dtype == BF16 else 4)
        o.dur = 0.08e-6 if q is self.sp else 1.0e-6
        o.lat = 2.0e-6

    def custom(self, q, ch, fn, reads=(), writes=(), lat=25e-6):
        assert self.open is None
        p = _Proxy()
        fn(p)
        o = self._new(q, "custom")
        o.calls = list(p.calls)
        o.ch = ch
        o.reads, o.writes = list(reads), list(writes)
        o.dur = 1.0e-6
        o.lat = lat

    def finish(self, q, tiles):
        assert self.open is None
        o = self._new(q, "finish")
        o.reads = list(tiles)
        o.dur = 0.05e-6

    def _all_tiles(self):
        seen = {}
        for o in self.ops:
            for t in o.reads:
                seen[id(t)] = t
            for t in o.writes:
                seen[id(t)] = t
        return seen.values()

    def schedule(self, seed=None, noise=0.0, win=0.3e-6):
        import random
        rng = random.Random(seed)
        ops = self.ops
        for t_ in self._all_tiles():
            t_.lw = None
            t_.lr = []
        for o in ops:
            deps = set()
            for t in o.reads:
                if t.lw is not None:
                    deps.add(t.lw)
            for t in o.writes:
                if t.lw is not None:
                    deps.add(t.lw)
                for r in t.lr:
                    deps.add(r)
            deps.discard(o)
            o.deps = deps
            for t in o.reads:
                t.lr.append(o)
            for t in o.writes:
                t.lw = o
                t.lr = []
        for o in ops:
            o.succ = []
            o.ndeps = len(o.deps)
            o.start = None
        for o in ops:
            for d in o.deps:
                d.succ.append(o)
        engs = [self.pe, self.act, self.dve, self.pool, self.sp]
        bl = {}
        for o in reversed(ops):
            m = 0.0
            for s_ in o.succ:
                if bl[s_] > m:
                    m = bl[s_]
            extra = o.lat + (o.bytes / self.DMA_BW if o.kind == "dma" else 0.0)
            bl[o] = m + o.dur + extra
        if noise > 0.0:
            blp = {o: v * (1.0 + noise * (rng.random() - 0.5)) for o, v in bl.items()}
        else:
            blp = bl
        free = {e: 0.0 for e in engs}
        dma_free = 0.0
        ready = {e: [] for e in engs}
        for o in ops:
            if o.ndeps == 0:
                ready[o.eng].append(o)
        order = []
        n = len(ops)
        WIN = win
        cur_tbl = ["exp"]
        self.n_switch = 0
        while len(order) < n:
            best = None
            for e in engs:
                fe = free[e]
                cands = []
                mn = None
                for o in ready[e]:
                    st = fe
                    if o.tbl is not None and e is self.act and o.tbl != cur_tbl[0]:
                        st = fe + self.TBL_SWITCH
                    for d in o.deps:
                        if d.fin > st:
                            st = d.fin
                    cands.append((st, o))
                    if mn is None or st < mn:
                        mn = st
                if mn is None:
                    continue
                pick = None
                for st, o in cands:
                    if st <= mn + WIN:
                        if pick is None or (o.prio, blp[o]) > (pick[1].prio, blp[pick[1]]):
                            pick = (st, o)
                key = (pick[0], -blp[pick[1]])
                if best is None or key < best[0]:
                    best = (key, pick[1], pick[0])
            _, o, st = best
            ready[o.eng].remove(o)
            o.start = st
            free[o.eng] = st + o.dur
            if o.tbl is not None and o.eng is self.act and o.tbl != cur_tbl[0]:
                cur_tbl[0] = o.tbl
                self.n_switch += 1
            if o.kind == "dma":
                t0 = max(st + o.dur, dma_free)
                dma_free = t0 + o.bytes / self.DMA_BW
                o.fin = dma_free + o.lat
            elif o.kind == "custom":
                o.fin = st + o.dur + o.lat
            else:
                o.fin = st + o.dur
            order.append(o)
            for s_ in o.succ:
                s_.ndeps -= 1
                if s_.ndeps == 0:
                    ready[s_.eng].append(s_)
        self.sim_end = max(o.fin for o in ops)
        self.sim_busy = {e.name: sum(o.dur for o in ops if o.eng is e) for e in engs}
        return order

    def _merge(self, eng, src, idx):
        if eng.know.get(src, 0) < idx:
            eng.know[src] = idx
        for s2, v in src.snap.get(idx, {}).items():
            if eng.know.get(s2, 0) < v:
                eng.know[s2] = v

    def _waits(self, eng, reads, writes):
        needs = {}
        for t in reads:
            for s, i in t.w.items():
                if needs.get(s, 0) < i:
                    needs[s] = i
        for t in writes:
            for d in (t.w, t.r):
                for s, i in d.items():
                    if needs.get(s, 0) < i:
                        needs[s] = i
        for s, i in sorted(needs.items(), key=lambda kv: kv[0] is eng):
            if eng.know.get(s, 0) >= i:
                continue
            eng.h.wait_ge(s.sem, i * s.step)
            eng.nwaits += 1
            self._merge(eng, s, i)

    def _record(self, src, idx, reads, writes):
        for t in reads:
            if t.r.get(src, 0) < idx:
                t.r[src] = idx
        for t in writes:
            t.w = {src: idx}
            t.r = {}

    def emit(self, order):
        for o in order:
            eng = o.eng
            self._waits(eng, o.reads, o.writes)
            if o.kind == "finish":
                continue
            ins = None
            for (name, a, k) in o.calls:
                ins = getattr(eng.h, name)(*a, **k)
                eng.nins += 1
            if o.kind == "op":
                eng.count += 1
                ins.then_inc(eng.sem, 1)
                eng.snap[eng.count] = dict(eng.know)
                self._record(eng, eng.count, o.reads, o.writes)
            else:
                ch = o.ch
                ch.count += 1
                ins.then_inc(ch.sem, ch.step)
                ch.snap[ch.count] = dict(eng.know)
                self._record(ch, ch.count, o.reads, o.writes)


class Rot:
    def __init__(self, tiles):
        self.tiles = tiles
        self.i = 0

    def next(self):
        t = self.tiles[self.i % len(self.tiles)]
        self.i += 1
        return t


def build_program():
    nc = bass.Bass("TRN2", target_bir_lowering=False)
    fw = FW(nc)
    PE, ACT, DVE, POOL, SP = fw.pe, fw.act, fw.dve, fw.pool, fw.sp

    def din(name, shape, dt):
        return nc.dram_tensor(name, shape, dt, kind="ExternalInput").ap()

    x_d = din("x", [SEQ, D_MODEL], F32)
    meta_d = din("meta", [N_META, D_MODEL], F32)
    xres_d = din("xres", [SEQ // 2, D_MODEL], F32)
    win_d = din("w_in", [D_MODEL, 2560], F32)
    wout_d = din("w_out", [2048, D_MODEL], F32)
    wg_d = din("w_g", [128, 2, 4, 128], F32)
    vecs_d = din("vecs", [128, NV], F32)
    fgain_d = din("fgain", [128, D_MODEL], F32)
    ident_d = din("ident", [128, 128], BF16)
    cs_d = din("cs", [128, NCHUNK, 64], F32)
    dec_d = din("dec", [128, NDEC], F32)
    mask_d = din("maskT", [128, 128], BF16)
    out_d = nc.dram_tensor("out", [SEQ // 2, D_MODEL], F32, kind="ExternalOutput").ap()
    ysrc_d = [None] + [nc.dram_tensor(f"ysrc{s}", [1024, TILE_NT[s]], BF16) for s in range(1, NTILES)]
    ydst_d = [None] + [nc.dram_tensor(f"ydst{s}", [2048, TILE_NT[s]], BF16) for s in range(1, NTILES)]
    ysrc_t = [None] + [T(ysrc_d[s].ap(), f"ysrc{s}") for s in range(1, NTILES)]
    ydst_t = [None] + [T(ydst_d[s].ap(), f"ydst{s}") for s in range(1, NTILES)]
    out_t = T(out_d, "out")

    sb_bytes = [0]

    def sb(name, shape, dt):
        n = 1
        for d in shape[1:]:
            n *= d
        sb_bytes[0] += n * (2 if dt == BF16 else 4)
        return T(nc.alloc_sbuf_tensor(name, shape, dt).ap(), name)

    w_in_bf = sb("w_in_bf", [128, 8, 2560], BF16)
    w_in_cb = [T(w_in_bf.ap[:, :, cb * 512:(cb + 1) * 512], f"w_in_cb{cb}") for cb in range(5)]
    w_out_bf = sb("w_out_bf", [128, 16, 1024], BF16)
    wg_bf = sb("wg_bf", [128, 2, 4, 128], BF16)
    diag = sb("diag", [128, 4, 4, 128], BF16)
    fgain = sb("fgain_sb", [128, D_MODEL], F32)
    cs_rot = Rot([sb(f"cs_sb{i}", [128, 4, 64], F32) for i in range(2)])
    cs_of_tile = {}
    ident = sb("ident_sb", [128, 128], BF16)
    maskT = sb("mask_sb", [128, 128], BF16)
    vecs = sb("vecs_sb", [128, NV], F32)
    dec = sb("dec_sb", [128, NDEC], F32)
    negh = sb("negh", [128, 16], F32)
    posh = sb("posh", [128, 1], F32)
    hv = sb("hv", [128, 24], F32)
    xs_rot = Rot([sb(f"xs{i}", [128, D_MODEL], F32) for i in range(2)])
    xn_rot = Rot([sb(f"xn{i}", [128, D_MODEL], BF16) for i in range(2)])
    xnT_rot = [sb(f"xnT{i}", [128, 8, 512], BF16) for i in range(2)]
    ssq_rot = Rot([sb(f"ssq{i}", [128, 4], F32) for i in range(2)])
    rstd_rot = Rot([sb(f"rstd{i}", [128, 4], F32) for i in range(2)])
    lx = [sb(f"lx{i}", [128, 4, 515], BF16) for i in range(2)]
    thg = sb("thg", [128, 512], BF16)
    ghalf = sb("ghalf", [128, 512], BF16)
    sg_all = [[sb(f"sg{i}_{h}", [128, 512], BF16) for h in range(4)] for i in range(2)]
    NSET = 2
    xcb = Rot([sb(f"xcb{i}", [128, 512], BF16) for i in range(NSET)])
    xch = Rot([sb(f"xch{i}", [128, 512], F32) for i in range(NSET)])
    thr = Rot([sb(f"thr{i}", [128, 512], F32) for i in range(NSET)])
    thi = Rot([sb(f"thi{i}", [128, 512], F32) for i in range(NSET)])
    a_t = Rot([sb(f"a{i}", [128, 512], F32) for i in range(NSET)])
    a2_t = Rot([sb(f"a2{i}", [128, 512], F32) for i in range(NSET)])
    hst = [sb(f"hst{h}", [128, 1], F32) for h in range(4)]
    yT_all = nc.alloc_sbuf_tensor("yT_all", [128, 8, 512], BF16).ap()
    yT = [T(yT_all[:, k, :], f"yT{k}") for k in range(8)]
    qk_sb = Rot([sb(f"qk_sb{i}", [128, 512], F32) for i in range(1)])
    tmpA = sb("tmpA", [128, 256], F32)
    tmpB = sb("tmpB", [128, 256], F32)
    qk_rot = sb("qk_rot", [128, 512], F32)
    qd = Rot([sb(f"qd{i}", [128, 4, 64], BF16) for i in range(2)])
    kd = Rot([sb(f"kd{i}", [128, 4, 64], BF16) for i in range(2)])
    kdec = Rot([sb(f"kdec{i}", [128, 4, 64], BF16) for i in range(2)])
    v_bf = Rot([sb(f"v_bf{i}", [128, 512], BF16) for i in range(2)])
    thrg = thg
    rghalf = ghalf
    sgr_rot = Rot([sb(f"sgr{j}", [128, 512], BF16) for j in range(2)])
    qkT = Rot([sb(f"qkT{i}", [64, 8, 128], BF16) for i in range(2)])
    sm = Rot([sb(f"sm{i}", [128, 4, 128], BF16) for i in range(2)])
    o_sb_rot = Rot([sb(f"o_sb{j}", [128, 4, 128], F32) for j in range(2)])
    bst = sb("bst", [128, 4, 6], F32)
    bst_h = [T(bst.ap[:, h, :], f"bst{h}") for h in range(4)]
    mv_rot = Rot([sb(f"mv{i}", [128, 4, 2], F32) for i in range(2)])
    for t_ in mv_rot.tiles:
        t_.parts = [T(t_.ap[:, h, :], t_.name + f"_{h}") for h in range(4)]
    rs_rot = Rot([sb(f"rs{i}", [128, 4], F32) for i in range(2)])
    on = sb("on", [128, 4, 128], BF16)
    yr = Rot([sb(f"yr{i}", [128, 4, 128], BF16) for i in range(2)])
    R = sb("R", [64, 4, 128], F32)
    R_h = [T(R.ap[:, h, :], f"R{h}") for h in range(4)]
    Rbf = sb("Rbf", [64, 4, 128], BF16)
    yfull = sb("yfull", [128, 16, 256], BF16)
    xr_rot = Rot([sb(f"xr{i}", [128, 2, D_MODEL], F32) for i in range(1)])
    fss4 = sb("fss4", [128, 4], F32)
    frs = sb("frs", [128, 2], F32)

    banks = [T(nc.alloc_psum_tensor(f"pb{i}", [128, 512], F32).ap(), f"pb{i}") for i in range(8)]
    p_in = Rot(banks[0:2])
    p_out = Rot(banks[2:3])
    p_tr = Rot(banks[3:4])
    p_lru = Rot(banks[4:6])
    p_ret = Rot(banks[6:8])

    def bfv(bank):
        return bank.ap.bitcast(BF16)

    c_const = fw.chan("const")
    c_stage = [fw.chan("stage0"), fw.chan("stage1")]
    c_x = [fw.chan("x0"), fw.chan("x1")]
    c_cs = [fw.chan("cs0"), fw.chan("cs1")]
    c_ysrc = fw.chan("ysrc")
    c_cc = fw.chan("cc", step=1)
    c_yfull = fw.chan("yfull")
    c_xr = [fw.chan("xr0"), fw.chan("xr1")]
    c_out = fw.chan("out")
    for t_, i_ in zip(xs_rot.tiles, range(2)):
        t_.chan = c_x[i_]
    for t_, i_ in zip(xr_rot.tiles, range(2)):
        t_.chan = c_xr[i_]
    class _Stg:
        pass
    stg = []
    for i_, (t_, v_) in enumerate(((xr_rot.tiles[0], xr_rot.tiles[0].ap.rearrange("p a d -> p (a d)")),
                                   (yfull, yfull.ap.rearrange("p k t -> p (k t)").bitcast(F32)))):
        g_ = _Stg()
        g_.t, g_.v, g_.schan = t_, v_, c_stage[i_]
        stg.append(g_)
    stg_rot = Rot(stg)

    for dst, src in ((ident, ident_d), (maskT, mask_d), (vecs, vecs_d), (dec, dec_d), (fgain, fgain_d)):
        fw.dma(SP, fw.chan("const_" + dst.name), dst[:], src, writes=[dst])

    fw.op(DVE, lambda e: e.tensor_scalar_mul(out=hv[:, 0:12], in0=vecs[:, V_CB:V_CB + 12], scalar1=0.5), reads=[vecs], writes=[hv])
    fw.op(ACT, lambda e: e.activation(out=hv[:, 20:24], in_=vecs[:, V_LAM:V_LAM + 4], func=AF.Exp, scale=-1.0), reads=[vecs, hv], writes=[hv])
    fw.op(ACT, lambda e: e.activation(out=hv[:, 20:24], in_=hv[:, 20:24], func=AF.Ln, bias=1.0), reads=[hv], writes=[hv])
    fw.op(DVE, lambda e: e.tensor_scalar_mul(out=hv[:, 12:16], in0=hv[:, 20:24], scalar1=-4.0), reads=[hv], writes=[hv])
    fw.op(DVE, lambda e: e.tensor_scalar_mul(out=hv[:, 16:20], in0=hv[:, 20:24], scalar1=-8.0), reads=[hv], writes=[hv])
    for h in range(4):
        for k in range(4):
            fw.op(DVE, lambda e: e.tensor_scalar_mul(out=diag[:, h, k, :], in0=ident[:], scalar1=vecs[:, V_CW + h * 4 + k:V_CW + h * 4 + k + 1]),
                  reads=[ident, vecs], writes=[diag])
    fw.op(POOL, lambda e: e.memset(negh[:], -0.5), writes=[negh])
    fw.op(POOL, lambda e: e.memset(posh[:], 0.5), writes=[posh])
    fw.op(DVE, lambda e: e.memset(R[:], 0.0), writes=R_h)
    fw.op(DVE, lambda e: e.memset(Rbf[:], 0.0), writes=[Rbf])
    fw.op(DVE, lambda e: e.memset(lx[0][:], 0.0), writes=[lx[0]])
    fw.op(DVE, lambda e: e.memset(lx[1][:], 0.0), writes=[lx[1]])
    for h in range(4):
        fw.op(DVE, lambda e: e.memset(hst[h][:], 0.0), writes=[hst[h]])

    cast_rr = [0]

    def cast(out_ap, in_ap, scalar_ap, reads, writes):
        k = cast_rr[0] % 2
        cast_rr[0] += 1
        if k == 0:
            if scalar_ap is None:
                fw.op(ACT, lambda e: e.activation(out=out_ap, in_=in_ap, func=AF.Copy), reads=reads, writes=writes)
            else:
                fw.op(ACT, lambda e: e.activation(out=out_ap, in_=in_ap, func=AF.Copy, scale=scalar_ap), reads=reads, writes=writes)
        else:
            eng = DVE if k == 1 else POOL
            if scalar_ap is None:
                fw.op(eng, lambda e: e.tensor_copy(out=out_ap, in_=in_ap), reads=reads, writes=writes)
            else:
                fw.op(eng, lambda e: e.tensor_scalar_mul(out=out_ap, in0=in_ap, scalar1=scalar_ap), reads=reads, writes=writes)

    win_v = win_d.rearrange("(dc p) c -> p dc c", p=128)
    for cb in (2, 3, 4, 0, 1):
        for half in range(2):
            sg_ = stg_rot.next()
            st, stv = sg_.t, sg_.v
            fw.dma(SP, sg_.schan, stv.rearrange("p (a c) -> p a c", a=4), win_v[:, 4 * half:4 * half + 4, cb * 512:(cb + 1) * 512], writes=[st])
            for dl in range(4):
                dc = 4 * half + dl
                cast(w_in_bf[:, dc, cb * 512:(cb + 1) * 512], stv[:, dl * 512:(dl + 1) * 512], vecs[:, V_NG + dc:V_NG + dc + 1], [st, vecs], [w_in_cb[cb]])
    sg_ = stg_rot.next()
    st, stv = sg_.t, sg_.v
    fw.dma(SP, sg_.schan, stv[:, 0:1024].rearrange("p (a h j) -> p a h j", a=2, h=4), wg_d, writes=[st])
    fw.op(DVE, lambda e: e.tensor_copy(out=wg_bf[:].rearrange("p a h j -> p (a h j)"), in_=stv[:, 0:1024]), reads=[st], writes=[wg_bf])
    wout_v = wout_d.rearrange("(kc p) n -> p kc n", p=128)

    def load_w_out(g):
        sg_ = stg_rot.next()
        st, stv = sg_.t, sg_.v
        fw.dma(SP, sg_.schan, stv.rearrange("p (l n) -> p l n", l=2), wout_v[:, 2 * g:2 * g + 2, :], writes=[st])
        for l in range(2):
            kc = 2 * g + l
            if kc % 8 >= 4:
                hglob = (kc // 8) * 4 + (kc % 8 - 4)
                sc_ap = vecs[:, V_RG + hglob:V_RG + hglob + 1]
            else:
                sc_ap = None
            for nh in range(2):
                cast(w_out_bf[:, kc, nh * 512:(nh + 1) * 512], stv[:, l * 1024 + nh * 512:l * 1024 + (nh + 1) * 512], sc_ap, [st, vecs], [w_out_bf])

    pid = nc.partition_id([mybir.EngineType.Pool])
    par = pid % 2

    def stage_A(s):
        xnT = xnT_rot[s % 2]
        fw.cur_prio = 2
        chunks = TILE_CHUNKS[s]
        nch = len(chunks)
        ssq = ssq_rot.next()
        rstd = rstd_rot.next()
        xs_l = []
        cst = cs_rot.next()
        cs_of_tile[s] = cst
        fw.dma(SP, c_cs[s % 2], cst[:, 0:nch, :], cs_d[:, chunks[0]:chunks[0] + nch, :], writes=[cst])
        for j, n in enumerate(chunks):
            xs = xs_rot.next()
            xs_l.append(xs)
            if n == 0:
                fw.op(POOL, lambda e: e.memset(xs[:], 0.0), writes=[xs])
                fw.dma(SP, xs.chan, xs[128 - N_META:128, :], meta_d, writes=[xs])
            else:
                fw.dma(SP, xs.chan, xs[:], x_d[(n - 1) * 128:n * 128, :], writes=[xs])
            xnj = xn_rot.tiles[(xn_rot.i + (j % 2 if nch > 1 else 0)) % 2]
            fw.op(ACT, lambda e: e.activation(out=xnj[:], in_=xs[:], func=AF.Square, accum_out=ssq[:, j:j + 1]), reads=[xs], writes=[xnj, ssq])
            if (nch > 1 and j % 2 == 1) or nch == 1:
                j0 = j - 1 if nch > 1 else 0
                fw.op(POOL, lambda e: e.tensor_scalar(out=rstd[:, j0:j + 1], in0=ssq[:, j0:j + 1], scalar1=1.0 / D_MODEL, scalar2=EPS, op0=ALU.mult, op1=ALU.add), reads=[ssq], writes=[rstd])
                fw.op(POOL, lambda e: e.tensor_tensor(out=rstd[:, j0:j + 1], in0=rstd[:, j0:j + 1], in1=negh[:, j0:j + 1], op=ALU.pow), reads=[rstd, negh], writes=[rstd])
                for jj in range(j0, j + 1):
                    xs2 = xs_l[jj]
                    xn = xn_rot.next()
                    fw.op(DVE, lambda e: e.tensor_scalar_mul(out=xn[:], in0=xs2[:], scalar1=rstd[:, jj:jj + 1]), reads=[xs2, rstd], writes=[xn])
                    ptr = p_tr.next()
                    pv = bfv(ptr)
                    for dc in range(8):
                        fw.op(PE, lambda e: e.transpose(out=pv[:, dc * 128:(dc + 1) * 128], in_=xn[:, dc * 128:(dc + 1) * 128], identity=ident[:]),
                              reads=[xn, ident], writes=[ptr], inc=(dc == 7))
                    fw.op(DVE, lambda e: e.tensor_copy(out=xnT[:, :, jj * 128:(jj + 1) * 128], in_=pv.rearrange("p (a t) -> p a t", a=8)),
                          reads=[ptr], writes=[xnT])

    def stage_B1(s):
        sg = sg_all[s % 2]
        xnT = xnT_rot[s % 2]
        fw.cur_prio = 0
        NT = 128 * len(TILE_CHUNKS[s])
        lxc = lx[s % 2]
        for ec in (4, 0, 5, 1, 6, 2, 7, 3):
            if ec >= 4 and s == 0:
                continue
            pb = p_in.next()
            for dc in range(8):
                fw.op(PE, lambda e: e.matmul(pb[:, 0:NT], lhsT=w_in_bf[:, dc, ec * 128:(ec + 1) * 128], rhs=xnT[:, dc, 0:NT], start=(dc == 0), stop=(dc == 7)),
                      reads=[w_in_cb[ec // 4], xnT], writes=[pb], inc=(dc == 7))
            if ec < 4:
                h = ec
                fw.op(DVE, lambda e: e.tensor_copy(out=lxc[:, h, 3:3 + NT], in_=pb[:, 0:NT]), reads=[pb], writes=[lxc])
            else:
                h = ec - 4
                fw.op(ACT, lambda e: e.activation(out=thg[:, 0:NT], in_=pb[:, 0:NT], func=AF.Tanh, scale=0.5), reads=[pb], writes=[thg])
                fw.op(ACT, lambda e: e.activation(out=ghalf[:, 0:NT], in_=pb[:, 0:NT], func=AF.Copy, scale=0.5), reads=[pb], writes=[ghalf])
                fw.op(DVE, lambda e: e.scalar_tensor_tensor(out=sg[h][:, 0:NT], in0=thg[:, 0:NT], scalar=1.0, in1=ghalf[:, 0:NT], op0=ALU.add, op1=ALU.mult),
                      reads=[thg, ghalf], writes=[sg[h]])

    def stage_halo(s):
        NT = 128 * len(TILE_CHUNKS[s])
        lxc = lx[s % 2]
        nxt = lx[(s + 1) % 2]
        fw.op(POOL, lambda e: e.tensor_copy(out=nxt[:, :, 0:3], in_=lxc[:, :, NT:NT + 3]), reads=[lxc], writes=[nxt])

    def stage_C(s, h):
        sg = sg_all[s % 2]
        fw.cur_prio = 0
        NT = 128 * len(TILE_CHUNKS[s])
        lxc = lx[s % 2]
        pc = p_lru.next()
        for k in range(4):
            fw.op(PE, lambda e: e.matmul(pc[:, 0:NT], lhsT=diag[:, h, k, :], rhs=lxc[:, h, k:k + NT], start=(k == 0), stop=(k == 3)),
                  reads=[diag, lxc], writes=[pc], inc=(k == 3))
        xcb_, xch_, thr_, thi_, a_, a2_ = xcb.next(), xch.next(), thr.next(), thi.next(), a_t.next(), a2_t.next()
        fw.op(ACT, lambda e: e.activation(out=xcb_[:, 0:NT], in_=pc[:, 0:NT], func=AF.Identity, bias=vecs[:, V_CB + h:V_CB + h + 1], scale=1.0),
              reads=[pc, vecs], writes=[xcb_])
        fw.op(ACT, lambda e: e.activation(out=xch_[:, 0:NT], in_=pc[:, 0:NT], func=AF.Identity, bias=hv[:, h:h + 1], scale=0.5),
              reads=[pc, hv], writes=[xch_])
        pr = p_lru.next()
        fw.op(PE, lambda e: e.matmul(pr[:, 0:NT], lhsT=wg_bf[:, 0, h, :], rhs=xcb_[:, 0:NT], start=True, stop=True), reads=[wg_bf, xcb_], writes=[pr])
        pi = p_lru.next()
        fw.op(PE, lambda e: e.matmul(pi[:, 0:NT], lhsT=wg_bf[:, 1, h, :], rhs=xcb_[:, 0:NT], start=True, stop=True), reads=[wg_bf, xcb_], writes=[pi])
        fw.op(ACT, lambda e: e.activation(out=thr_[:, 0:NT], in_=pr[:, 0:NT], func=AF.Tanh, bias=hv[:, 4 + h:5 + h], scale=0.5), reads=[pr, hv], writes=[thr_])
        fw.op(ACT, lambda e: e.activation(out=thi_[:, 0:NT], in_=pi[:, 0:NT], func=AF.Tanh, bias=hv[:, 8 + h:9 + h], scale=0.5), reads=[pi, hv], writes=[thi_])
        fw.op(ACT, lambda e: e.activation(out=a_[:, 0:NT], in_=thr_[:, 0:NT], func=AF.Exp, bias=hv[:, 12 + h:13 + h], scale=hv[:, 12 + h:13 + h]), reads=[thr_, hv], writes=[a_])
        fw.op(ACT, lambda e: e.activation(out=a2_[:, 0:NT], in_=thr_[:, 0:NT], func=AF.Exp, bias=hv[:, 16 + h:17 + h], scale=hv[:, 16 + h:17 + h]), reads=[thr_, hv], writes=[a2_])
        fw.op(ACT, lambda e: e.activation(out=a2_[:, 0:NT], in_=a2_[:, 0:NT], func=AF.Relu, bias=1.0, scale=-1.0), reads=[a2_], writes=[a2_])
        fw.op(ACT, lambda e: e.activation(out=a2_[:, 0:NT], in_=a2_[:, 0:NT], func=AF.Sqrt), reads=[a2_], writes=[a2_])
        fw.op(DVE, lambda e: e.scalar_tensor_tensor(out=thi_[:, 0:NT], in0=thi_[:, 0:NT], scalar=1.0, in1=xch_[:, 0:NT], op0=ALU.add, op1=ALU.mult),
              reads=[thi_, xch_], writes=[thi_])
        fw.op(DVE, lambda e: e.tensor_tensor(out=a2_[:, 0:NT], in0=a2_[:, 0:NT], in1=thi_[:, 0:NT], op=ALU.mult), reads=[a2_, thi_], writes=[a2_])
        c0 = 128 - N_META if s == 0 else 0
        fw.op(DVE, lambda e: e.tensor_tensor_scan(out=xch_[:, c0:NT], data0=a_[:, c0:NT], data1=a2_[:, c0:NT], initial=hst[h][:, 0:1], op0=ALU.mult, op1=ALU.add),
              reads=[a_, a2_, hst[h], xch_], writes=[xch_])
        fw.op(POOL, lambda e: e.tensor_copy(out=hst[h][:, 0:1], in_=xch_[:, NT - 1:NT]), reads=[xch_], writes=[hst[h]])
        if s > 0:
            fw.op(DVE, lambda e: e.tensor_tensor(out=yT[h][:, 0:NT], in0=sg[h][:, 0:NT], in1=xch_[:, 0:NT], op=ALU.mult), reads=[sg[h], xch_], writes=[yT[h]])

    def stage_D(s, j):
        xnT = xnT_rot[s % 2]
        fw.cur_prio = 0
        n = TILE_CHUNKS[s][j]
        tsl = slice(j * 128, (j + 1) * 128)

        def proj(c0):
            pb = p_in.next()
            for dc in range(8):
                fw.op(PE, lambda e: e.matmul(pb[:, :], lhsT=xnT[:, dc, tsl], rhs=w_in_bf[:, dc, c0:c0 + 512], start=(dc == 0), stop=(dc == 7)),
                      reads=[xnT, w_in_cb[c0 // 512]], writes=[pb], inc=(dc == 7))
            return pb

        p_qk = proj(1024)
        qks = qk_sb.next()
        fw.op(ACT, lambda e: e.activation(out=qks[:], in_=p_qk[:, :], func=AF.Copy), reads=[p_qk], writes=[qks])
        p_v = proj(1536)
        vb = v_bf.next()
        fw.op(DVE, lambda e: e.tensor_copy(out=vb[:], in_=p_v[:, :]), reads=[p_v], writes=[vb])
        if s > 0:
            p_g = proj(2048)
            fw.op(ACT, lambda e: e.activation(out=thrg[:], in_=p_g[:, :], func=AF.Tanh, scale=0.5), reads=[p_g], writes=[thrg])
            fw.op(ACT, lambda e: e.activation(out=rghalf[:], in_=p_g[:, :], func=AF.Copy, scale=0.5), reads=[p_g], writes=[rghalf])
            sgr_ = sgr_rot.next()
            fw.op(DVE, lambda e: e.scalar_tensor_tensor(out=sgr_[:], in0=thrg[:], scalar=1.0, in1=rghalf[:], op0=ALU.add, op1=ALU.mult),
                  reads=[thrg, rghalf], writes=[sgr_])
        qv = qks.ap.rearrange("p (g t f) -> p g t f", g=8, t=2)
        rv = qk_rot.ap.rearrange("p (g t f) -> p g t f", g=8, t=2)
        t1, t2 = qv[:, :, 0, :], qv[:, :, 1, :]
        cs = cs_of_tile[s]
        cosb = cs[:, j, 0:32].unsqueeze(1).to_broadcast([128, 8, 32])
        sinb = cs[:, j, 32:64].unsqueeze(1).to_broadcast([128, 8, 32])
        tA = tmpA.ap.rearrange("p (g f) -> p g f", g=8)
        tB = tmpB.ap.rearrange("p (g f) -> p g f", g=8)
        fw.op(POOL, lambda e: e.tensor_tensor(out=tA, in0=t1, in1=cosb, op=ALU.mult), reads=[qks, cs], writes=[tmpA])
        fw.op(POOL, lambda e: e.tensor_tensor(out=tB, in0=t2, in1=sinb, op=ALU.mult), reads=[qks, cs], writes=[tmpB])
        fw.op(POOL, lambda e: e.tensor_tensor(out=rv[:, :, 0, :], in0=tA, in1=tB, op=ALU.subtract), reads=[tmpA, tmpB], writes=[qk_rot])
        fw.op(POOL, lambda e: e.tensor_tensor(out=tA, in0=t1, in1=sinb, op=ALU.mult), reads=[qks, cs], writes=[tmpA])
        fw.op(POOL, lambda e: e.tensor_tensor(out=tB, in0=t2, in1=cosb, op=ALU.mult), reads=[qks, cs], writes=[tmpB])
        fw.op(POOL, lambda e: e.tensor_tensor(out=rv[:, :, 1, :], in0=tA, in1=tB, op=ALU.add), reads=[tmpA, tmpB], writes=[qk_rot])
        qd_, kd_, kdec_ = qd.next(), kd.next(), kdec.next()
        qr = qk_rot.ap[:, 0:256].rearrange("p (h f) -> p h f", h=4)
        kr = qk_rot.ap[:, 256:512].rearrange("p (h f) -> p h f", h=4)

        def dcol(c):
            return dec[:, c:c + 4].unsqueeze(2).to_broadcast([128, 4, 64])

        fw.op(POOL, lambda e: e.tensor_tensor(out=qd_[:], in0=qr, in1=dcol(DC_Q), op=ALU.mult), reads=[qk_rot, dec], writes=[qd_])
        fw.op(POOL, lambda e: e.tensor_tensor(out=kd_[:], in0=kr, in1=dcol(DC_KI), op=ALU.mult), reads=[qk_rot, dec], writes=[kd_])
        fw.op(POOL, lambda e: e.tensor_tensor(out=kdec_[:], in0=kr, in1=dcol(DC_KD), op=ALU.mult), reads=[qk_rot, dec], writes=[kdec_])
        ptr = p_tr.next()
        pv = bfv(ptr)[0:64, :].rearrange("p (a t) -> p a t", a=8)
        for h in range(4):
            fw.op(PE, lambda e: e.transpose(out=pv[:, h, :], in_=qd_[:, h, :], identity=ident[:]), reads=[qd_, ident], writes=[ptr], inc=False)
        for h in range(4):
            fw.op(PE, lambda e: e.transpose(out=pv[:, 4 + h, :], in_=kd_[:, h, :], identity=ident[:]), reads=[kd_, ident], writes=[ptr], inc=(h == 3))
        qkT_ = qkT.next()
        fw.op(ACT, lambda e: e.activation(out=qkT_[:], in_=pv, func=AF.Copy), reads=[ptr], writes=[qkT_])
        ps = p_ret.next()
        psv = ps.ap.rearrange("p (h c) -> p h c", h=4)
        for h in range(4):
            fw.op(PE, lambda e: e.matmul(psv[:, h, :], lhsT=qkT_[:, 4 + h, :], rhs=qkT_[:, h, :], start=True, stop=True), reads=[qkT_], writes=[ps], inc=(h == 3))
        sm_ = sm.next()
        fw.op(DVE, lambda e: e.tensor_tensor(out=sm_[:], in0=psv, in1=maskT[:].unsqueeze(1).to_broadcast([128, 4, 128]), op=ALU.mult), reads=[ps, maskT], writes=[sm_])
        po = p_ret.next()
        pov = po.ap.rearrange("p (h c) -> p h c", h=4)
        for h in range(4):
            fw.op(PE, lambda e: e.matmul(pov[:, h, :], lhsT=qkT_[:, h, :], rhs=Rbf[:, h, :], start=True, stop=False), reads=[qkT_, Rbf], writes=[po], inc=False)
            fw.op(PE, lambda e: e.matmul(pov[:, h, :], lhsT=sm_[:, h, :], rhs=vb[:, h * 128:(h + 1) * 128], start=False, stop=True), reads=[sm_, vb], writes=[po], inc=(h == 3))
        pk = p_ret.next()
        pkv = pk.ap[0:64, :].rearrange("p (h c) -> p h c", h=4)
        for h in range(4):
            fw.op(PE, lambda e: e.matmul(pkv[:, h, :], lhsT=kdec_[:, h, :], rhs=vb[:, h * 128:(h + 1) * 128], start=True, stop=True), reads=[kdec_, vb], writes=[pk], inc=(h == 3))
        if s > 0:
            o_ = o_sb_rot.next()
            mv_ = mv_rot.next()
            rs_ = rs_rot.next()
            fw.op(DVE, lambda e: e.tensor_copy(out=o_[:], in_=pov), reads=[po], writes=[o_])
            for h in range(4):
                fw.op(DVE, lambda e: e.bn_stats(out=bst[:, h, :], in_=o_[:, h, :]), reads=[o_], writes=[bst_h[h]])
            for h in range(4):
                fw.op(DVE, lambda e: e.bn_aggr(out=mv_[:, h, :], in_=bst[:, h, :]), reads=[bst_h[h]], writes=[mv_.parts[h]])
        for h in range(4):
            fw.op(DVE, lambda e: e.scalar_tensor_tensor(out=R[:, h, :], in0=R[:, h, :], scalar=dec[0:64, DC_G128 + h:DC_G128 + h + 1], in1=pkv[:, h, :], op0=ALU.mult, op1=ALU.add),
                  reads=[R_h[h], dec, pk], writes=[R_h[h]])
        fw.op(DVE, lambda e: e.tensor_copy(out=Rbf[:], in_=R[:]), reads=R_h, writes=[Rbf])
        if s > 0:
            fw.op(POOL, lambda e: e.tensor_scalar(out=rs_[:], in0=mv_[:, :, 1], scalar1=1.0, scalar2=EPS, op0=ALU.mult, op1=ALU.add), reads=mv_.parts, writes=[rs_])
            fw.op(POOL, lambda e: e.tensor_tensor(out=rs_[:], in0=rs_[:], in1=negh[:, 0:4], op=ALU.pow), reads=[rs_, negh], writes=[rs_])
            mean_b = mv_[:, :, 0:1].to_broadcast([128, 4, 128])
            rs_b = rs_[:].unsqueeze(2).to_broadcast([128, 4, 128])
            fw.op(POOL, lambda e: e.tensor_tensor(out=o_[:], in0=o_[:], in1=mean_b, op=ALU.subtract), reads=[o_] + mv_.parts, writes=[o_])
            fw.op(POOL, lambda e: e.tensor_tensor(out=on[:], in0=o_[:], in1=rs_b, op=ALU.mult), reads=[o_, rs_], writes=[on])
            yr_ = yr.next()
            fw.op(DVE, lambda e: e.tensor_tensor(out=yr_[:].rearrange("p h c -> p (h c)"), in0=on[:].rearrange("p h c -> p (h c)"), in1=sgr_[:], op=ALU.mult),
                  reads=[on, sgr_], writes=[yr_])
            ptr2 = p_tr.next()
            pv2 = bfv(ptr2)[:, 0:512].rearrange("p (h t) -> p h t", h=4)
            for h in range(4):
                fw.op(PE, lambda e: e.transpose(out=pv2[:, h, :], in_=yr_[:, h, :], identity=ident[:]), reads=[yr_, ident], writes=[ptr2], inc=(h == 3))
            fw.op(DVE, lambda e: e.tensor_copy(out=yT_all[:, 4:8, j * 128:(j + 1) * 128], in_=pv2), reads=[ptr2], writes=yT[4:8])

    def stage_E1(s):
        fw.cur_prio = 0
        fw.dma(SP, c_ysrc, ysrc_d[s].ap().rearrange("(k p) t -> p k t", p=128), yT_all[:, :, 0:TILE_NT[s]], reads=yT, writes=[ysrc_t[s]])
        fw.custom(POOL, c_cc, lambda e: e.collective_compute("AllGather", ALU.bypass, replica_groups=[[0, 1], [2, 3], [4, 5], [6, 7]],
                                                              ins=[ysrc_d[s].ap().opt()], outs=[ydst_d[s].ap().opt()]),
                  reads=[ysrc_t[s]], writes=[ydst_t[s]])

    def stage_E2(s):
        NH = TILE_NT[s] // 2
        nj2 = NH // 128
        off = TILE_T0[s] // 2
        fw.dma(POOL, c_yfull, yfull[:, :, 0:NH], ydst_d[s].ap().rearrange("(k p) t -> p k t", p=128)[:, :, bass.ds(par * NH, NH)], reads=[ydst_t[s]], writes=[yfull])
        xr = xr_rot.next()
        fw.dma(SP, xr.chan, xr[:, 0:nj2, :], xres_d[off:off + NH, :].rearrange("(j p) d -> p j d", p=128), writes=[xr])
        for j2 in range(nj2):
            for nh in range(2):
                pb = p_out.next()
                for kc in range(16):
                    fw.op(PE, lambda e: e.matmul(pb[:, :], lhsT=yfull[:, kc, j2 * 128:(j2 + 1) * 128], rhs=w_out_bf[:, kc, nh * 512:(nh + 1) * 512], start=(kc == 0), stop=(kc == 15)),
                          reads=[yfull, w_out_bf], writes=[pb], inc=(kc == 15))
                fw.op(DVE, lambda e: e.tensor_tensor(out=xr[:, j2, nh * 512:(nh + 1) * 512], in0=xr[:, j2, nh * 512:(nh + 1) * 512], in1=pb[:, :], op=ALU.add),
                      reads=[xr, pb], writes=[xr])
                fw.op(ACT, lambda e: e.activation(out=pb[:, :], in_=xr[:, j2, nh * 512:(nh + 1) * 512], func=AF.Square, accum_out=fss4[:, j2 * 2 + nh:j2 * 2 + nh + 1]),
                      reads=[xr], writes=[pb, fss4])
        f4 = fss4.ap.rearrange("p (j n) -> p j n", n=2)
        fw.op(POOL, lambda e: e.tensor_tensor(out=frs[:, 0:nj2], in0=f4[:, 0:nj2, 0], in1=f4[:, 0:nj2, 1], op=ALU.add), reads=[fss4], writes=[frs])
        fw.op(POOL, lambda e: e.tensor_scalar(out=frs[:, 0:nj2], in0=frs[:, 0:nj2], scalar1=1.0 / D_MODEL, scalar2=EPS, op0=ALU.mult, op1=ALU.add), reads=[frs], writes=[frs])
        fw.op(POOL, lambda e: e.tensor_tensor(out=frs[:, 0:nj2], in0=frs[:, 0:nj2], in1=negh[:, 0:nj2], op=ALU.pow), reads=[frs, negh], writes=[frs])
        for j2 in range(nj2):
            fw.op(DVE, lambda e: e.scalar_tensor_tensor(out=xr[:, j2, :], in0=xr[:, j2, :], scalar=frs[:, j2:j2 + 1], in1=fgain[:], op0=ALU.mult, op1=ALU.mult),
                  reads=[xr, frs, fgain], writes=[xr])
        fw.dma(SP, c_out, out_d[off:off + NH, :].rearrange("(j p) d -> p j d", p=128), xr[:, 0:nj2, :], reads=[xr], writes=[out_t])

    marks = {}
    stage_A(0)
    for g in range(8):
        load_w_out(g)
    marks["A0"] = fw.ops[-1]
    for s in range(NTILES):
        stage_B1(s)
        if s + 1 < NTILES:
            stage_A(s + 1)
        nch = len(TILE_CHUNKS[s])
        hpc = 4 // nch
        for j in range(nch):
            stage_D(s, j)
            for h in range(j * hpc, (j + 1) * hpc):
                stage_C(s, h)
        stage_halo(s)
        if s > 0:
            stage_E1(s)
        if s > 0:
            stage_E2(s)
        marks[f"t{s}"] = fw.ops[-1]
    fw.finish(POOL, [out_t])
    fw.finish(SP, [out_t])
    best = None
    for trial in range(N_SCHED_TRIALS):
        if trial == 0:
            order = fw.schedule()
        else:
            order = fw.schedule(seed=trial, noise=0.02 * (1 + trial % 5), win=(0.1e-6, 0.3e-6, 0.6e-6)[trial % 3])
        if best is None or fw.sim_end < best[0]:
            best = (fw.sim_end, list(order), fw.n_switch, trial)
    fw.sim_end, order, fw.n_switch, best_trial = best
    fw.emit(order)
    stats = {e.name: (e.nins, e.nwaits) for e in (PE, ACT, DVE, POOL, SP)}
    stats["fw"] = fw
    stats["order"] = order
    stats["sim_end_us"] = fw.sim_end * 1e6
    stats["n_switch"] = fw.n_switch
    stats["best_trial"] = best_trial
    stats["sbuf_bytes"] = sb_bytes[0] + 8192
    stats["marks"] = {k: round(v.fin * 1e6, 1) for k, v in marks.items()}
    stats["sim_busy_us"] = {k: round(v * 1e6, 1) for k, v in fw.sim_busy.items()}
    return nc, stats


def _consts():
    half = 32
    inv = np.power(np.float64(10000.0), -(np.arange(half, dtype=np.float64) / np.float64(half)))
    c = np.arange(128)[:, None]
    n = np.arange(NCHUNK)[None, :]
    pos = np.maximum(n * 128 + c - (128 - N_META), 0).astype(np.float64)
    ang = pos[:, :, None] * inv[None, None, :]
    cs = np.concatenate([np.cos(ang), np.sin(ang)], axis=-1).astype(np.float32)
    ident = np.eye(128, dtype=np.float32).astype(ml_dtypes.bfloat16)
    m = np.arange(128)[:, None]
    cc = np.arange(128)[None, :]
    maskT = (cc >= m).astype(np.float32).astype(ml_dtypes.bfloat16)
    return cs, ident, maskT


def _dec_table(heads):
    log_g = np.log1p(-np.exp2(-5.0 - np.arange(8, dtype=np.float64)))
    c = np.arange(128, dtype=np.float64)[:, None]
    lg = log_g[heads][None, :]
    dec = np.zeros((128, NDEC), np.float32)
    dec[:, DC_Q:DC_Q + 4] = np.exp((c + 1.0) * lg)
    dec[:, DC_KI:DC_KI + 4] = np.exp(-(c + 1.0) * lg) * 0.125
    dec[:, DC_KD:DC_KD + 4] = np.exp((127.0 - c) * lg) * 0.125
    dec[:, DC_G128:DC_G128 + 4] = np.exp(128.0 * lg)
    return dec


_PROG = None


def kernel(x, meta_tokens, norm_gain, w_in, conv_w, conv_b, w_rg, b_rg, w_ig, b_ig,
           lru_lambda, ret_norm_gain, w_out, final_norm_gain):
    global _PROG
    f = lambda a: np.ascontiguousarray(np.asarray(a), dtype=np.float32)
    x, meta_tokens, norm_gain, w_in = f(x), f(meta_tokens), f(norm_gain), f(w_in)
    conv_w, conv_b, w_rg, b_rg, w_ig, b_ig = f(conv_w), f(conv_b), f(w_rg), f(b_rg), f(w_ig), f(b_ig)
    lru_lambda, ret_norm_gain, w_out, final_norm_gain = f(lru_lambda), f(ret_norm_gain), f(w_out), f(final_norm_gain)
    if _PROG is None:
        _PROG = build_program()
    nc = _PROG[0]
    cs, ident, maskT = _consts()
    w_in0 = w_in[0]
    perm = np.concatenate([np.arange(0, 512), np.arange(1024, 1536), np.arange(512, 1024), np.arange(1536, 2048)])
    w_out_p = np.ascontiguousarray(w_out[0][perm])
    fgain = np.ascontiguousarray(np.broadcast_to(final_norm_gain[None, :], (128, D_MODEL)))
    in_maps = []
    for c in range(8):
        b, p = c // 2, c % 2
        lch = np.arange(512 * p, 512 * p + 512)
        heads = np.arange(4 * p, 4 * p + 4)
        qk_cols = (heads[:, None] * 64 + np.arange(64)[None, :]).reshape(-1)
        v_cols = (heads[:, None] * 128 + np.arange(128)[None, :]).reshape(-1)
        cols = np.concatenate([lch, 1024 + lch, 2048 + qk_cols, 2560 + qk_cols, 3072 + v_cols, 4096 + v_cols])
        w_in_c = np.ascontiguousarray(w_in0[:, cols])
        wg = np.stack([w_rg[0, 4 * p:4 * p + 4], w_ig[0, 4 * p:4 * p + 4]], 0)
        wg = np.ascontiguousarray(wg.transpose(2, 0, 1, 3))
        vecs = np.zeros((128, NV), np.float32)
        for h in range(4):
            ch = 512 * p + h * 128 + np.arange(128)
            for k in range(4):
                vecs[:, V_CW + h * 4 + k] = conv_w[0, k, ch]
            vecs[:, V_CB + h] = conv_b[0, ch]
            vecs[:, V_BRG + h] = b_rg[0, ch]
            vecs[:, V_BIG + h] = b_ig[0, ch]
            vecs[:, V_LAM + h] = lru_lambda[0, ch]
        for dc in range(8):
            vecs[:, V_NG + dc] = norm_gain[0, dc * 128:(dc + 1) * 128]
        for h in range(8):
            vecs[:, V_RG + h] = ret_norm_gain[0, h * 128:(h + 1) * 128]
        xb = x[b]
        xres = np.ascontiguousarray(np.concatenate([xb[TILE_T0[t_] + p * (TILE_NT[t_] // 2):TILE_T0[t_] + (p + 1) * (TILE_NT[t_] // 2)] for t_ in range(1, NTILES)], 0))
        in_maps.append({
            "x": xb, "meta": meta_tokens, "xres": xres, "w_in": w_in_c, "w_out": w_out_p, "w_g": wg,
            "vecs": vecs, "fgain": fgain, "ident": ident, "cs": cs, "dec": _dec_table(heads), "maskT": maskT,
        })
    res = run_bass_kernel_spmd(nc, in_maps, core_ids=list(range(8)))
    out = np.zeros((BATCH, SEQ, D_MODEL), np.float32)
    for c in range(8):
        b, p = c // 2, c % 2
        o = np.asarray(res.results[c]["out"])
        for t_ in range(1, NTILES):
            nh_ = TILE_NT[t_] // 2
            out[b, TILE_T0[t_] + p * nh_:TILE_T0[t_] + (p + 1) * nh_] = o[TILE_T0[t_] // 2:TILE_T0[t_] // 2 + nh_]
    return out
```

```python
import numpy as np
import ml_dtypes
import concourse.bass as bass
import concourse.mybir as mybir
from concourse.bass_utils import run_bass_kernel_spmd

F32 = mybir.dt.float32
BF16 = mybir.dt.bfloat16
ALU = mybir.AluOpType
AF = mybir.ActivationFunctionType

D_MODEL = 1024
BATCH = 4
SEQ = 4096
N_META = 16
CH = 128
NCHUNK = 33
EPS = 1e-6
N_SCHED_TRIALS = 300
FORCE_TRIAL = 15
TILE_NCH = [1, 4, 4, 4, 4, 4, 4, 4, 2, 2]
TILE_CHUNKS = []
_c = 0
for _n in TILE_NCH:
    TILE_CHUNKS.append(list(range(_c, _c + _n)))
    _c += _n
assert _c == NCHUNK
NTILES = len(TILE_CHUNKS)
TILE_T0 = [None] + [(TILE_CHUNKS[s][0] - 1) * 128 for s in range(1, NTILES)]
TILE_NT = [128 * n for n in TILE_NCH]

V_CW = 0
V_CB = 16
V_BRG = 20
V_BIG = 24
V_LAM = 28
V_NG = 32
V_RG = 40
NV = 48
DC_Q = 0
DC_KI = 4
DC_KD = 8
DC_G128 = 12
NDEC = 16


class Src:
    def __init__(self, name, sem, step):
        self.name, self.sem, self.step = name, sem, step
        self.count = 0
        self.snap = {}


class Eng(Src):
    def __init__(self, name, sem, handle):
        super().__init__(name, sem, 1)
        self.h = handle
        self.know = {}
        self.nwaits = 0
        self.nins = 0


class T:
    def __init__(self, ap, name=""):
        self.ap = ap
        self.name = name
        self.w = {}
        self.r = {}
        self.lw = None
        self.lr = []

    def __getitem__(self, k):
        return self.ap[k]


class _Proxy:
    def __init__(self):
        self.calls = []

    def __getattr__(self, name):
        def f(*a, **k):
            self.calls.append((name, a, k))
            return None
        return f


def _free_size(ap):
    n = 1
    for d in ap.shape[1:]:
        n *= int(d)
    return n


class Op:
    __slots__ = ("prio", "tbl", "eng", "calls", "reads", "writes", "kind", "ch", "dur", "lat", "idx", "deps", "nsucc", "succ", "ndeps", "start", "fin", "closed", "bytes")


class FW:
    DMA_BW = 150e9
    TBL_SWITCH = 2.7e-6

    def __init__(self, nc):
        self.nc = nc
        self.pe = self._eng("pe", nc.tensor)
        self.act = self._eng("act", nc.scalar)
        self.dve = self._eng("dve", nc.vector)
        self.pool = self._eng("pool", nc.gpsimd)
        self.sp = self._eng("sp", nc.sync)
        self.ops = []
        self.open = None
        self.cur_prio = 0

    def _eng(self, name, handle):
        return Eng(name, self.nc.alloc_semaphore("s_" + name), handle)

    def chan(self, name, step=16):
        return Src(name, self.nc.alloc_semaphore("c_" + name), step)

    def _est(self, eng, name, a, k):
        out = k.get("out", a[0] if a else None)
        n = _free_size(out) if out is not None and hasattr(out, "shape") else 64
        if eng is self.pe:
            return max(64, n) / 2.0e9 + 0.02e-6
        if eng is self.act:
            return 0.30e-6 + n * 0.84e-9
        if eng is self.dve:
            return 0.17e-6 + n * 1.05e-9
        return 0.32e-6 + n * 1.5e-9

    def _new(self, eng, kind):
        o = Op()
        o.eng, o.kind, o.calls, o.reads, o.writes = eng, kind, [], [], []
        o.ch, o.dur, o.lat, o.idx, o.closed, o.bytes = None, 0.0, 0.0, len(self.ops), True, 0
        o.tbl = None
        o.prio = self.cur_prio
        self.ops.append(o)
        return o

    def op(self, eng, fn, reads=(), writes=(), inc=True):
        p = _Proxy()
        fn(p)
        if self.open is not None and self.open.eng is eng:
            o = self.open
        else:
            assert self.open is None, "unterminated inc=False group"
            o = self._new(eng, "op")
        for (name, a, k) in p.calls:
            o.calls.append((name, a, k))
            o.dur += self._est(eng, name, a, k)
            if name == "activation":
                f_ = k.get("func")
                if f_ == AF.Sqrt:
                    o.tbl = "sqrt"
                elif f_ in (AF.Exp, AF.Tanh):
                    o.tbl = "exp"
                elif f_ == AF.Ln:
                    o.tbl = "ln"
        for t in reads:
            if t not in o.reads:
                o.reads.append(t)
        for t in writes:
            if t not in o.writes:
                o.writes.append(t)
        self.open = None if inc else o

    def dma(self, q, ch, out, in_, reads=(), writes=(), **kw):
        assert self.open is None
        o = self._new(q, "dma")
        o.calls.append(("dma_start", (), dict(out=out, in_=in_, **kw)))
        o.ch = ch
        o.reads, o.writes = list(reads), list(writes)
        nbytes = 1
        for d in out.shape:
            nbytes *= int(d)
        o.bytes = nbytes * (2 if out.dtype == BF16 else 4)
        o.dur = 0.08e-6 if q is self.sp else 1.0e-6
        o.lat = 2.0e-6

    def custom(self, q, ch, fn, reads=(), writes=(), lat=25e-6):
        assert self.open is None
        p = _Proxy()
        fn(p)
        o = self._new(q, "custom")
        o.calls = list(p.calls)
        o.ch = ch
        o.reads, o.writes = list(reads), list(writes)
        o.dur = 1.0e-6
        o.lat = lat

    def finish(self, q, tiles):
        assert self.open is None
        o = self._new(q, "finish")
        o.reads = list(tiles)
        o.dur = 0.05e-6

    def _all_tiles(self):
        seen = {}
        for o in self.ops:
            for t in o.reads:
                seen[id(t)] = t
            for t in o.writes:
                seen[id(t)] = t
        return seen.values()

    def schedule(self, seed=None, noise=0.0, win=0.3e-6):
        import random
        rng = random.Random(seed)
        ops = self.ops
        for t_ in self._all_tiles():
            t_.lw = None
            t_.lr = []
        for o in ops:
            deps = set()
            for t in o.reads:
                if t.lw is not None:
                    deps.add(t.lw)
            for t in o.writes:
                if t.lw is not None:
                    deps.add(t.lw)
                for r in t.lr:
                    deps.add(r)
            deps.discard(o)
            o.deps = deps
            for t in o.reads:
                t.lr.append(o)
            for t in o.writes:
                t.lw = o
                t.lr = []
        for o in ops:
            o.succ = []
            o.ndeps = len(o.deps)
            o.start = None
        for o in ops:
            for d in o.deps:
                d.succ.append(o)
        engs = [self.pe, self.act, self.dve, self.pool, self.sp]
        bl = {}
        for o in reversed(ops):
            m = 0.0
            for s_ in o.succ:
                if bl[s_] > m:
                    m = bl[s_]
            extra = o.lat + (o.bytes / self.DMA_BW if o.kind == "dma" else 0.0)
            bl[o] = m + o.dur + extra
        if noise > 0.0:
            blp = {o: v * (1.0 + noise * (rng.random() - 0.5)) for o, v in bl.items()}
        else:
            blp = bl
        free = {e: 0.0 for e in engs}
        dma_free = 0.0
        ready = {e: [] for e in engs}
        for o in ops:
            if o.ndeps == 0:
                ready[o.eng].append(o)
        order = []
        n = len(ops)
        WIN = win
        cur_tbl = ["exp"]
        self.n_switch = 0
        while len(order) < n:
            best = None
            for e in engs:
                fe = free[e]
                cands = []
                mn = None
                for o in ready[e]:
                    st = fe
                    if o.tbl is not None and e is self.act and o.tbl != cur_tbl[0]:
                        st = fe + self.TBL_SWITCH
                    for d in o.deps:
                        if d.fin > st:
                            st = d.fin
                    cands.append((st, o))
                    if mn is None or st < mn:
                        mn = st
                if mn is None:
                    continue
                pick = None
                for st, o in cands:
                    if st <= mn + WIN:
                        if pick is None or (o.prio, blp[o]) > (pick[1].prio, blp[pick[1]]):
                            pick = (st, o)
                key = (pick[0], -blp[pick[1]])
                if best is None or key < best[0]:
                    best = (key, pick[1], pick[0])
            _, o, st = best
            ready[o.eng].remove(o)
            o.start = st
            free[o.eng] = st + o.dur
            if o.tbl is not None and o.eng is self.act and o.tbl != cur_tbl[0]:
                cur_tbl[0] = o.tbl
                self.n_switch += 1
            if o.kind == "dma":
                t0 = max(st + o.dur, dma_free)
                dma_free = t0 + o.bytes / self.DMA_BW
                o.fin = dma_free + o.lat
            elif o.kind == "custom":
                o.fin = st + o.dur + o.lat
            else:
                o.fin = st + o.dur
            order.append(o)
            for s_ in o.succ:
                s_.ndeps -= 1
                if s_.ndeps == 0:
                    ready[s_.eng].append(s_)
        self.sim_end = max(o.fin for o in ops)
        self.sim_busy = {e.name: sum(o.dur for o in ops if o.eng is e) for e in engs}
        return order

    def _merge(self, eng, src, idx):
        if eng.know.get(src, 0) < idx:
            eng.know[src] = idx
        for s2, v in src.snap.get(idx, {}).items():
            if eng.know.get(s2, 0) < v:
                eng.know[s2] = v

    def _waits(self, eng, reads, writes):
        needs = {}
        for t in reads:
            for s, i in t.w.items():
                if needs.get(s, 0) < i:
                    needs[s] = i
        for t in writes:
            for d in (t.w, t.r):
                for s, i in d.items():
                    if needs.get(s, 0) < i:
                        needs[s] = i
        for s, i in sorted(needs.items(), key=lambda kv: kv[0] is eng):
            if eng.know.get(s, 0) >= i:
                continue
            eng.h.wait_ge(s.sem, i * s.step)
            eng.nwaits += 1
            self._merge(eng, s, i)

    def _record(self, src, idx, reads, writes):
        for t in reads:
            if t.r.get(src, 0) < idx:
                t.r[src] = idx
        for t in writes:
            t.w = {src: idx}
            t.r = {}

    def emit(self, order):
        for o in order:
            eng = o.eng
            self._waits(eng, o.reads, o.writes)
            if o.kind == "finish":
                continue
            ins = None
            for (name, a, k) in o.calls:
                ins = getattr(eng.h, name)(*a, **k)
                eng.nins += 1
            if o.kind == "op":
                eng.count += 1
                ins.then_inc(eng.sem, 1)
                eng.snap[eng.count] = dict(eng.know)
                self._record(eng, eng.count, o.reads, o.writes)
            else:
                ch = o.ch
                ch.count += 1
                ins.then_inc(ch.sem, ch.step)
                ch.snap[ch.count] = dict(eng.know)
                self._record(ch, ch.count, o.reads, o.writes)


class Rot:
    def __init__(self, tiles):
        self.tiles = tiles
        self.i = 0

    def next(self):
        t = self.tiles[self.i % len(self.tiles)]
        self.i += 1
        return t


def build_program():
    nc = bass.Bass("TRN2", target_bir_lowering=False)
    fw = FW(nc)
    PE, ACT, DVE, POOL, SP = fw.pe, fw.act, fw.dve, fw.pool, fw.sp

    def din(name, shape, dt):
        return nc.dram_tensor(name, shape, dt, kind="ExternalInput").ap()

    x_d = din("x", [SEQ, D_MODEL], F32)
    meta_d = din("meta", [N_META, D_MODEL], F32)
    xres_d = din("xres", [SEQ // 2, D_MODEL], F32)
    win_d = din("w_in", [D_MODEL, 2560], F32)
    wout_d = din("w_out", [2048, D_MODEL], F32)
    wg_d = din("w_g", [128, 2, 4, 128], F32)
    vecs_d = din("vecs", [128, NV], F32)
    fgain_d = din("fgain", [128, D_MODEL], F32)
    ident_d = din("ident", [128, 128], BF16)
    cs_d = din("cs", [128, NCHUNK, 64], F32)
    dec_d = din("dec", [128, NDEC], F32)
    mask_d = din("maskT", [128, 128], BF16)
    out_d = nc.dram_tensor("out", [SEQ // 2, D_MODEL], F32, kind="ExternalOutput").ap()
    ysrc_d = [None] + [nc.dram_tensor(f"ysrc{s}", [1024, TILE_NT[s]], BF16) for s in range(1, NTILES)]
    ydst_d = [None] + [nc.dram_tensor(f"ydst{s}", [2048, TILE_NT[s]], BF16) for s in range(1, NTILES)]
    ysrc_t = [None] + [T(ysrc_d[s].ap(), f"ysrc{s}") for s in range(1, NTILES)]
    ydst_t = [None] + [T(ydst_d[s].ap(), f"ydst{s}") for s in range(1, NTILES)]
    out_t = T(out_d, "out")

    sb_bytes = [0]

    def sb(name, shape, dt):
        n = 1
        for d in shape[1:]:
            n *= d
        sb_bytes[0] += n * (2 if dt == BF16 else 4)
        return T(nc.alloc_sbuf_tensor(name, shape, dt).ap(), name)

    w_in_bf = sb("w_in_bf", [128, 8, 2560], BF16)
    w_in_cb = [T(w_in_bf.ap[:, :, cb * 512:(cb + 1) * 512], f"w_in_cb{cb}") for cb in range(5)]
    w_out_bf = sb("w_out_bf", [128, 16, 1024], BF16)
    wg_bf = sb("wg_bf", [128, 2, 4, 128], BF16)
    diag = sb("diag", [128, 4, 4, 128], BF16)
    fgain = sb("fgain_sb", [128, D_MODEL], F32)
    cs_rot = Rot([sb(f"cs_sb{i}", [128, 4, 64], F32) for i in range(2)])
    cs_of_tile = {}
    ident = sb("ident_sb", [128, 128], BF16)
    maskT = sb("mask_sb", [128, 128], BF16)
    vecs = sb("vecs_sb", [128, NV], F32)
    dec = sb("dec_sb", [128, NDEC], F32)
    negh = sb("negh", [128, 16], F32)
    posh = sb("posh", [128, 1], F32)
    hv = sb("hv", [128, 24], F32)
    xs_rot = Rot([sb(f"xs{i}", [128, D_MODEL], F32) for i in range(2)])
    xn_rot = Rot([sb(f"xn{i}", [128, D_MODEL], BF16) for i in range(2)])
    xnT_rot = [sb(f"xnT{i}", [128, 8, 512], BF16) for i in range(2)]
    ssq_rot = Rot([sb(f"ssq{i}", [128, 4], F32) for i in range(2)])
    rstd_rot = Rot([sb(f"rstd{i}", [128, 4], F32) for i in range(2)])
    lx = [sb(f"lx{i}", [128, 4, 515], BF16) for i in range(2)]
    thg = sb("thg", [128, 512], BF16)
    ghalf = sb("ghalf", [128, 512], BF16)
    sg_all = [[sb(f"sg{i}_{h}", [128, 512], BF16) for h in range(4)] for i in range(2)]
    NSET = 2
    xcb = Rot([sb(f"xcb{i}", [128, 512], BF16) for i in range(NSET)])
    xch = Rot([sb(f"xch{i}", [128, 512], F32) for i in range(NSET)])
    thr = Rot([sb(f"thr{i}", [128, 512], F32) for i in range(NSET)])
    thi = Rot([sb(f"thi{i}", [128, 512], F32) for i in range(NSET)])
    a_t = Rot([sb(f"a{i}", [128, 512], F32) for i in range(NSET)])
    a2_t = Rot([sb(f"a2{i}", [128, 512], F32) for i in range(NSET)])
    hst = [sb(f"hst{h}", [128, 1], F32) for h in range(4)]
    yT_all = nc.alloc_sbuf_tensor("yT_all", [128, 8, 512], BF16).ap()
    yT = [T(yT_all[:, k, :], f"yT{k}") for k in range(8)]
    qk_sb = Rot([sb(f"qk_sb{i}", [128, 512], F32) for i in range(1)])
    tmpA = sb("tmpA", [128, 256], F32)
    tmpB = sb("tmpB", [128, 256], F32)
    qk_rot = sb("qk_rot", [128, 512], F32)
    qd = Rot([sb(f"qd{i}", [128, 4, 64], BF16) for i in range(2)])
    kd = Rot([sb(f"kd{i}", [128, 4, 64], BF16) for i in range(2)])
    kdec = Rot([sb(f"kdec{i}", [128, 4, 64], BF16) for i in range(2)])
    v_bf = Rot([sb(f"v_bf{i}", [128, 512], BF16) for i in range(2)])
    thrg = thg
    rghalf = ghalf
    sgr_rot = Rot([sb(f"sgr{j}", [128, 512], BF16) for j in range(2)])
    qkT = Rot([sb(f"qkT{i}", [64, 8, 128], BF16) for i in range(2)])
    sm = Rot([sb(f"sm{i}", [128, 4, 128], BF16) for i in range(2)])
    o_sb_rot = Rot([sb(f"o_sb{j}", [128, 4, 128], F32) for j in range(2)])
    bst = sb("bst", [128, 4, 6], F32)
    bst_h = [T(bst.ap[:, h, :], f"bst{h}") for h in range(4)]
    mv_rot = Rot([sb(f"mv{i}", [128, 4, 2], F32) for i in range(2)])
    for t_ in mv_rot.tiles:
        t_.parts = [T(t_.ap[:, h, :], t_.name + f"_{h}") for h in range(4)]
    rs_rot = Rot([sb(f"rs{i}", [128, 4], F32) for i in range(2)])
    on = sb("on", [128, 4, 128], BF16)
    yr = Rot([sb(f"yr{i}", [128, 4, 128], BF16) for i in range(2)])
    R = sb("R", [64, 4, 128], F32)
    R_h = [T(R.ap[:, h, :], f"R{h}") for h in range(4)]
    Rbf = sb("Rbf", [64, 4, 128], BF16)
    yfull = sb("yfull", [128, 16, 256], BF16)
    xr_rot = Rot([sb(f"xr{i}", [128, 2, D_MODEL], F32) for i in range(1)])
    fss4 = sb("fss4", [128, 4], F32)
    frs = sb("frs", [128, 2], F32)

    banks = [T(nc.alloc_psum_tensor(f"pb{i}", [128, 512], F32).ap(), f"pb{i}") for i in range(8)]
    p_in = Rot(banks[0:2])
    p_out = Rot(banks[2:3])
    p_tr = Rot(banks[3:4])
    p_lru = Rot(banks[4:6])
    p_ret = Rot(banks[6:8])

    def bfv(bank):
        return bank.ap.bitcast(BF16)

    c_const = fw.chan("const")
    c_stage = [fw.chan("stage0"), fw.chan("stage1")]
    c_x = [fw.chan("x0"), fw.chan("x1")]
    c_cs = [fw.chan("cs0"), fw.chan("cs1")]
    c_ysrc = fw.chan("ysrc")
    c_cc = fw.chan("cc", step=1)
    c_yfull = fw.chan("yfull")
    c_xr = [fw.chan("xr0"), fw.chan("xr1")]
    c_out = fw.chan("out")
    for t_, i_ in zip(xs_rot.tiles, range(2)):
        t_.chan = c_x[i_]
    for t_, i_ in zip(xr_rot.tiles, range(2)):
        t_.chan = c_xr[i_]
    class _Stg:
        pass
    stg = []
    for i_, (t_, v_) in enumerate(((xr_rot.tiles[0], xr_rot.tiles[0].ap.rearrange("p a d -> p (a d)")),
                                   (yfull, yfull.ap.rearrange("p k t -> p (k t)").bitcast(F32)))):
        g_ = _Stg()
        g_.t, g_.v, g_.schan = t_, v_, c_stage[i_]
        stg.append(g_)
    stg_rot = Rot(stg)

    for dst, src in ((ident, ident_d), (maskT, mask_d), (vecs, vecs_d), (dec, dec_d), (fgain, fgain_d)):
        fw.dma(SP, fw.chan("const_" + dst.name), dst[:], src, writes=[dst])

    fw.op(DVE, lambda e: e.tensor_scalar_mul(out=hv[:, 0:12], in0=vecs[:, V_CB:V_CB + 12], scalar1=0.5), reads=[vecs], writes=[hv])
    fw.op(ACT, lambda e: e.activation(out=hv[:, 20:24], in_=vecs[:, V_LAM:V_LAM + 4], func=AF.Exp, scale=-1.0), reads=[vecs, hv], writes=[hv])
    fw.op(ACT, lambda e: e.activation(out=hv[:, 20:24], in_=hv[:, 20:24], func=AF.Ln, bias=1.0), reads=[hv], writes=[hv])
    fw.op(DVE, lambda e: e.tensor_scalar_mul(out=hv[:, 12:16], in0=hv[:, 20:24], scalar1=-4.0), reads=[hv], writes=[hv])
    fw.op(DVE, lambda e: e.tensor_scalar_mul(out=hv[:, 16:20], in0=hv[:, 20:24], scalar1=-8.0), reads=[hv], writes=[hv])
    for h in range(4):
        for k in range(4):
            fw.op(DVE, lambda e: e.tensor_scalar_mul(out=diag[:, h, k, :], in0=ident[:], scalar1=vecs[:, V_CW + h * 4 + k:V_CW + h * 4 + k + 1]),
                  reads=[ident, vecs], writes=[diag])
    fw.op(POOL, lambda e: e.memset(negh[:], -0.5), writes=[negh])
    fw.op(POOL, lambda e: e.memset(posh[:], 0.5), writes=[posh])
    fw.op(DVE, lambda e: e.memset(R[:], 0.0), writes=R_h)
    fw.op(DVE, lambda e: e.memset(Rbf[:], 0.0), writes=[Rbf])
    fw.op(DVE, lambda e: e.memset(lx[0][:], 0.0), writes=[lx[0]])
    fw.op(DVE, lambda e: e.memset(lx[1][:], 0.0), writes=[lx[1]])
    for h in range(4):
        fw.op(DVE, lambda e: e.memset(hst[h][:], 0.0), writes=[hst[h]])

    cast_rr = [0]

    def cast(out_ap, in_ap, scalar_ap, reads, writes):
        k = cast_rr[0] % 2
        cast_rr[0] += 1
        if k == 0:
            if scalar_ap is None:
                fw.op(ACT, lambda e: e.activation(out=out_ap, in_=in_ap, func=AF.Copy), reads=reads, writes=writes)
            else:
                fw.op(ACT, lambda e: e.activation(out=out_ap, in_=in_ap, func=AF.Copy, scale=scalar_ap), reads=reads, writes=writes)
        else:
            eng = DVE if k == 1 else POOL
            if scalar_ap is None:
                fw.op(eng, lambda e: e.tensor_copy(out=out_ap, in_=in_ap), reads=reads, writes=writes)
            else:
                fw.op(eng, lambda e: e.tensor_scalar_mul(out=out_ap, in0=in_ap, scalar1=scalar_ap), reads=reads, writes=writes)

    win_v = win_d.rearrange("(dc p) c -> p dc c", p=128)
    for cb in (2, 3, 4, 0, 1):
        for half in range(2):
            sg_ = stg_rot.next()
            st, stv = sg_.t, sg_.v
            fw.dma(SP, sg_.schan, stv.rearrange("p (a c) -> p a c", a=4), win_v[:, 4 * half:4 * half + 4, cb * 512:(cb + 1) * 512], writes=[st])
            for dl in range(4):
                dc = 4 * half + dl
                cast(w_in_bf[:, dc, cb * 512:(cb + 1) * 512], stv[:, dl * 512:(dl + 1) * 512], vecs[:, V_NG + dc:V_NG + dc + 1], [st, vecs], [w_in_cb[cb]])
    sg_ = stg_rot.next()
    st, stv = sg_.t, sg_.v
    fw.dma(SP, sg_.schan, stv[:, 0:1024].rearrange("p (a h j) -> p a h j", a=2, h=4), wg_d, writes=[st])
    fw.op(DVE, lambda e: e.tensor_copy(out=wg_bf[:].rearrange("p a h j -> p (a h j)"), in_=stv[:, 0:1024]), reads=[st], writes=[wg_bf])
    wout_v = wout_d.rearrange("(kc p) n -> p kc n", p=128)

    def load_w_out(g):
        sg_ = stg_rot.next()
        st, stv = sg_.t, sg_.v
        fw.dma(SP, sg_.schan, stv.rearrange("p (l n) -> p l n", l=2), wout_v[:, 2 * g:2 * g + 2, :], writes=[st])
        for l in range(2):
            kc = 2 * g + l
            if kc % 8 >= 4:
                hglob = (kc // 8) * 4 + (kc % 8 - 4)
                sc_ap = vecs[:, V_RG + hglob:V_RG + hglob + 1]
            else:
                sc_ap = None
            for nh in range(2):
                cast(w_out_bf[:, kc, nh * 512:(nh + 1) * 512], stv[:, l * 1024 + nh * 512:l * 1024 + (nh + 1) * 512], sc_ap, [st, vecs], [w_out_bf])

    pid = nc.partition_id([mybir.EngineType.Pool])
    par = pid % 2

    def stage_A(s):
        xnT = xnT_rot[s % 2]
        fw.cur_prio = 2
        chunks = TILE_CHUNKS[s]
        nch = len(chunks)
        ssq = ssq_rot.next()
        rstd = rstd_rot.next()
        xs_l = []
        cst = cs_rot.next()
        cs_of_tile[s] = cst
        fw.dma(SP, c_cs[s % 2], cst[:, 0:nch, :], cs_d[:, chunks[0]:chunks[0] + nch, :], writes=[cst])
        for j, n in enumerate(chunks):
            xs = xs_rot.next()
            xs_l.append(xs)
            if n == 0:
                fw.op(POOL, lambda e: e.memset(xs[:], 0.0), writes=[xs])
                fw.dma(SP, xs.chan, xs[128 - N_META:128, :], meta_d, writes=[xs])
            else:
                fw.dma(SP, xs.chan, xs[:], x_d[(n - 1) * 128:n * 128, :], writes=[xs])
            xnj = xn_rot.tiles[(xn_rot.i + (j % 2 if nch > 1 else 0)) % 2]
            fw.op(ACT, lambda e: e.activation(out=xnj[:], in_=xs[:], func=AF.Square, accum_out=ssq[:, j:j + 1]), reads=[xs], writes=[xnj, ssq])
            if (nch > 1 and j % 2 == 1) or nch == 1:
                j0 = j - 1 if nch > 1 else 0
                fw.op(POOL, lambda e: e.tensor_scalar(out=rstd[:, j0:j + 1], in0=ssq[:, j0:j + 1], scalar1=1.0 / D_MODEL, scalar2=EPS, op0=ALU.mult, op1=ALU.add), reads=[ssq], writes=[rstd])
                fw.op(POOL, lambda e: e.tensor_tensor(out=rstd[:, j0:j + 1], in0=rstd[:, j0:j + 1], in1=negh[:, j0:j + 1], op=ALU.pow), reads=[rstd, negh], writes=[rstd])
                for jj in range(j0, j + 1):
                    xs2 = xs_l[jj]
                    xn = xn_rot.next()
                    fw.op(DVE, lambda e: e.tensor_scalar_mul(out=xn[:], in0=xs2[:], scalar1=rstd[:, jj:jj + 1]), reads=[xs2, rstd], writes=[xn])
                    ptr = p_tr.next()
                    pv = bfv(ptr)
                    for dc in range(8):
                        fw.op(PE, lambda e: e.transpose(out=pv[:, dc * 128:(dc + 1) * 128], in_=xn[:, dc * 128:(dc + 1) * 128], identity=ident[:]),
                              reads=[xn, ident], writes=[ptr], inc=(dc == 7))
                    fw.op(DVE, lambda e: e.tensor_copy(out=xnT[:, :, jj * 128:(jj + 1) * 128], in_=pv.rearrange("p (a t) -> p a t", a=8)),
                          reads=[ptr], writes=[xnT])

    def stage_B1(s):
        sg = sg_all[s % 2]
        xnT = xnT_rot[s % 2]
        fw.cur_prio = 0
        NT = 128 * len(TILE_CHUNKS[s])
        lxc = lx[s % 2]
        for ec in (4, 0, 5, 1, 6, 2, 7, 3):
            if ec >= 4 and s == 0:
                continue
            pb = p_in.next()
            for dc in range(8):
                fw.op(PE, lambda e: e.matmul(pb[:, 0:NT], lhsT=w_in_bf[:, dc, ec * 128:(ec + 1) * 128], rhs=xnT[:, dc, 0:NT], start=(dc == 0), stop=(dc == 7)),
                      reads=[w_in_cb[ec // 4], xnT], writes=[pb], inc=(dc == 7))
            if ec < 4:
                h = ec
                fw.op(DVE, lambda e: e.tensor_copy(out=lxc[:, h, 3:3 + NT], in_=pb[:, 0:NT]), reads=[pb], writes=[lxc])
            else:
                h = ec - 4
                fw.op(ACT, lambda e: e.activation(out=thg[:, 0:NT], in_=pb[:, 0:NT], func=AF.Tanh, scale=0.5), reads=[pb], writes=[thg])
                fw.op(ACT, lambda e: e.activation(out=ghalf[:, 0:NT], in_=pb[:, 0:NT], func=AF.Copy, scale=0.5), reads=[pb], writes=[ghalf])
                fw.op(DVE, lambda e: e.scalar_tensor_tensor(out=sg[h][:, 0:NT], in0=thg[:, 0:NT], scalar=1.0, in1=ghalf[:, 0:NT], op0=ALU.add, op1=ALU.mult),
                      reads=[thg, ghalf], writes=[sg[h]])

    def stage_halo(s):
        NT = 128 * len(TILE_CHUNKS[s])
        lxc = lx[s % 2]
        nxt = lx[(s + 1) % 2]
        fw.op(POOL, lambda e: e.tensor_copy(out=nxt[:, :, 0:3], in_=lxc[:, :, NT:NT + 3]), reads=[lxc], writes=[nxt])

    def stage_C(s, h):
        sg = sg_all[s % 2]
        fw.cur_prio = 0
        NT = 128 * len(TILE_CHUNKS[s])
        lxc = lx[s % 2]
        pc = p_lru.next()
        for k in range(4):
            fw.op(PE, lambda e: e.matmul(pc[:, 0:NT], lhsT=diag[:, h, k, :], rhs=lxc[:, h, k:k + NT], start=(k == 0), stop=(k == 3)),
                  reads=[diag, lxc], writes=[pc], inc=(k == 3))
        xcb_, xch_, thr_, thi_, a_, a2_ = xcb.next(), xch.next(), thr.next(), thi.next(), a_t.next(), a2_t.next()
        fw.op(ACT, lambda e: e.activation(out=xcb_[:, 0:NT], in_=pc[:, 0:NT], func=AF.Identity, bias=vecs[:, V_CB + h:V_CB + h + 1], scale=1.0),
              reads=[pc, vecs], writes=[xcb_])
        fw.op(ACT, lambda e: e.activation(out=xch_[:, 0:NT], in_=pc[:, 0:NT], func=AF.Identity, bias=hv[:, h:h + 1], scale=0.5),
              reads=[pc, hv], writes=[xch_])
        pr = p_lru.next()
        fw.op(PE, lambda e: e.matmul(pr[:, 0:NT], lhsT=wg_bf[:, 0, h, :], rhs=xcb_[:, 0:NT], start=True, stop=True), reads=[wg_bf, xcb_], writes=[pr])
        pi = p_lru.next()
        fw.op(PE, lambda e: e.matmul(pi[:, 0:NT], lhsT=wg_bf[:, 1, h, :], rhs=xcb_[:, 0:NT], start=True, stop=True), reads=[wg_bf, xcb_], writes=[pi])
        fw.op(ACT, lambda e: e.activation(out=thr_[:, 0:NT], in_=pr[:, 0:NT], func=AF.Tanh, bias=hv[:, 4 + h:5 + h], scale=0.5), reads=[pr, hv], writes=[thr_])
        fw.op(ACT, lambda e: e.activation(out=thi_[:, 0:NT], in_=pi[:, 0:NT], func=AF.Tanh, bias=hv[:, 8 + h:9 + h], scale=0.5), reads=[pi, hv], writes=[thi_])
        fw.op(ACT, lambda e: e.activation(out=a_[:, 0:NT], in_=thr_[:, 0:NT], func=AF.Exp, bias=hv[:, 12 + h:13 + h], scale=hv[:, 12 + h:13 + h]), reads=[thr_, hv], writes=[a_])
        fw.op(ACT, lambda e: e.activation(out=a2_[:, 0:NT], in_=thr_[:, 0:NT], func=AF.Exp, bias=hv[:, 16 + h:17 + h], scale=hv[:, 16 + h:17 + h]), reads=[thr_, hv], writes=[a2_])
        fw.op(ACT, lambda e: e.activation(out=a2_[:, 0:NT], in_=a2_[:, 0:NT], func=AF.Relu, bias=1.0, scale=-1.0), reads=[a2_], writes=[a2_])
        fw.op(ACT, lambda e: e.activation(out=a2_[:, 0:NT], in_=a2_[:, 0:NT], func=AF.Sqrt), reads=[a2_], writes=[a2_])
        fw.op(DVE, lambda e: e.scalar_tensor_tensor(out=thi_[:, 0:NT], in0=thi_[:, 0:NT], scalar=1.0, in1=xch_[:, 0:NT], op0=ALU.add, op1=ALU.mult),
              reads=[thi_, xch_], writes=[thi_])
        fw.op(DVE, lambda e: e.tensor_tensor(out=a2_[:, 0:NT], in0=a2_[:, 0:NT], in1=thi_[:, 0:NT], op=ALU.mult), reads=[a2_, thi_], writes=[a2_])
        c0 = 128 - N_META if s == 0 else 0
        fw.op(DVE, lambda e: e.tensor_tensor_scan(out=xch_[:, c0:NT], data0=a_[:, c0:NT], data1=a2_[:, c0:NT], initial=hst[h][:, 0:1], op0=ALU.mult, op1=ALU.add),
              reads=[a_, a2_, hst[h], xch_], writes=[xch_])
        fw.op(POOL, lambda e: e.tensor_copy(out=hst[h][:, 0:1], in_=xch_[:, NT - 1:NT]), reads=[xch_], writes=[hst[h]])
        if s > 0:
            fw.op(DVE, lambda e: e.tensor_tensor(out=yT[h][:, 0:NT], in0=sg[h][:, 0:NT], in1=xch_[:, 0:NT], op=ALU.mult), reads=[sg[h], xch_], writes=[yT[h]])

    def stage_D(s, j):
        xnT = xnT_rot[s % 2]
        fw.cur_prio = 0
        n = TILE_CHUNKS[s][j]
        tsl = slice(j * 128, (j + 1) * 128)

        def proj(c0):
            pb = p_in.next()
            for dc in range(8):
                fw.op(PE, lambda e: e.matmul(pb[:, :], lhsT=xnT[:, dc, tsl], rhs=w_in_bf[:, dc, c0:c0 + 512], start=(dc == 0), stop=(dc == 7)),
                      reads=[xnT, w_in_cb[c0 // 512]], writes=[pb], inc=(dc == 7))
            return pb

        p_qk = proj(1024)
        qks = qk_sb.next()
        fw.op(ACT, lambda e: e.activation(out=qks[:], in_=p_qk[:, :], func=AF.Copy), reads=[p_qk], writes=[qks])
        p_v = proj(1536)
        vb = v_bf.next()
        fw.op(DVE, lambda e: e.tensor_copy(out=vb[:], in_=p_v[:, :]), reads=[p_v], writes=[vb])
        if s > 0:
            p_g = proj(2048)
            fw.op(ACT, lambda e: e.activation(out=thrg[:], in_=p_g[:, :], func=AF.Tanh, scale=0.5), reads=[p_g], writes=[thrg])
            fw.op(ACT, lambda e: e.activation(out=rghalf[:], in_=p_g[:, :], func=AF.Copy, scale=0.5), reads=[p_g], writes=[rghalf])
            sgr_ = sgr_rot.next()
            fw.op(DVE, lambda e: e.scalar_tensor_tensor(out=sgr_[:], in0=thrg[:], scalar=1.0, in1=rghalf[:], op0=ALU.add, op1=ALU.mult),
                  reads=[thrg, rghalf], writes=[sgr_])
        qv = qks.ap.rearrange("p (g t f) -> p g t f", g=8, t=2)
        rv = qk_rot.ap.rearrange("p (g t f) -> p g t f", g=8, t=2)
        t1, t2 = qv[:, :, 0, :], qv[:, :, 1, :]
        cs = cs_of_tile[s]
        cosb = cs[:, j, 0:32].unsqueeze(1).to_broadcast([128, 8, 32])
        sinb = cs[:, j, 32:64].unsqueeze(1).to_broadcast([128, 8, 32])
        tA = tmpA.ap.rearrange("p (g f) -> p g f", g=8)
        tB = tmpB.ap.rearrange("p (g f) -> p g f", g=8)
        fw.op(POOL, lambda e: e.tensor_tensor(out=tA, in0=t1, in1=cosb, op=ALU.mult), reads=[qks, cs], writes=[tmpA])
        fw.op(POOL, lambda e: e.tensor_tensor(out=tB, in0=t2, in1=sinb, op=ALU.mult), reads=[qks, cs], writes=[tmpB])
        fw.op(POOL, lambda e: e.tensor_tensor(out=rv[:, :, 0, :], in0=tA, in1=tB, op=ALU.subtract), reads=[tmpA, tmpB], writes=[qk_rot])
        fw.op(POOL, lambda e: e.tensor_tensor(out=tA, in0=t1, in1=sinb, op=ALU.mult), reads=[qks, cs], writes=[tmpA])
        fw.op(POOL, lambda e: e.tensor_tensor(out=tB, in0=t2, in1=cosb, op=ALU.mult), reads=[qks, cs], writes=[tmpB])
        fw.op(POOL, lambda e: e.tensor_tensor(out=rv[:, :, 1, :], in0=tA, in1=tB, op=ALU.add), reads=[tmpA, tmpB], writes=[qk_rot])
        qd_, kd_, kdec_ = qd.next(), kd.next(), kdec.next()
        qr = qk_rot.ap[:, 0:256].rearrange("p (h f) -> p h f", h=4)
        kr = qk_rot.ap[:, 256:512].rearrange("p (h f) -> p h f", h=4)

        def dcol(c):
            return dec[:, c:c + 4].unsqueeze(2).to_broadcast([128, 4, 64])

        fw.op(POOL, lambda e: e.tensor_tensor(out=qd_[:], in0=qr, in1=dcol(DC_Q), op=ALU.mult), reads=[qk_rot, dec], writes=[qd_])
        fw.op(POOL, lambda e: e.tensor_tensor(out=kd_[:], in0=kr, in1=dcol(DC_KI), op=ALU.mult), reads=[qk_rot, dec], writes=[kd_])
        fw.op(POOL, lambda e: e.tensor_tensor(out=kdec_[:], in0=kr, in1=dcol(DC_KD), op=ALU.mult), reads=[qk_rot, dec], writes=[kdec_])
        ptr = p_tr.next()
        pv = bfv(ptr)[0:64, :].rearrange("p (a t) -> p a t", a=8)
        for h in range(4):
            fw.op(PE, lambda e: e.transpose(out=pv[:, h, :], in_=qd_[:, h, :], identity=ident[:]), reads=[qd_, ident], writes=[ptr], inc=False)
        for h in range(4):
            fw.op(PE, lambda e: e.transpose(out=pv[:, 4 + h, :], in_=kd_[:, h, :], identity=ident[:]), reads=[kd_, ident], writes=[ptr], inc=(h == 3))
        qkT_ = qkT.next()
        fw.op(DVE, lambda e: e.tensor_copy(out=qkT_[:], in_=pv), reads=[ptr], writes=[qkT_])
        ps = p_ret.next()
        psv = ps.ap.rearrange("p (h c) -> p h c", h=4)
        for h in range(4):
            fw.op(PE, lambda e: e.matmul(psv[:, h, :], lhsT=qkT_[:, 4 + h, :], rhs=qkT_[:, h, :], start=True, stop=True), reads=[qkT_], writes=[ps], inc=(h == 3))
        sm_ = sm.next()
        fw.op(DVE, lambda e: e.tensor_tensor(out=sm_[:], in0=psv, in1=maskT[:].unsqueeze(1).to_broadcast([128, 4, 128]), op=ALU.mult), reads=[ps, maskT], writes=[sm_])
        po = p_ret.next()
        pov = po.ap.rearrange("p (h c) -> p h c", h=4)
        for h in range(4):
            fw.op(PE, lambda e: e.matmul(pov[:, h, :], lhsT=qkT_[:, h, :], rhs=Rbf[:, h, :], start=True, stop=False), reads=[qkT_, Rbf], writes=[po], inc=False)
            fw.op(PE, lambda e: e.matmul(pov[:, h, :], lhsT=sm_[:, h, :], rhs=vb[:, h * 128:(h + 1) * 128], start=False, stop=True), reads=[sm_, vb], writes=[po], inc=(h == 3))
        pk = p_ret.next()
        pkv = pk.ap[0:64, :].rearrange("p (h c) -> p h c", h=4)
        for h in range(4):
            fw.op(PE, lambda e: e.matmul(pkv[:, h, :], lhsT=kdec_[:, h, :], rhs=vb[:, h * 128:(h + 1) * 128], start=True, stop=True), reads=[kdec_, vb], writes=[pk], inc=(h == 3))
        if s > 0:
            o_ = o_sb_rot.next()
            mv_ = mv_rot.next()
            rs_ = rs_rot.next()
            fw.op(DVE, lambda e: e.tensor_copy(out=o_[:], in_=pov), reads=[po], writes=[o_])
            for h in range(4):
                fw.op(DVE, lambda e: e.bn_stats(out=bst[:, h, :], in_=o_[:, h, :]), reads=[o_], writes=[bst_h[h]])
            for h in range(4):
                fw.op(DVE, lambda e: e.bn_aggr(out=mv_[:, h, :], in_=bst[:, h, :]), reads=[bst_h[h]], writes=[mv_.parts[h]])
        for h in range(4):
            fw.op(DVE, lambda e: e.scalar_tensor_tensor(out=R[:, h, :], in0=R[:, h, :], scalar=dec[0:64, DC_G128 + h:DC_G128 + h + 1], in1=pkv[:, h, :], op0=ALU.mult, op1=ALU.add),
                  reads=[R_h[h], dec, pk], writes=[R_h[h]])
        fw.op(DVE, lambda e: e.tensor_copy(out=Rbf[:], in_=R[:]), reads=R_h, writes=[Rbf])
        if s > 0:
            fw.op(POOL, lambda e: e.tensor_scalar(out=rs_[:], in0=mv_[:, :, 1], scalar1=1.0, scalar2=EPS, op0=ALU.mult, op1=ALU.add), reads=mv_.parts, writes=[rs_])
            fw.op(POOL, lambda e: e.tensor_tensor(out=rs_[:], in0=rs_[:], in1=negh[:, 0:4], op=ALU.pow), reads=[rs_, negh], writes=[rs_])
            mean_b = mv_[:, :, 0:1].to_broadcast([128, 4, 128])
            rs_b = rs_[:].unsqueeze(2).to_broadcast([128, 4, 128])
            fw.op(POOL, lambda e: e.tensor_tensor(out=o_[:], in0=o_[:], in1=mean_b, op=ALU.subtract), reads=[o_] + mv_.parts, writes=[o_])
            fw.op(POOL, lambda e: e.tensor_tensor(out=on[:], in0=o_[:], in1=rs_b, op=ALU.mult), reads=[o_, rs_], writes=[on])
            yr_ = yr.next()
            fw.op(DVE, lambda e: e.tensor_tensor(out=yr_[:].rearrange("p h c -> p (h c)"), in0=on[:].rearrange("p h c -> p (h c)"), in1=sgr_[:], op=ALU.mult),
                  reads=[on, sgr_], writes=[yr_])
            ptr2 = p_tr.next()
            pv2 = bfv(ptr2)[:, 0:512].rearrange("p (h t) -> p h t", h=4)
            for h in range(4):
                fw.op(PE, lambda e: e.transpose(out=pv2[:, h, :], in_=yr_[:, h, :], identity=ident[:]), reads=[yr_, ident], writes=[ptr2], inc=(h == 3))
            fw.op(DVE, lambda e: e.tensor_copy(out=yT_all[:, 4:8, j * 128:(j + 1) * 128], in_=pv2), reads=[ptr2], writes=yT[4:8])

    def stage_E1(s):
        fw.cur_prio = 0
        fw.dma(SP, c_ysrc, ysrc_d[s].ap().rearrange("(k p) t -> p k t", p=128), yT_all[:, :, 0:TILE_NT[s]], reads=yT, writes=[ysrc_t[s]])
        fw.custom(POOL, c_cc, lambda e: e.collective_compute("AllGather", ALU.bypass, replica_groups=[[0, 1], [2, 3], [4, 5], [6, 7]],
                                                              ins=[ysrc_d[s].ap().opt()], outs=[ydst_d[s].ap().opt()]),
                  reads=[ysrc_t[s]], writes=[ydst_t[s]])

    def stage_E2(s):
        NH = TILE_NT[s] // 2
        nj2 = NH // 128
        off = TILE_T0[s] // 2
        fw.dma(POOL, c_yfull, yfull[:, :, 0:NH], ydst_d[s].ap().rearrange("(k p) t -> p k t", p=128)[:, :, bass.ds(par * NH, NH)], reads=[ydst_t[s]], writes=[yfull])
        xr = xr_rot.next()
        fw.dma(SP, xr.chan, xr[:, 0:nj2, :], xres_d[off:off + NH, :].rearrange("(j p) d -> p j d", p=128), writes=[xr])
        for j2 in range(nj2):
            for nh in range(2):
                pb = p_out.next()
                for kc in range(16):
                    fw.op(PE, lambda e: e.matmul(pb[:, :], lhsT=yfull[:, kc, j2 * 128:(j2 + 1) * 128], rhs=w_out_bf[:, kc, nh * 512:(nh + 1) * 512], start=(kc == 0), stop=(kc == 15)),
                          reads=[yfull, w_out_bf], writes=[pb], inc=(kc == 15))
                fw.op(DVE, lambda e: e.tensor_tensor(out=xr[:, j2, nh * 512:(nh + 1) * 512], in0=xr[:, j2, nh * 512:(nh + 1) * 512], in1=pb[:, :], op=ALU.add),
                      reads=[xr, pb], writes=[xr])
                fw.op(ACT, lambda e: e.activation(out=pb[:, :], in_=xr[:, j2, nh * 512:(nh + 1) * 512], func=AF.Square, accum_out=fss4[:, j2 * 2 + nh:j2 * 2 + nh + 1]),
                      reads=[xr], writes=[pb, fss4])
        f4 = fss4.ap.rearrange("p (j n) -> p j n", n=2)
        fw.op(POOL, lambda e: e.tensor_tensor(out=frs[:, 0:nj2], in0=f4[:, 0:nj2, 0], in1=f4[:, 0:nj2, 1], op=ALU.add), reads=[fss4], writes=[frs])
        fw.op(POOL, lambda e: e.tensor_scalar(out=frs[:, 0:nj2], in0=frs[:, 0:nj2], scalar1=1.0 / D_MODEL, scalar2=EPS, op0=ALU.mult, op1=ALU.add), reads=[frs], writes=[frs])
        fw.op(POOL, lambda e: e.tensor_tensor(out=frs[:, 0:nj2], in0=frs[:, 0:nj2], in1=negh[:, 0:nj2], op=ALU.pow), reads=[frs, negh], writes=[frs])
        for j2 in range(nj2):
            fw.op(DVE, lambda e: e.scalar_tensor_tensor(out=xr[:, j2, :], in0=xr[:, j2, :], scalar=frs[:, j2:j2 + 1], in1=fgain[:], op0=ALU.mult, op1=ALU.mult),
                  reads=[xr, frs, fgain], writes=[xr])
        fw.dma(SP, c_out, out_d[off:off + NH, :].rearrange("(j p) d -> p j d", p=128), xr[:, 0:nj2, :], reads=[xr], writes=[out_t])

    marks = {}
    stage_A(0)
    for g in range(8):
        load_w_out(g)
    marks["A0"] = fw.ops[-1]
    for s in range(NTILES):
        stage_B1(s)
        if s + 1 < NTILES:
            stage_A(s + 1)
        nch = len(TILE_CHUNKS[s])
        hpc = 4 // nch
        for j in range(nch):
            stage_D(s, j)
            for h in range(j * hpc, (j + 1) * hpc):
                stage_C(s, h)
        stage_halo(s)
        if s > 0:
            stage_E1(s)
        if s > 0:
            stage_E2(s)
        marks[f"t{s}"] = fw.ops[-1]
    fw.finish(POOL, [out_t])
    fw.finish(SP, [out_t])
    best = None
    for trial in (range(N_SCHED_TRIALS) if FORCE_TRIAL is None else [FORCE_TRIAL]):
        if trial == 0:
            order = fw.schedule()
        else:
            order = fw.schedule(seed=trial, noise=0.02 * (1 + trial % 5), win=(0.1e-6, 0.3e-6, 0.6e-6)[trial % 3])
        if best is None or fw.sim_end < best[0]:
            best = (fw.sim_end, list(order), fw.n_switch, trial)
    fw.sim_end, order, fw.n_switch, best_trial = best
    fw.emit(order)
    stats = {e.name: (e.nins, e.nwaits) for e in (PE, ACT, DVE, POOL, SP)}
    stats["fw"] = fw
    stats["order"] = order
    stats["sim_end_us"] = fw.sim_end * 1e6
    stats["n_switch"] = fw.n_switch
    stats["best_trial"] = best_trial
    stats["sbuf_bytes"] = sb_bytes[0] + 8192
    stats["marks"] = {k: round(v.fin * 1e6, 1) for k, v in marks.items()}
    stats["sim_busy_us"] = {k: round(v * 1e6, 1) for k, v in fw.sim_busy.items()}
    return nc, stats


def _consts():
    half = 32
    inv = np.power(np.float64(10000.0), -(np.arange(half, dtype=np.float64) / np.float64(half)))
    c = np.arange(128)[:, None]
    n = np.arange(NCHUNK)[None, :]
    pos = np.maximum(n * 128 + c - (128 - N_META), 0).astype(np.float64)
    ang = pos[:, :, None] * inv[None, None, :]
    cs = np.concatenate([np.cos(ang), np.sin(ang)], axis=-1).astype(np.float32)
    ident = np.eye(128, dtype=np.float32).astype(ml_dtypes.bfloat16)
    m = np.arange(128)[:, None]
    cc = np.arange(128)[None, :]
    maskT = (cc >= m).astype(np.float32).astype(ml_dtypes.bfloat16)
    return cs, ident, maskT


def _dec_table(heads):
    log_g = np.log1p(-np.exp2(-5.0 - np.arange(8, dtype=np.float64)))
    c = np.arange(128, dtype=np.float64)[:, None]
    lg = log_g[heads][None, :]
    dec = np.zeros((128, NDEC), np.float32)
    dec[:, DC_Q:DC_Q + 4] = np.exp((c + 1.0) * lg)
    dec[:, DC_KI:DC_KI + 4] = np.exp(-(c + 1.0) * lg) * 0.125
    dec[:, DC_KD:DC_KD + 4] = np.exp((127.0 - c) * lg) * 0.125
    dec[:, DC_G128:DC_G128 + 4] = np.exp(128.0 * lg)
    return dec


_PROG = None


def kernel(x, meta_tokens, norm_gain, w_in, conv_w, conv_b, w_rg, b_rg, w_ig, b_ig,
           lru_lambda, ret_norm_gain, w_out, final_norm_gain):
    global _PROG
    f = lambda a: np.ascontiguousarray(np.asarray(a), dtype=np.float32)
    x, meta_tokens, norm_gain, w_in = f(x), f(meta_tokens), f(norm_gain), f(w_in)
    conv_w, conv_b, w_rg, b_rg, w_ig, b_ig = f(conv_w), f(conv_b), f(w_rg), f(b_rg), f(w_ig), f(b_ig)
    lru_lambda, ret_norm_gain, w_out, final_norm_gain = f(lru_lambda), f(ret_norm_gain), f(w_out), f(final_norm_gain)
    if _PROG is None:
        _PROG = build_program()
    nc = _PROG[0]
    cs, ident, maskT = _consts()
    w_in0 = w_in[0]
    perm = np.concatenate([np.arange(0, 512), np.arange(1024, 1536), np.arange(512, 1024), np.arange(1536, 2048)])
    w_out_p = np.ascontiguousarray(w_out[0][perm])
    fgain = np.ascontiguousarray(np.broadcast_to(final_norm_gain[None, :], (128, D_MODEL)))
    in_maps = []
    for c in range(8):
        b, p = c // 2, c % 2
        lch = np.arange(512 * p, 512 * p + 512)
        heads = np.arange(4 * p, 4 * p + 4)
        qk_cols = (heads[:, None] * 64 + np.arange(64)[None, :]).reshape(-1)
        v_cols = (heads[:, None] * 128 + np.arange(128)[None, :]).reshape(-1)
        cols = np.concatenate([lch, 1024 + lch, 2048 + qk_cols, 2560 + qk_cols, 3072 + v_cols, 4096 + v_cols])
        w_in_c = np.ascontiguousarray(w_in0[:, cols])
        wg = np.stack([w_rg[0, 4 * p:4 * p + 4], w_ig[0, 4 * p:4 * p + 4]], 0)
        wg = np.ascontiguousarray(wg.transpose(2, 0, 1, 3))
        vecs = np.zeros((128, NV), np.float32)
        for h in range(4):
            ch = 512 * p + h * 128 + np.arange(128)
            for k in range(4):
                vecs[:, V_CW + h * 4 + k] = conv_w[0, k, ch]
            vecs[:, V_CB + h] = conv_b[0, ch]
            vecs[:, V_BRG + h] = b_rg[0, ch]
            vecs[:, V_BIG + h] = b_ig[0, ch]
            vecs[:, V_LAM + h] = lru_lambda[0, ch]
        for dc in range(8):
            vecs[:, V_NG + dc] = norm_gain[0, dc * 128:(dc + 1) * 128]
        for h in range(8):
            vecs[:, V_RG + h] = ret_norm_gain[0, h * 128:(h + 1) * 128]
        xb = x[b]
        xres = np.ascontiguousarray(np.concatenate([xb[TILE_T0[t_] + p * (TILE_NT[t_] // 2):TILE_T0[t_] + (p + 1) * (TILE_NT[t_] // 2)] for t_ in range(1, NTILES)], 0))
        in_maps.append({
            "x": xb, "meta": meta_tokens, "xres": xres, "w_in": w_in_c, "w_out": w_out_p, "w_g": wg,
            "vecs": vecs, "fgain": fgain, "ident": ident, "cs": cs, "dec": _dec_table(heads), "maskT": maskT,
        })
    res = run_bass_kernel_spmd(nc, in_maps, core_ids=list(range(8)))
    out = np.zeros((BATCH, SEQ, D_MODEL), np.float32)
    for c in range(8):
        b, p = c // 2, c % 2
        o = np.asarray(res.results[c]["out"])
        for t_ in range(1, NTILES):
            nh_ = TILE_NT[t_] // 2
            out[b, TILE_T0[t_] + p * nh_:TILE_T0[t_] + (p + 1) * nh_] = o[TILE_T0[t_] // 2:TILE_T0[t_] // 2 + nh_]
    return out
```

```python
import numpy as np
import ml_dtypes
import concourse.bass as bass
import concourse.mybir as mybir
from concourse.bass_utils import run_bass_kernel_spmd

F32 = mybir.dt.float32
BF16 = mybir.dt.bfloat16
ALU = mybir.AluOpType
AF = mybir.ActivationFunctionType

D_MODEL = 1024
BATCH = 4
SEQ = 4096
N_META = 16
CH = 128
NCHUNK = 33
EPS = 1e-6
N_SCHED_TRIALS = 300
FORCE_TRIAL = 141
TILE_NCH = [1, 4, 4, 4, 4, 4, 4, 4, 2, 2]
TILE_CHUNKS = []
_c = 0
for _n in TILE_NCH:
    TILE_CHUNKS.append(list(range(_c, _c + _n)))
    _c += _n
assert _c == NCHUNK
NTILES = len(TILE_CHUNKS)
TILE_T0 = [None] + [(TILE_CHUNKS[s][0] - 1) * 128 for s in range(1, NTILES)]
TILE_NT = [128 * n for n in TILE_NCH]

V_CW = 0
V_CB = 16
V_BRG = 20
V_BIG = 24
V_LAM = 28
V_NG = 32
V_RG = 40
NV = 48
DC_Q = 0
DC_KI = 4
DC_KD = 8
DC_G128 = 12
NDEC = 16


class Src:
    def __init__(self, name, sem, step):
        self.name, self.sem, self.step = name, sem, step
        self.count = 0
        self.snap = {}


class Eng(Src):
    def __init__(self, name, sem, handle):
        super().__init__(name, sem, 1)
        self.h = handle
        self.know = {}
        self.nwaits = 0
        self.nins = 0


class T:
    def __init__(self, ap, name=""):
        self.ap = ap
        self.name = name
        self.w = {}
        self.r = {}
        self.lw = None
        self.lr = []

    def __getitem__(self, k):
        return self.ap[k]


class _Proxy:
    def __init__(self):
        self.calls = []

    def __getattr__(self, name):
        def f(*a, **k):
            self.calls.append((name, a, k))
            return None
        return f


def _free_size(ap):
    n = 1
    for d in ap.shape[1:]:
        n *= int(d)
    return n


class Op:
    __slots__ = ("prio", "tbl", "eng", "calls", "reads", "writes", "kind", "ch", "dur", "lat", "idx", "deps", "nsucc", "succ", "ndeps", "start", "fin", "closed", "bytes")


class FW:
    DMA_BW = 150e9
    TBL_SWITCH = 2.7e-6

    def __init__(self, nc):
        self.nc = nc
        self.pe = self._eng("pe", nc.tensor)
        self.act = self._eng("act", nc.scalar)
        self.dve = self._eng("dve", nc.vector)
        self.pool = self._eng("pool", nc.gpsimd)
        self.sp = self._eng("sp", nc.sync)
        self.ops = []
        self.open = None
        self.cur_prio = 0

    def _eng(self, name, handle):
        return Eng(name, self.nc.alloc_semaphore("s_" + name), handle)

    def chan(self, name, step=16):
        return Src(name, self.nc.alloc_semaphore("c_" + name), step)

    def _est(self, eng, name, a, k):
        out = k.get("out", a[0] if a else None)
        n = _free_size(out) if out is not None and hasattr(out, "shape") else 64
        if eng is self.pe:
            return max(64, n) / 2.0e9 + 0.02e-6
        if eng is self.act:
            return 0.30e-6 + n * 0.84e-9
        if eng is self.dve:
            return 0.17e-6 + n * 1.05e-9
        return 0.32e-6 + n * 1.5e-9

    def _new(self, eng, kind):
        o = Op()
        o.eng, o.kind, o.calls, o.reads, o.writes = eng, kind, [], [], []
        o.ch, o.dur, o.lat, o.idx, o.closed, o.bytes = None, 0.0, 0.0, len(self.ops), True, 0
        o.tbl = None
        o.prio = self.cur_prio
        self.ops.append(o)
        return o

    def op(self, eng, fn, reads=(), writes=(), inc=True):
        p = _Proxy()
        fn(p)
        if self.open is not None and self.open.eng is eng:
            o = self.open
        else:
            assert self.open is None, "unterminated inc=False group"
            o = self._new(eng, "op")
        for (name, a, k) in p.calls:
            o.calls.append((name, a, k))
            o.dur += self._est(eng, name, a, k)
            if name == "activation":
                f_ = k.get("func")
                if f_ == AF.Sqrt:
                    o.tbl = "sqrt"
                elif f_ in (AF.Exp, AF.Tanh):
                    o.tbl = "exp"
                elif f_ == AF.Ln:
                    o.tbl = "ln"
        for t in reads:
            if t not in o.reads:
                o.reads.append(t)
        for t in writes:
            if t not in o.writes:
                o.writes.append(t)
        self.open = None if inc else o

    def dma(self, q, ch, out, in_, reads=(), writes=(), **kw):
        assert self.open is None
        o = self._new(q, "dma")
        o.calls.append(("dma_start", (), dict(out=out, in_=in_, **kw)))
        o.ch = ch
        o.reads, o.writes = list(reads), list(writes)
        nbytes = 1
        for d in out.shape:
            nbytes *= int(d)
        o.bytes = nbytes * (2 if out.dtype == BF16 else 4)
        o.dur = 0.08e-6 if q is self.sp else 1.0e-6
        o.lat = 2.0e-6

    def custom(self, q, ch, fn, reads=(), writes=(), lat=25e-6):
        assert self.open is None
        p = _Proxy()
        fn(p)
        o = self._new(q, "custom")
        o.calls = list(p.calls)
        o.ch = ch
        o.reads, o.writes = list(reads), list(writes)
        o.dur = 1.0e-6
        o.lat = lat

    def finish(self, q, tiles):
        assert self.open is None
        o = self._new(q, "finish")
        o.reads = list(tiles)
        o.dur = 0.05e-6

    def _all_tiles(self):
        seen = {}
        for o in self.ops:
            for t in o.reads:
                seen[id(t)] = t
            for t in o.writes:
                seen[id(t)] = t
        return seen.values()

    def schedule(self, seed=None, noise=0.0, win=0.3e-6):
        import random
        rng = random.Random(seed)
        ops = self.ops
        for t_ in self._all_tiles():
            t_.lw = None
            t_.lr = []
        for o in ops:
            deps = set()
            for t in o.reads:
                if t.lw is not None:
                    deps.add(t.lw)
            for t in o.writes:
                if t.lw is not None:
                    deps.add(t.lw)
                for r in t.lr:
                    deps.add(r)
            deps.discard(o)
            o.deps = deps
            for t in o.reads:
                t.lr.append(o)
            for t in o.writes:
                t.lw = o
                t.lr = []
        for o in ops:
            o.succ = []
            o.ndeps = len(o.deps)
            o.start = None
        for o in ops:
            for d in o.deps:
                d.succ.append(o)
        engs = [self.pe, self.act, self.dve, self.pool, self.sp]
        bl = {}
        for o in reversed(ops):
            m = 0.0
            for s_ in o.succ:
                if bl[s_] > m:
                    m = bl[s_]
            extra = o.lat + (o.bytes / self.DMA_BW if o.kind == "dma" else 0.0)
            bl[o] = m + o.dur + extra
        if noise > 0.0:
            blp = {o: v * (1.0 + noise * (rng.random() - 0.5)) for o, v in bl.items()}
        else:
            blp = bl
        free = {e: 0.0 for e in engs}
        dma_free = 0.0
        ready = {e: [] for e in engs}
        for o in ops:
            if o.ndeps == 0:
                ready[o.eng].append(o)
        order = []
        n = len(ops)
        WIN = win
        cur_tbl = ["exp"]
        self.n_switch = 0
        while len(order) < n:
            best = None
            for e in engs:
                fe = free[e]
                cands = []
                mn = None
                for o in ready[e]:
                    st = fe
                    if o.tbl is not None and e is self.act and o.tbl != cur_tbl[0]:
                        st = fe + self.TBL_SWITCH
                    for d in o.deps:
                        if d.fin > st:
                            st = d.fin
                    cands.append((st, o))
                    if mn is None or st < mn:
                        mn = st
                if mn is None:
                    continue
                pick = None
                for st, o in cands:
                    if st <= mn + WIN:
                        if pick is None or (o.prio, blp[o]) > (pick[1].prio, blp[pick[1]]):
                            pick = (st, o)
                key = (pick[0], -blp[pick[1]])
                if best is None or key < best[0]:
                    best = (key, pick[1], pick[0])
            _, o, st = best
            ready[o.eng].remove(o)
            o.start = st
            free[o.eng] = st + o.dur
            if o.tbl is not None and o.eng is self.act and o.tbl != cur_tbl[0]:
                cur_tbl[0] = o.tbl
                self.n_switch += 1
            if o.kind == "dma":
                t0 = max(st + o.dur, dma_free)
                dma_free = t0 + o.bytes / self.DMA_BW
                o.fin = dma_free + o.lat
            elif o.kind == "custom":
                o.fin = st + o.dur + o.lat
            else:
                o.fin = st + o.dur
            order.append(o)
            for s_ in o.succ:
                s_.ndeps -= 1
                if s_.ndeps == 0:
                    ready[s_.eng].append(s_)
        self.sim_end = max(o.fin for o in ops)
        self.sim_busy = {e.name: sum(o.dur for o in ops if o.eng is e) for e in engs}
        return order

    def _merge(self, eng, src, idx):
        if eng.know.get(src, 0) < idx:
            eng.know[src] = idx
        for s2, v in src.snap.get(idx, {}).items():
            if eng.know.get(s2, 0) < v:
                eng.know[s2] = v

    def _waits(self, eng, reads, writes):
        needs = {}
        for t in reads:
            for s, i in t.w.items():
                if needs.get(s, 0) < i:
                    needs[s] = i
        for t in writes:
            for d in (t.w, t.r):
                for s, i in d.items():
                    if needs.get(s, 0) < i:
                        needs[s] = i
        for s, i in sorted(needs.items(), key=lambda kv: kv[0] is eng):
            if eng.know.get(s, 0) >= i:
                continue
            eng.h.wait_ge(s.sem, i * s.step)
            eng.nwaits += 1
            self._merge(eng, s, i)

    def _record(self, src, idx, reads, writes):
        for t in reads:
            if t.r.get(src, 0) < idx:
                t.r[src] = idx
        for t in writes:
            t.w = {src: idx}
            t.r = {}

    def emit(self, order):
        for o in order:
            eng = o.eng
            self._waits(eng, o.reads, o.writes)
            if o.kind == "finish":
                continue
            ins = None
            for (name, a, k) in o.calls:
                ins = getattr(eng.h, name)(*a, **k)
                eng.nins += 1
            if o.kind == "op":
                eng.count += 1
                ins.then_inc(eng.sem, 1)
                eng.snap[eng.count] = dict(eng.know)
                self._record(eng, eng.count, o.reads, o.writes)
            else:
                ch = o.ch
                ch.count += 1
                ins.then_inc(ch.sem, ch.step)
                ch.snap[ch.count] = dict(eng.know)
                self._record(ch, ch.count, o.reads, o.writes)


class Rot:
    def __init__(self, tiles):
        self.tiles = tiles
        self.i = 0

    def next(self):
        t = self.tiles[self.i % len(self.tiles)]
        self.i += 1
        return t


def build_program():
    nc = bass.Bass("TRN2", target_bir_lowering=False)
    fw = FW(nc)
    PE, ACT, DVE, POOL, SP = fw.pe, fw.act, fw.dve, fw.pool, fw.sp

    def din(name, shape, dt):
        return nc.dram_tensor(name, shape, dt, kind="ExternalInput").ap()

    x_d = din("x", [SEQ, D_MODEL], F32)
    meta_d = din("meta", [N_META, D_MODEL], F32)
    xres_d = din("xres", [SEQ // 2, D_MODEL], F32)
    win_d = din("w_in", [D_MODEL, 2560], F32)
    wout_d = din("w_out", [2048, D_MODEL], F32)
    wg_d = din("w_g", [128, 2, 4, 128], F32)
    vecs_d = din("vecs", [128, NV], F32)
    fgain_d = din("fgain", [128, D_MODEL], F32)
    ident_d = din("ident", [128, 128], BF16)
    cs_d = din("cs", [128, NCHUNK, 64], F32)
    dec_d = din("dec", [128, NDEC], F32)
    mask_d = din("maskT", [128, 128], BF16)
    out_d = nc.dram_tensor("out", [SEQ // 2, D_MODEL], F32, kind="ExternalOutput").ap()
    ysrc_d = [None] + [nc.dram_tensor(f"ysrc{s}", [1024, TILE_NT[s]], BF16) for s in range(1, NTILES)]
    ydst_d = [None] + [nc.dram_tensor(f"ydst{s}", [2048, TILE_NT[s]], BF16) for s in range(1, NTILES)]
    ysrc_t = [None] + [T(ysrc_d[s].ap(), f"ysrc{s}") for s in range(1, NTILES)]
    ydst_t = [None] + [T(ydst_d[s].ap(), f"ydst{s}") for s in range(1, NTILES)]
    out_t = T(out_d, "out")

    sb_bytes = [0]

    def sb(name, shape, dt):
        n = 1
        for d in shape[1:]:
            n *= d
        sb_bytes[0] += n * (2 if dt == BF16 else 4)
        return T(nc.alloc_sbuf_tensor(name, shape, dt).ap(), name)

    w_in_bf = sb("w_in_bf", [128, 8, 2560], BF16)
    w_in_cb = [T(w_in_bf.ap[:, :, cb * 512:(cb + 1) * 512], f"w_in_cb{cb}") for cb in range(5)]
    w_out_bf = sb("w_out_bf", [128, 16, 1024], BF16)
    wg_bf = sb("wg_bf", [128, 2, 4, 128], BF16)
    diag = sb("diag", [128, 4, 4, 128], BF16)
    fgain = sb("fgain_sb", [128, D_MODEL], F32)
    cs_rot = Rot([sb(f"cs_sb{i}", [128, 4, 64], F32) for i in range(2)])
    cs_of_tile = {}
    ident = sb("ident_sb", [128, 128], BF16)
    maskT = sb("mask_sb", [128, 128], BF16)
    vecs = sb("vecs_sb", [128, NV], F32)
    dec = sb("dec_sb", [128, NDEC], F32)
    negh = sb("negh", [128, 16], F32)
    posh = sb("posh", [128, 1], F32)
    hv = sb("hv", [128, 24], F32)
    xs_rot = Rot([sb(f"xs{i}", [128, D_MODEL], F32) for i in range(2)])
    xn_rot = Rot([sb(f"xn{i}", [128, D_MODEL], BF16) for i in range(2)])
    xnT_rot = [sb(f"xnT{i}", [128, 8, 512], BF16) for i in range(2)]
    ssq_rot = Rot([sb(f"ssq{i}", [128, 4], F32) for i in range(2)])
    rstd_rot = Rot([sb(f"rstd{i}", [128, 4], F32) for i in range(2)])
    lx = [sb(f"lx{i}", [128, 4, 515], BF16) for i in range(2)]
    thg = sb("thg", [128, 512], BF16)
    ghalf = sb("ghalf", [128, 512], BF16)
    sg_all = [[sb(f"sg{i}_{h}", [128, 512], BF16) for h in range(4)] for i in range(2)]
    NSET = 2
    xcb = Rot([sb(f"xcb{i}", [128, 512], BF16) for i in range(NSET)])
    xch = Rot([sb(f"xch{i}", [128, 512], F32) for i in range(NSET)])
    thr = Rot([sb(f"thr{i}", [128, 512], F32) for i in range(NSET)])
    thi = Rot([sb(f"thi{i}", [128, 512], F32) for i in range(NSET)])
    a_t = Rot([sb(f"a{i}", [128, 512], F32) for i in range(NSET)])
    a2_t = Rot([sb(f"a2{i}", [128, 512], F32) for i in range(NSET)])
    hst = [sb(f"hst{h}", [128, 1], F32) for h in range(4)]
    yT_all = nc.alloc_sbuf_tensor("yT_all", [128, 8, 512], BF16).ap()
    yT = [T(yT_all[:, k, :], f"yT{k}") for k in range(8)]
    qk_sb = Rot([sb(f"qk_sb{i}", [128, 512], F32) for i in range(1)])
    tmpA = sb("tmpA", [128, 256], F32)
    tmpB = sb("tmpB", [128, 256], F32)
    qk_rot = sb("qk_rot", [128, 512], F32)
    qd = Rot([sb(f"qd{i}", [128, 4, 64], BF16) for i in range(2)])
    kd = Rot([sb(f"kd{i}", [128, 4, 64], BF16) for i in range(2)])
    kdec = Rot([sb(f"kdec{i}", [128, 4, 64], BF16) for i in range(2)])
    v_bf = Rot([sb(f"v_bf{i}", [128, 512], BF16) for i in range(2)])
    thrg = thg
    rghalf = ghalf
    sgr_rot = Rot([sb(f"sgr{j}", [128, 512], BF16) for j in range(2)])
    qkT = Rot([sb(f"qkT{i}", [64, 8, 128], BF16) for i in range(2)])
    sm = Rot([sb(f"sm{i}", [128, 4, 128], BF16) for i in range(2)])
    o_sb_rot = Rot([sb(f"o_sb{j}", [128, 4, 128], F32) for j in range(2)])
    bst = sb("bst", [128, 4, 6], F32)
    bst_h = [T(bst.ap[:, h, :], f"bst{h}") for h in range(4)]
    mv_rot = Rot([sb(f"mv{i}", [128, 4, 2], F32) for i in range(2)])
    for t_ in mv_rot.tiles:
        t_.parts = [T(t_.ap[:, h, :], t_.name + f"_{h}") for h in range(4)]
    rs_rot = Rot([sb(f"rs{i}", [128, 4], F32) for i in range(2)])
    on = sb("on", [128, 4, 128], BF16)
    yr = Rot([sb(f"yr{i}", [128, 4, 128], BF16) for i in range(2)])
    R = sb("R", [64, 4, 128], F32)
    R_h = [T(R.ap[:, h, :], f"R{h}") for h in range(4)]
    Rbf = sb("Rbf", [64, 4, 128], BF16)
    yfull = sb("yfull", [128, 16, 256], BF16)
    xr_rot = Rot([sb(f"xr{i}", [128, 2, D_MODEL], F32) for i in range(1)])
    fss4 = sb("fss4", [128, 4], F32)
    frs = sb("frs", [128, 2], F32)

    banks = [T(nc.alloc_psum_tensor(f"pb{i}", [128, 512], F32).ap(), f"pb{i}") for i in range(8)]
    p_in = Rot(banks[0:2])
    p_out = Rot(banks[2:3])
    p_tr = Rot(banks[3:4])
    p_lru = Rot(banks[4:6])
    p_ret = Rot(banks[6:8])

    def bfv(bank):
        return bank.ap.bitcast(BF16)

    c_const = fw.chan("const")
    c_stage = [fw.chan("stage0"), fw.chan("stage1")]
    c_x = [fw.chan("x0"), fw.chan("x1")]
    c_cs = [fw.chan("cs0"), fw.chan("cs1")]
    c_ysrc = fw.chan("ysrc")
    c_cc = fw.chan("cc", step=1)
    c_yfull = fw.chan("yfull")
    c_xr = [fw.chan("xr0"), fw.chan("xr1")]
    c_out = fw.chan("out")
    for t_, i_ in zip(xs_rot.tiles, range(2)):
        t_.chan = c_x[i_]
    for t_, i_ in zip(xr_rot.tiles, range(2)):
        t_.chan = c_xr[i_]
    class _Stg:
        pass
    stg = []
    for i_, (t_, v_) in enumerate(((xr_rot.tiles[0], xr_rot.tiles[0].ap.rearrange("p a d -> p (a d)")),
                                   (yfull, yfull.ap.rearrange("p k t -> p (k t)").bitcast(F32)))):
        g_ = _Stg()
        g_.t, g_.v, g_.schan = t_, v_, c_stage[i_]
        stg.append(g_)
    stg_rot = Rot(stg)

    for dst, src in ((ident, ident_d), (maskT, mask_d), (vecs, vecs_d), (dec, dec_d), (fgain, fgain_d)):
        fw.dma(SP, fw.chan("const_" + dst.name), dst[:], src, writes=[dst])

    fw.op(DVE, lambda e: e.tensor_scalar_mul(out=hv[:, 0:12], in0=vecs[:, V_CB:V_CB + 12], scalar1=0.5), reads=[vecs], writes=[hv])
    fw.op(ACT, lambda e: e.activation(out=hv[:, 20:24], in_=vecs[:, V_LAM:V_LAM + 4], func=AF.Exp, scale=-1.0), reads=[vecs, hv], writes=[hv])
    fw.op(ACT, lambda e: e.activation(out=hv[:, 20:24], in_=hv[:, 20:24], func=AF.Ln, bias=1.0), reads=[hv], writes=[hv])
    fw.op(DVE, lambda e: e.tensor_scalar_mul(out=hv[:, 12:16], in0=hv[:, 20:24], scalar1=-4.0), reads=[hv], writes=[hv])
    fw.op(DVE, lambda e: e.tensor_scalar_mul(out=hv[:, 16:20], in0=hv[:, 20:24], scalar1=-8.0), reads=[hv], writes=[hv])
    for h in range(4):
        for k in range(4):
            fw.op(DVE, lambda e: e.tensor_scalar_mul(out=diag[:, h, k, :], in0=ident[:], scalar1=vecs[:, V_CW + h * 4 + k:V_CW + h * 4 + k + 1]),
                  reads=[ident, vecs], writes=[diag])
    fw.op(POOL, lambda e: e.memset(negh[:], -0.5), writes=[negh])
    fw.op(POOL, lambda e: e.memset(posh[:], 0.5), writes=[posh])
    fw.op(DVE, lambda e: e.memset(R[:], 0.0), writes=R_h)
    fw.op(DVE, lambda e: e.memset(Rbf[:], 0.0), writes=[Rbf])
    fw.op(DVE, lambda e: e.memset(lx[0][:], 0.0), writes=[lx[0]])
    fw.op(DVE, lambda e: e.memset(lx[1][:], 0.0), writes=[lx[1]])
    for h in range(4):
        fw.op(DVE, lambda e: e.memset(hst[h][:], 0.0), writes=[hst[h]])

    cast_rr = [0]

    def cast(out_ap, in_ap, scalar_ap, reads, writes):
        k = cast_rr[0] % 2
        cast_rr[0] += 1
        if k == 0:
            if scalar_ap is None:
                fw.op(ACT, lambda e: e.activation(out=out_ap, in_=in_ap, func=AF.Copy), reads=reads, writes=writes)
            else:
                fw.op(ACT, lambda e: e.activation(out=out_ap, in_=in_ap, func=AF.Copy, scale=scalar_ap), reads=reads, writes=writes)
        else:
            eng = DVE if k == 1 else POOL
            if scalar_ap is None:
                fw.op(eng, lambda e: e.tensor_copy(out=out_ap, in_=in_ap), reads=reads, writes=writes)
            else:
                fw.op(eng, lambda e: e.tensor_scalar_mul(out=out_ap, in0=in_ap, scalar1=scalar_ap), reads=reads, writes=writes)

    win_v = win_d.rearrange("(dc p) c -> p dc c", p=128)
    for cb in (2, 3, 4, 0, 1):
        for half in range(2):
            sg_ = stg_rot.next()
            st, stv = sg_.t, sg_.v
            fw.dma(SP, sg_.schan, stv.rearrange("p (a c) -> p a c", a=4), win_v[:, 4 * half:4 * half + 4, cb * 512:(cb + 1) * 512], writes=[st])
            for dl in range(4):
                dc = 4 * half + dl
                cast(w_in_bf[:, dc, cb * 512:(cb + 1) * 512], stv[:, dl * 512:(dl + 1) * 512], vecs[:, V_NG + dc:V_NG + dc + 1], [st, vecs], [w_in_cb[cb]])
    sg_ = stg_rot.next()
    st, stv = sg_.t, sg_.v
    fw.dma(SP, sg_.schan, stv[:, 0:1024].rearrange("p (a h j) -> p a h j", a=2, h=4), wg_d, writes=[st])
    fw.op(DVE, lambda e: e.tensor_copy(out=wg_bf[:].rearrange("p a h j -> p (a h j)"), in_=stv[:, 0:1024]), reads=[st], writes=[wg_bf])
    wout_v = wout_d.rearrange("(kc p) n -> p kc n", p=128)

    def load_w_out(g):
        sg_ = stg_rot.next()
        st, stv = sg_.t, sg_.v
        fw.dma(SP, sg_.schan, stv.rearrange("p (l n) -> p l n", l=2), wout_v[:, 2 * g:2 * g + 2, :], writes=[st])
        for l in range(2):
            kc = 2 * g + l
            if kc % 8 >= 4:
                hglob = (kc // 8) * 4 + (kc % 8 - 4)
                sc_ap = vecs[:, V_RG + hglob:V_RG + hglob + 1]
            else:
                sc_ap = None
            for nh in range(2):
                cast(w_out_bf[:, kc, nh * 512:(nh + 1) * 512], stv[:, l * 1024 + nh * 512:l * 1024 + (nh + 1) * 512], sc_ap, [st, vecs], [w_out_bf])

    pid = nc.partition_id([mybir.EngineType.Pool])
    par = pid % 2

    def stage_A(s):
        xnT = xnT_rot[s % 2]
        fw.cur_prio = 2
        chunks = TILE_CHUNKS[s]
        nch = len(chunks)
        ssq = ssq_rot.next()
        rstd = rstd_rot.next()
        xs_l = []
        cst = cs_rot.next()
        cs_of_tile[s] = cst
        fw.dma(SP, c_cs[s % 2], cst[:, 0:nch, :], cs_d[:, chunks[0]:chunks[0] + nch, :], writes=[cst])
        for j, n in enumerate(chunks):
            xs = xs_rot.next()
            xs_l.append(xs)
            if n == 0:
                fw.op(POOL, lambda e: e.memset(xs[:], 0.0), writes=[xs])
                fw.dma(SP, xs.chan, xs[128 - N_META:128, :], meta_d, writes=[xs])
            else:
                fw.dma(SP, xs.chan, xs[:], x_d[(n - 1) * 128:n * 128, :], writes=[xs])
            xnj = xn_rot.tiles[(xn_rot.i + (j % 2 if nch > 1 else 0)) % 2]
            fw.op(ACT, lambda e: e.activation(out=xnj[:], in_=xs[:], func=AF.Square, accum_out=ssq[:, j:j + 1]), reads=[xs], writes=[xnj, ssq])
            if (nch > 1 and j % 2 == 1) or nch == 1:
                j0 = j - 1 if nch > 1 else 0
                fw.op(POOL, lambda e: e.tensor_scalar(out=rstd[:, j0:j + 1], in0=ssq[:, j0:j + 1], scalar1=1.0 / D_MODEL, scalar2=EPS, op0=ALU.mult, op1=ALU.add), reads=[ssq], writes=[rstd])
                fw.op(POOL, lambda e: e.tensor_tensor(out=rstd[:, j0:j + 1], in0=rstd[:, j0:j + 1], in1=negh[:, j0:j + 1], op=ALU.pow), reads=[rstd, negh], writes=[rstd])
                for jj in range(j0, j + 1):
                    xs2 = xs_l[jj]
                    xn = xn_rot.next()
                    fw.op(DVE, lambda e: e.tensor_scalar_mul(out=xn[:], in0=xs2[:], scalar1=rstd[:, jj:jj + 1]), reads=[xs2, rstd], writes=[xn])
                    ptr = p_tr.next()
                    pv = bfv(ptr)
                    for dc in range(8):
                        fw.op(PE, lambda e: e.transpose(out=pv[:, dc * 128:(dc + 1) * 128], in_=xn[:, dc * 128:(dc + 1) * 128], identity=ident[:]),
                              reads=[xn, ident], writes=[ptr], inc=(dc == 7))
                    fw.op(DVE, lambda e: e.tensor_copy(out=xnT[:, :, jj * 128:(jj + 1) * 128], in_=pv.rearrange("p (a t) -> p a t", a=8)),
                          reads=[ptr], writes=[xnT])

    def stage_B1(s):
        sg = sg_all[s % 2]
        xnT = xnT_rot[s % 2]
        fw.cur_prio = 0
        NT = 128 * len(TILE_CHUNKS[s])
        lxc = lx[s % 2]
        for ec in (4, 0, 5, 1, 6, 2, 7, 3):
            if ec >= 4 and s == 0:
                continue
            pb = p_in.next()
            for dc in range(8):
                fw.op(PE, lambda e: e.matmul(pb[:, 0:NT], lhsT=w_in_bf[:, dc, ec * 128:(ec + 1) * 128], rhs=xnT[:, dc, 0:NT], start=(dc == 0), stop=(dc == 7)),
                      reads=[w_in_cb[ec // 4], xnT], writes=[pb], inc=(dc == 7))
            if ec < 4:
                h = ec
                fw.op(DVE, lambda e: e.tensor_copy(out=lxc[:, h, 3:3 + NT], in_=pb[:, 0:NT]), reads=[pb], writes=[lxc])
            else:
                h = ec - 4
                fw.op(ACT, lambda e: e.activation(out=thg[:, 0:NT], in_=pb[:, 0:NT], func=AF.Tanh, scale=0.5), reads=[pb], writes=[thg])
                fw.op(ACT, lambda e: e.activation(out=ghalf[:, 0:NT], in_=pb[:, 0:NT], func=AF.Copy, scale=0.5), reads=[pb], writes=[ghalf])
                fw.op(DVE, lambda e: e.scalar_tensor_tensor(out=sg[h][:, 0:NT], in0=thg[:, 0:NT], scalar=1.0, in1=ghalf[:, 0:NT], op0=ALU.add, op1=ALU.mult),
                      reads=[thg, ghalf], writes=[sg[h]])

    def stage_halo(s):
        NT = 128 * len(TILE_CHUNKS[s])
        lxc = lx[s % 2]
        nxt = lx[(s + 1) % 2]
        fw.op(POOL, lambda e: e.tensor_copy(out=nxt[:, :, 0:3], in_=lxc[:, :, NT:NT + 3]), reads=[lxc], writes=[nxt])

    def stage_C(s, h):
        sg = sg_all[s % 2]
        fw.cur_prio = 0
        NT = 128 * len(TILE_CHUNKS[s])
        lxc = lx[s % 2]
        pc = p_lru.next()
        for k in range(4):
            fw.op(PE, lambda e: e.matmul(pc[:, 0:NT], lhsT=diag[:, h, k, :], rhs=lxc[:, h, k:k + NT], start=(k == 0), stop=(k == 3)),
                  reads=[diag, lxc], writes=[pc], inc=(k == 3))
        xcb_, xch_, thr_, thi_, a_, a2_ = xcb.next(), xch.next(), thr.next(), thi.next(), a_t.next(), a2_t.next()
        fw.op(ACT, lambda e: e.activation(out=xcb_[:, 0:NT], in_=pc[:, 0:NT], func=AF.Identity, bias=vecs[:, V_CB + h:V_CB + h + 1], scale=1.0),
              reads=[pc, vecs], writes=[xcb_])
        fw.op(ACT, lambda e: e.activation(out=xch_[:, 0:NT], in_=pc[:, 0:NT], func=AF.Identity, bias=hv[:, h:h + 1], scale=0.5),
              reads=[pc, hv], writes=[xch_])
        pr = p_lru.next()
        fw.op(PE, lambda e: e.matmul(pr[:, 0:NT], lhsT=wg_bf[:, 0, h, :], rhs=xcb_[:, 0:NT], start=True, stop=True), reads=[wg_bf, xcb_], writes=[pr])
        pi = p_lru.next()
        fw.op(PE, lambda e: e.matmul(pi[:, 0:NT], lhsT=wg_bf[:, 1, h, :], rhs=xcb_[:, 0:NT], start=True, stop=True), reads=[wg_bf, xcb_], writes=[pi])
        fw.op(ACT, lambda e: e.activation(out=thr_[:, 0:NT], in_=pr[:, 0:NT], func=AF.Tanh, bias=hv[:, 4 + h:5 + h], scale=0.5), reads=[pr, hv], writes=[thr_])
        fw.op(ACT, lambda e: e.activation(out=thi_[:, 0:NT], in_=pi[:, 0:NT], func=AF.Tanh, bias=hv[:, 8 + h:9 + h], scale=0.5), reads=[pi, hv], writes=[thi_])
        fw.op(ACT, lambda e: e.activation(out=a_[:, 0:NT], in_=thr_[:, 0:NT], func=AF.Exp, bias=hv[:, 12 + h:13 + h], scale=hv[:, 12 + h:13 + h]), reads=[thr_, hv], writes=[a_])
        fw.op(ACT, lambda e: e.activation(out=a2_[:, 0:NT], in_=thr_[:, 0:NT], func=AF.Exp, bias=hv[:, 16 + h:17 + h], scale=hv[:, 16 + h:17 + h]), reads=[thr_, hv], writes=[a2_])
        fw.op(ACT, lambda e: e.activation(out=a2_[:, 0:NT], in_=a2_[:, 0:NT], func=AF.Relu, bias=1.0, scale=-1.0), reads=[a2_], writes=[a2_])
        fw.op(ACT, lambda e: e.activation(out=a2_[:, 0:NT], in_=a2_[:, 0:NT], func=AF.Sqrt), reads=[a2_], writes=[a2_])
        fw.op(DVE, lambda e: e.scalar_tensor_tensor(out=thi_[:, 0:NT], in0=thi_[:, 0:NT], scalar=1.0, in1=xch_[:, 0:NT], op0=ALU.add, op1=ALU.mult),
              reads=[thi_, xch_], writes=[thi_])
        fw.op(DVE, lambda e: e.tensor_tensor(out=a2_[:, 0:NT], in0=a2_[:, 0:NT], in1=thi_[:, 0:NT], op=ALU.mult), reads=[a2_, thi_], writes=[a2_])
        c0 = 128 - N_META if s == 0 else 0
        fw.op(DVE, lambda e: e.tensor_tensor_scan(out=xch_[:, c0:NT], data0=a_[:, c0:NT], data1=a2_[:, c0:NT], initial=hst[h][:, 0:1], op0=ALU.mult, op1=ALU.add),
              reads=[a_, a2_, hst[h], xch_], writes=[xch_])
        fw.op(POOL, lambda e: e.tensor_copy(out=hst[h][:, 0:1], in_=xch_[:, NT - 1:NT]), reads=[xch_], writes=[hst[h]])
        if s > 0:
            fw.op(DVE, lambda e: e.tensor_tensor(out=yT[h][:, 0:NT], in0=sg[h][:, 0:NT], in1=xch_[:, 0:NT], op=ALU.mult), reads=[sg[h], xch_], writes=[yT[h]])

    def stage_D(s, j):
        xnT = xnT_rot[s % 2]
        fw.cur_prio = 0
        n = TILE_CHUNKS[s][j]
        tsl = slice(j * 128, (j + 1) * 128)

        def proj(c0):
            pb = p_in.next()
            for dc in range(8):
                fw.op(PE, lambda e: e.matmul(pb[:, :], lhsT=xnT[:, dc, tsl], rhs=w_in_bf[:, dc, c0:c0 + 512], start=(dc == 0), stop=(dc == 7)),
                      reads=[xnT, w_in_cb[c0 // 512]], writes=[pb], inc=(dc == 7))
            return pb

        p_qk = proj(1024)
        qks = qk_sb.next()
        fw.op(ACT, lambda e: e.activation(out=qks[:], in_=p_qk[:, :], func=AF.Copy), reads=[p_qk], writes=[qks])
        p_v = proj(1536)
        vb = v_bf.next()
        fw.op(DVE, lambda e: e.tensor_copy(out=vb[:], in_=p_v[:, :]), reads=[p_v], writes=[vb])
        if s > 0:
            p_g = proj(2048)
            fw.op(ACT, lambda e: e.activation(out=thrg[:], in_=p_g[:, :], func=AF.Tanh, scale=0.5), reads=[p_g], writes=[thrg])
            fw.op(ACT, lambda e: e.activation(out=rghalf[:], in_=p_g[:, :], func=AF.Copy, scale=0.5), reads=[p_g], writes=[rghalf])
            sgr_ = sgr_rot.next()
            fw.op(DVE, lambda e: e.scalar_tensor_tensor(out=sgr_[:], in0=thrg[:], scalar=1.0, in1=rghalf[:], op0=ALU.add, op1=ALU.mult),
                  reads=[thrg, rghalf], writes=[sgr_])
        qv = qks.ap.rearrange("p (g t f) -> p g t f", g=8, t=2)
        rv = qk_rot.ap.rearrange("p (g t f) -> p g t f", g=8, t=2)
        t1, t2 = qv[:, :, 0, :], qv[:, :, 1, :]
        cs = cs_of_tile[s]
        cosb = cs[:, j, 0:32].unsqueeze(1).to_broadcast([128, 8, 32])
        sinb = cs[:, j, 32:64].unsqueeze(1).to_broadcast([128, 8, 32])
        tA = tmpA.ap.rearrange("p (g f) -> p g f", g=8)
        tB = tmpB.ap.rearrange("p (g f) -> p g f", g=8)
        fw.op(POOL, lambda e: e.tensor_tensor(out=tA, in0=t1, in1=cosb, op=ALU.mult), reads=[qks, cs], writes=[tmpA])
        fw.op(POOL, lambda e: e.tensor_tensor(out=tB, in0=t2, in1=sinb, op=ALU.mult), reads=[qks, cs], writes=[tmpB])
        fw.op(POOL, lambda e: e.tensor_tensor(out=rv[:, :, 0, :], in0=tA, in1=tB, op=ALU.subtract), reads=[tmpA, tmpB], writes=[qk_rot])
        fw.op(POOL, lambda e: e.tensor_tensor(out=tA, in0=t1, in1=sinb, op=ALU.mult), reads=[qks, cs], writes=[tmpA])
        fw.op(POOL, lambda e: e.tensor_tensor(out=tB, in0=t2, in1=cosb, op=ALU.mult), reads=[qks, cs], writes=[tmpB])
        fw.op(POOL, lambda e: e.tensor_tensor(out=rv[:, :, 1, :], in0=tA, in1=tB, op=ALU.add), reads=[tmpA, tmpB], writes=[qk_rot])
        qd_, kd_, kdec_ = qd.next(), kd.next(), kdec.next()
        qr = qk_rot.ap[:, 0:256].rearrange("p (h f) -> p h f", h=4)
        kr = qk_rot.ap[:, 256:512].rearrange("p (h f) -> p h f", h=4)

        def dcol(c):
            return dec[:, c:c + 4].unsqueeze(2).to_broadcast([128, 4, 64])

        fw.op(POOL, lambda e: e.tensor_tensor(out=qd_[:], in0=qr, in1=dcol(DC_Q), op=ALU.mult), reads=[qk_rot, dec], writes=[qd_])
        fw.op(POOL, lambda e: e.tensor_tensor(out=kd_[:], in0=kr, in1=dcol(DC_KI), op=ALU.mult), reads=[qk_rot, dec], writes=[kd_])
        fw.op(POOL, lambda e: e.tensor_tensor(out=kdec_[:], in0=kr, in1=dcol(DC_KD), op=ALU.mult), reads=[qk_rot, dec], writes=[kdec_])
        ptr = p_tr.next()
        pv = bfv(ptr)[0:64, :].rearrange("p (a t) -> p a t", a=8)
        for h in range(4):
            fw.op(PE, lambda e: e.transpose(out=pv[:, h, :], in_=qd_[:, h, :], identity=ident[:]), reads=[qd_, ident], writes=[ptr], inc=False)
        for h in range(4):
            fw.op(PE, lambda e: e.transpose(out=pv[:, 4 + h, :], in_=kd_[:, h, :], identity=ident[:]), reads=[kd_, ident], writes=[ptr], inc=(h == 3))
        qkT_ = qkT.next()
        fw.op(DVE, lambda e: e.tensor_copy(out=qkT_[:], in_=pv), reads=[ptr], writes=[qkT_])
        ps = p_ret.next()
        psv = ps.ap.rearrange("p (h c) -> p h c", h=4)
        for h in range(4):
            fw.op(PE, lambda e: e.matmul(psv[:, h, :], lhsT=qkT_[:, 4 + h, :], rhs=qkT_[:, h, :], start=True, stop=True), reads=[qkT_], writes=[ps], inc=(h == 3))
        sm_ = sm.next()
        fw.op(DVE, lambda e: e.tensor_tensor(out=sm_[:], in0=psv, in1=maskT[:].unsqueeze(1).to_broadcast([128, 4, 128]), op=ALU.mult), reads=[ps, maskT], writes=[sm_])
        po = p_ret.next()
        pov = po.ap.rearrange("p (h c) -> p h c", h=4)
        for h in range(4):
            fw.op(PE, lambda e: e.matmul(pov[:, h, :], lhsT=qkT_[:, h, :], rhs=Rbf[:, h, :], start=True, stop=False), reads=[qkT_, Rbf], writes=[po], inc=False)
            fw.op(PE, lambda e: e.matmul(pov[:, h, :], lhsT=sm_[:, h, :], rhs=vb[:, h * 128:(h + 1) * 128], start=False, stop=True), reads=[sm_, vb], writes=[po], inc=(h == 3))
        pk = p_ret.next()
        pkv = pk.ap[0:64, :].rearrange("p (h c) -> p h c", h=4)
        for h in range(4):
            fw.op(PE, lambda e: e.matmul(pkv[:, h, :], lhsT=kdec_[:, h, :], rhs=vb[:, h * 128:(h + 1) * 128], start=True, stop=True), reads=[kdec_, vb], writes=[pk], inc=(h == 3))
        if s > 0:
            o_ = o_sb_rot.next()
            mv_ = mv_rot.next()
            rs_ = rs_rot.next()
            fw.op(DVE, lambda e: e.tensor_copy(out=o_[:], in_=pov), reads=[po], writes=[o_])
            for h in range(4):
                fw.op(DVE, lambda e: e.bn_stats(out=bst[:, h, :], in_=o_[:, h, :]), reads=[o_], writes=[bst_h[h]])
            for h in range(4):
                fw.op(DVE, lambda e: e.bn_aggr(out=mv_[:, h, :], in_=bst[:, h, :]), reads=[bst_h[h]], writes=[mv_.parts[h]])
        for h in range(4):
            fw.op(DVE, lambda e: e.scalar_tensor_tensor(out=R[:, h, :], in0=R[:, h, :], scalar=dec[0:64, DC_G128 + h:DC_G128 + h + 1], in1=pkv[:, h, :], op0=ALU.mult, op1=ALU.add),
                  reads=[R_h[h], dec, pk], writes=[R_h[h]])
        fw.op(DVE, lambda e: e.tensor_copy(out=Rbf[:], in_=R[:]), reads=R_h, writes=[Rbf])
        if s > 0:
            fw.op(POOL, lambda e: e.tensor_scalar(out=rs_[:], in0=mv_[:, :, 1], scalar1=1.0, scalar2=EPS, op0=ALU.mult, op1=ALU.add), reads=mv_.parts, writes=[rs_])
            fw.op(POOL, lambda e: e.tensor_tensor(out=rs_[:], in0=rs_[:], in1=negh[:, 0:4], op=ALU.pow), reads=[rs_, negh], writes=[rs_])
            mean_b = mv_[:, :, 0:1].to_broadcast([128, 4, 128])
            rs_b = rs_[:].unsqueeze(2).to_broadcast([128, 4, 128])
            fw.op(POOL, lambda e: e.tensor_tensor(out=o_[:], in0=o_[:], in1=mean_b, op=ALU.subtract), reads=[o_] + mv_.parts, writes=[o_])
            fw.op(POOL, lambda e: e.tensor_tensor(out=on[:], in0=o_[:], in1=rs_b, op=ALU.mult), reads=[o_, rs_], writes=[on])
            yr_ = yr.next()
            fw.op(DVE, lambda e: e.tensor_tensor(out=yr_[:].rearrange("p h c -> p (h c)"), in0=on[:].rearrange("p h c -> p (h c)"), in1=sgr_[:], op=ALU.mult),
                  reads=[on, sgr_], writes=[yr_])
            ptr2 = p_tr.next()
            pv2 = bfv(ptr2)[:, 0:512].rearrange("p (h t) -> p h t", h=4)
            for h in range(4):
                fw.op(PE, lambda e: e.transpose(out=pv2[:, h, :], in_=yr_[:, h, :], identity=ident[:]), reads=[yr_, ident], writes=[ptr2], inc=(h == 3))
            fw.op(DVE, lambda e: e.tensor_copy(out=yT_all[:, 4:8, j * 128:(j + 1) * 128], in_=pv2), reads=[ptr2], writes=yT[4:8])

    def stage_E1(s):
        fw.cur_prio = 0
        fw.dma(SP, c_ysrc, ysrc_d[s].ap().rearrange("(k p) t -> p k t", p=128), yT_all[:, :, 0:TILE_NT[s]], reads=yT, writes=[ysrc_t[s]])
        fw.custom(POOL, c_cc, lambda e: e.collective_compute("AllGather", ALU.bypass, replica_groups=[[0, 1], [2, 3], [4, 5], [6, 7]],
                                                              ins=[ysrc_d[s].ap().opt()], outs=[ydst_d[s].ap().opt()]),
                  reads=[ysrc_t[s]], writes=[ydst_t[s]])

    def stage_E2(s):
        NH = TILE_NT[s] // 2
        nj2 = NH // 128
        off = TILE_T0[s] // 2
        fw.dma(POOL, c_yfull, yfull[:, :, 0:NH], ydst_d[s].ap().rearrange("(k p) t -> p k t", p=128)[:, :, bass.ds(par * NH, NH)], reads=[ydst_t[s]], writes=[yfull])
        xr = xr_rot.next()
        fw.dma(SP, xr.chan, xr[:, 0:nj2, :], xres_d[off:off + NH, :].rearrange("(j p) d -> p j d", p=128), writes=[xr])
        for j2 in range(nj2):
            for nh in range(2):
                pb = p_out.next()
                for kc in range(16):
                    fw.op(PE, lambda e: e.matmul(pb[:, :], lhsT=yfull[:, kc, j2 * 128:(j2 + 1) * 128], rhs=w_out_bf[:, kc, nh * 512:(nh + 1) * 512], start=(kc == 0), stop=(kc == 15)),
                          reads=[yfull, w_out_bf], writes=[pb], inc=(kc == 15))
                fw.op(DVE, lambda e: e.tensor_tensor(out=xr[:, j2, nh * 512:(nh + 1) * 512], in0=xr[:, j2, nh * 512:(nh + 1) * 512], in1=pb[:, :], op=ALU.add),
                      reads=[xr, pb], writes=[xr])
                fw.op(ACT, lambda e: e.activation(out=pb[:, :], in_=xr[:, j2, nh * 512:(nh + 1) * 512], func=AF.Square, accum_out=fss4[:, j2 * 2 + nh:j2 * 2 + nh + 1]),
                      reads=[xr], writes=[pb, fss4])
        f4 = fss4.ap.rearrange("p (j n) -> p j n", n=2)
        fw.op(POOL, lambda e: e.tensor_tensor(out=frs[:, 0:nj2], in0=f4[:, 0:nj2, 0], in1=f4[:, 0:nj2, 1], op=ALU.add), reads=[fss4], writes=[frs])
        fw.op(POOL, lambda e: e.tensor_scalar(out=frs[:, 0:nj2], in0=frs[:, 0:nj2], scalar1=1.0 / D_MODEL, scalar2=EPS, op0=ALU.mult, op1=ALU.add), reads=[frs], writes=[frs])
        fw.op(POOL, lambda e: e.tensor_tensor(out=frs[:, 0:nj2], in0=frs[:, 0:nj2], in1=negh[:, 0:nj2], op=ALU.pow), reads=[frs, negh], writes=[frs])
        for j2 in range(nj2):
            fw.op(DVE, lambda e: e.scalar_tensor_tensor(out=xr[:, j2, :], in0=xr[:, j2, :], scalar=frs[:, j2:j2 + 1], in1=fgain[:], op0=ALU.mult, op1=ALU.mult),
                  reads=[xr, frs, fgain], writes=[xr])
        fw.dma(SP, c_out, out_d[off:off + NH, :].rearrange("(j p) d -> p j d", p=128), xr[:, 0:nj2, :], reads=[xr], writes=[out_t])

    marks = {}
    stage_A(0)
    for g in range(8):
        load_w_out(g)
    marks["A0"] = fw.ops[-1]
    for s in range(NTILES):
        stage_B1(s)
        if s + 1 < NTILES:
            stage_A(s + 1)
        nch = len(TILE_CHUNKS[s])
        hpc = 4 // nch
        for j in range(nch):
            stage_D(s, j)
            for h in range(j * hpc, (j + 1) * hpc):
                stage_C(s, h)
        stage_halo(s)
        if s > 0:
            stage_E1(s)
        if s > 0:
            stage_E2(s)
        marks[f"t{s}"] = fw.ops[-1]
    fw.finish(POOL, [out_t])
    fw.finish(SP, [out_t])
    best = None
    for trial in (range(N_SCHED_TRIALS) if FORCE_TRIAL is None else [FORCE_TRIAL]):
        if trial == 0:
            order = fw.schedule()
        else:
            order = fw.schedule(seed=trial, noise=0.02 * (1 + trial % 5), win=(0.1e-6, 0.3e-6, 0.6e-6)[trial % 3])
        if best is None or fw.sim_end < best[0]:
            best = (fw.sim_end, list(order), fw.n_switch, trial)
    fw.sim_end, order, fw.n_switch, best_trial = best
    fw.emit(order)
    stats = {e.name: (e.nins, e.nwaits) for e in (PE, ACT, DVE, POOL, SP)}
    stats["fw"] = fw
    stats["order"] = order
    stats["sim_end_us"] = fw.sim_end * 1e6
    stats["n_switch"] = fw.n_switch
    stats["best_trial"] = best_trial
    stats["sbuf_bytes"] = sb_bytes[0] + 8192
    stats["marks"] = {k: round(v.fin * 1e6, 1) for k, v in marks.items()}
    stats["sim_busy_us"] = {k: round(v * 1e6, 1) for k, v in fw.sim_busy.items()}
    return nc, stats


def _consts():
    half = 32
    inv = np.power(np.float64(10000.0), -(np.arange(half, dtype=np.float64) / np.float64(half)))
    c = np.arange(128)[:, None]
    n = np.arange(NCHUNK)[None, :]
    pos = np.maximum(n * 128 + c - (128 - N_META), 0).astype(np.float64)
    ang = pos[:, :, None] * inv[None, None, :]
    cs = np.concatenate([np.cos(ang), np.sin(ang)], axis=-1).astype(np.float32)
    ident = np.eye(128, dtype=np.float32).astype(ml_dtypes.bfloat16)
    m = np.arange(128)[:, None]
    cc = np.arange(128)[None, :]
    maskT = (cc >= m).astype(np.float32).astype(ml_dtypes.bfloat16)
    return cs, ident, maskT


def _dec_table(heads):
    log_g = np.log1p(-np.exp2(-5.0 - np.arange(8, dtype=np.float64)))
    c = np.arange(128, dtype=np.float64)[:, None]
    lg = log_g[heads][None, :]
    dec = np.zeros((128, NDEC), np.float32)
    dec[:, DC_Q:DC_Q + 4] = np.exp((c + 1.0) * lg)
    dec[:, DC_KI:DC_KI + 4] = np.exp(-(c + 1.0) * lg) * 0.125
    dec[:, DC_KD:DC_KD + 4] = np.exp((127.0 - c) * lg) * 0.125
    dec[:, DC_G128:DC_G128 + 4] = np.exp(128.0 * lg)
    return dec


_PROG = None


def kernel(x, meta_tokens, norm_gain, w_in, conv_w, conv_b, w_rg, b_rg, w_ig, b_ig,
           lru_lambda, ret_norm_gain, w_out, final_norm_gain):
    global _PROG
    f = lambda a: np.ascontiguousarray(np.asarray(a), dtype=np.float32)
    x, meta_tokens, norm_gain, w_in = f(x), f(meta_tokens), f(norm_gain), f(w_in)
    conv_w, conv_b, w_rg, b_rg, w_ig, b_ig = f(conv_w), f(conv_b), f(w_rg), f(b_rg), f(w_ig), f(b_ig)
    lru_lambda, ret_norm_gain, w_out, final_norm_gain = f(lru_lambda), f(ret_norm_gain), f(w_out), f(final_norm_gain)
    if _PROG is None:
        _PROG = build_program()
    nc = _PROG[0]
    cs, ident, maskT = _consts()
    w_in0 = w_in[0]
    perm = np.concatenate([np.arange(0, 512), np.arange(1024, 1536), np.arange(512, 1024), np.arange(1536, 2048)])
    w_out_p = np.ascontiguousarray(w_out[0][perm])
    fgain = np.ascontiguousarray(np.broadcast_to(final_norm_gain[None, :], (128, D_MODEL)))
    in_maps = []
    for c in range(8):
        b, p = c // 2, c % 2
        lch = np.arange(512 * p, 512 * p + 512)
        heads = np.arange(4 * p, 4 * p + 4)
        qk_cols = (heads[:, None] * 64 + np.arange(64)[None, :]).reshape(-1)
        v_cols = (heads[:, None] * 128 + np.arange(128)[None, :]).reshape(-1)
        cols = np.concatenate([lch, 1024 + lch, 2048 + qk_cols, 2560 + qk_cols, 3072 + v_cols, 4096 + v_cols])
        w_in_c = np.ascontiguousarray(w_in0[:, cols])
        wg = np.stack([w_rg[0, 4 * p:4 * p + 4], w_ig[0, 4 * p:4 * p + 4]], 0)
        wg = np.ascontiguousarray(wg.transpose(2, 0, 1, 3))
        vecs = np.zeros((128, NV), np.float32)
        for h in range(4):
            ch = 512 * p + h * 128 + np.arange(128)
            for k in range(4):
                vecs[:, V_CW + h * 4 + k] = conv_w[0, k, ch]
            vecs[:, V_CB + h] = conv_b[0, ch]
            vecs[:, V_BRG + h] = b_rg[0, ch]
            vecs[:, V_BIG + h] = b_ig[0, ch]
            vecs[:, V_LAM + h] = lru_lambda[0, ch]
        for dc in range(8):
            vecs[:, V_NG + dc] = norm_gain[0, dc * 128:(dc + 1) * 128]
        for h in range(8):
            vecs[:, V_RG + h] = ret_norm_gain[0, h * 128:(h + 1) * 128]
        xb = x[b]
        xres = np.ascontiguousarray(np.concatenate([xb[TILE_T0[t_] + p * (TILE_NT[t_] // 2):TILE_T0[t_] + (p + 1) * (TILE_NT[t_] // 2)] for t_ in range(1, NTILES)], 0))
        in_maps.append({
            "x": xb, "meta": meta_tokens, "xres": xres, "w_in": w_in_c, "w_out": w_out_p, "w_g": wg,
            "vecs": vecs, "fgain": fgain, "ident": ident, "cs": cs, "dec": _dec_table(heads), "maskT": maskT,
        })
    res = run_bass_kernel_spmd(nc, in_maps, core_ids=list(range(8)))
    out = np.zeros((BATCH, SEQ, D_MODEL), np.float32)
    for c in range(8):
        b, p = c // 2, c % 2
        o = np.asarray(res.results[c]["out"])
        for t_ in range(1, NTILES):
            nh_ = TILE_NT[t_] // 2
            out[b, TILE_T0[t_] + p * nh_:TILE_T0[t_] + (p + 1) * nh_] = o[TILE_T0[t_] // 2:TILE_T0[t_] // 2 + nh_]
    return out
```

```python
import numpy as np
import ml_dtypes
import concourse.bass as bass
import concourse.mybir as mybir
from concourse.bass_utils import run_bass_kernel_spmd

F32 = mybir.dt.float32
BF16 = mybir.dt.bfloat16
ALU = mybir.AluOpType
AF = mybir.ActivationFunctionType

D_MODEL = 1024
BATCH = 4
SEQ = 4096
N_META = 16
CH = 128
NCHUNK = 33
EPS = 1e-6
N_SCHED_TRIALS = 300
FORCE_TRIAL = 0
TILE_NCH = [1, 4, 4, 4, 4, 4, 4, 4, 2, 2]
TILE_CHUNKS = []
_c = 0
for _n in TILE_NCH:
    TILE_CHUNKS.append(list(range(_c, _c + _n)))
    _c += _n
assert _c == NCHUNK
NTILES = len(TILE_CHUNKS)
TILE_T0 = [None] + [(TILE_CHUNKS[s][0] - 1) * 128 for s in range(1, NTILES)]
TILE_NT = [128 * n for n in TILE_NCH]

V_CW = 0
V_CB = 16
V_BRG = 20
V_BIG = 24
V_LAM = 28
V_NG = 32
V_RG = 40
NV = 48
DC_Q = 0
DC_KI = 4
DC_KD = 8
DC_G128 = 12
NDEC = 16


class Src:
    def __init__(self, name, sem, step):
        self.name, self.sem, self.step = name, sem, step
        self.count = 0
        self.snap = {}


class Eng(Src):
    def __init__(self, name, sem, handle):
        super().__init__(name, sem, 1)
        self.h = handle
        self.know = {}
        self.nwaits = 0
        self.nins = 0


class T:
    def __init__(self, ap, name=""):
        self.ap = ap
        self.name = name
        self.w = {}
        self.r = {}
        self.lw = None
        self.lr = []

    def __getitem__(self, k):
        return self.ap[k]


class _Proxy:
    def __init__(self):
        self.calls = []

    def __getattr__(self, name):
        def f(*a, **k):
            self.calls.append((name, a, k))
            return None
        return f


def _free_size(ap):
    n = 1
    for d in ap.shape[1:]:
        n *= int(d)
    return n


class Op:
    __slots__ = ("prio", "tbl", "eng", "calls", "reads", "writes", "kind", "ch", "dur", "lat", "idx", "deps", "nsucc", "succ", "ndeps", "start", "fin", "closed", "bytes")


class FW:
    DMA_BW = 150e9
    TBL_SWITCH = 2.7e-6

    def __init__(self, nc):
        self.nc = nc
        self.pe = self._eng("pe", nc.tensor)
        self.act = self._eng("act", nc.scalar)
        self.dve = self._eng("dve", nc.vector)
        self.pool = self._eng("pool", nc.gpsimd)
        self.sp = self._eng("sp", nc.sync)
        self.ops = []
        self.open = None
        self.cur_prio = 0

    def _eng(self, name, handle):
        return Eng(name, self.nc.alloc_semaphore("s_" + name), handle)

    def chan(self, name, step=16):
        return Src(name, self.nc.alloc_semaphore("c_" + name), step)

    def _est(self, eng, name, a, k):
        out = k.get("out", a[0] if a else None)
        n = _free_size(out) if out is not None and hasattr(out, "shape") else 64
        if eng is self.pe:
            return max(64, n) / 2.0e9 + 0.02e-6
        if eng is self.act:
            return 0.30e-6 + n * 0.84e-9
        if eng is self.dve:
            return 0.17e-6 + n * 1.05e-9
        return 0.32e-6 + n * 1.5e-9

    def _new(self, eng, kind):
        o = Op()
        o.eng, o.kind, o.calls, o.reads, o.writes = eng, kind, [], [], []
        o.ch, o.dur, o.lat, o.idx, o.closed, o.bytes = None, 0.0, 0.0, len(self.ops), True, 0
        o.tbl = None
        o.prio = self.cur_prio
        self.ops.append(o)
        return o

    def op(self, eng, fn, reads=(), writes=(), inc=True):
        p = _Proxy()
        fn(p)
        if self.open is not None and self.open.eng is eng:
            o = self.open
        else:
            assert self.open is None, "unterminated inc=False group"
            o = self._new(eng, "op")
        for (name, a, k) in p.calls:
            o.calls.append((name, a, k))
            o.dur += self._est(eng, name, a, k)
            if name == "activation":
                f_ = k.get("func")
                if f_ == AF.Sqrt:
                    o.tbl = "sqrt"
                elif f_ in (AF.Exp, AF.Tanh):
                    o.tbl = "exp"
                elif f_ == AF.Ln:
                    o.tbl = "ln"
        for t in reads:
            if t not in o.reads:
                o.reads.append(t)
        for t in writes:
            if t not in o.writes:
                o.writes.append(t)
        self.open = None if inc else o

    def dma(self, q, ch, out, in_, reads=(), writes=(), **kw):
        assert self.open is None
        o = self._new(q, "dma")
        o.calls.append(("dma_start", (), dict(out=out, in_=in_, **kw)))
        o.ch = ch
        o.reads, o.writes = list(reads), list(writes)
        nbytes = 1
        for d in out.shape:
            nbytes *= int(d)
        o.bytes = nbytes * (2 if out.dtype == BF16 else 4)
        o.dur = 0.08e-6 if q is self.sp else 1.0e-6
        o.lat = 2.0e-6

    def custom(self, q, ch, fn, reads=(), writes=(), lat=25e-6):
        assert self.open is None
        p = _Proxy()
        fn(p)
        o = self._new(q, "custom")
        o.calls = list(p.calls)
        o.ch = ch
        o.reads, o.writes = list(reads), list(writes)
        o.dur = 1.0e-6
        o.lat = lat

    def finish(self, q, tiles):
        assert self.open is None
        o = self._new(q, "finish")
        o.reads = list(tiles)
        o.dur = 0.05e-6

    def _all_tiles(self):
        seen = {}
        for o in self.ops:
            for t in o.reads:
                seen[id(t)] = t
            for t in o.writes:
                seen[id(t)] = t
        return seen.values()

    def schedule(self, seed=None, noise=0.0, win=0.3e-6):
        import random
        rng = random.Random(seed)
        ops = self.ops
        for t_ in self._all_tiles():
            t_.lw = None
            t_.lr = []
        for o in ops:
            deps = set()
            for t in o.reads:
                if t.lw is not None:
                    deps.add(t.lw)
            for t in o.writes:
                if t.lw is not None:
                    deps.add(t.lw)
                for r in t.lr:
                    deps.add(r)
            deps.discard(o)
            o.deps = deps
            for t in o.reads:
                t.lr.append(o)
            for t in o.writes:
                t.lw = o
                t.lr = []
        for o in ops:
            o.succ = []
            o.ndeps = len(o.deps)
            o.start = None
        for o in ops:
            for d in o.deps:
                d.succ.append(o)
        engs = [self.pe, self.act, self.dve, self.pool, self.sp]
        bl = {}
        for o in reversed(ops):
            m = 0.0
            for s_ in o.succ:
                if bl[s_] > m:
                    m = bl[s_]
            extra = o.lat + (o.bytes / self.DMA_BW if o.kind == "dma" else 0.0)
            bl[o] = m + o.dur + extra
        if noise > 0.0:
            blp = {o: v * (1.0 + noise * (rng.random() - 0.5)) for o, v in bl.items()}
        else:
            blp = bl
        free = {e: 0.0 for e in engs}
        dma_free = 0.0
        ready = {e: [] for e in engs}
        for o in ops:
            if o.ndeps == 0:
                ready[o.eng].append(o)
        order = []
        n = len(ops)
        WIN = win
        cur_tbl = ["exp"]
        self.n_switch = 0
        while len(order) < n:
            best = None
            for e in engs:
                fe = free[e]
                cands = []
                mn = None
                for o in ready[e]:
                    st = fe
                    if o.tbl is not None and e is self.act and o.tbl != cur_tbl[0]:
                        st = fe + self.TBL_SWITCH
                    for d in o.deps:
                        if d.fin > st:
                            st = d.fin
                    cands.append((st, o))
                    if mn is None or st < mn:
                        mn = st
                if mn is None:
                    continue
                pick = None
                for st, o in cands:
                    if st <= mn + WIN:
                        if pick is None or (o.prio, blp[o]) > (pick[1].prio, blp[pick[1]]):
                            pick = (st, o)
                key = (pick[0], -blp[pick[1]])
                if best is None or key < best[0]:
                    best = (key, pick[1], pick[0])
            _, o, st = best
            ready[o.eng].remove(o)
            o.start = st
            free[o.eng] = st + o.dur
            if o.tbl is not None and o.eng is self.act and o.tbl != cur_tbl[0]:
                cur_tbl[0] = o.tbl
                self.n_switch += 1
            if o.kind == "dma":
                t0 = max(st + o.dur, dma_free)
                dma_free = t0 + o.bytes / self.DMA_BW
                o.fin = dma_free + o.lat
            elif o.kind == "custom":
                o.fin = st + o.dur + o.lat
            else:
                o.fin = st + o.dur
            order.append(o)
            for s_ in o.succ:
                s_.ndeps -= 1
                if s_.ndeps == 0:
                    ready[s_.eng].append(s_)
        self.sim_end = max(o.fin for o in ops)
        self.sim_busy = {e.name: sum(o.dur for o in ops if o.eng is e) for e in engs}
        return order

    def _merge(self, eng, src, idx):
        if eng.know.get(src, 0) < idx:
            eng.know[src] = idx
        for s2, v in src.snap.get(idx, {}).items():
            if eng.know.get(s2, 0) < v:
                eng.know[s2] = v

    def _waits(self, eng, reads, writes):
        needs = {}
        for t in reads:
            for s, i in t.w.items():
                if needs.get(s, 0) < i:
                    needs[s] = i
        for t in writes:
            for d in (t.w, t.r):
                for s, i in d.items():
                    if needs.get(s, 0) < i:
                        needs[s] = i
        for s, i in sorted(needs.items(), key=lambda kv: kv[0] is eng):
            if eng.know.get(s, 0) >= i:
                continue
            eng.h.wait_ge(s.sem, i * s.step)
            eng.nwaits += 1
            self._merge(eng, s, i)

    def _record(self, src, idx, reads, writes):
        for t in reads:
            if t.r.get(src, 0) < idx:
                t.r[src] = idx
        for t in writes:
            t.w = {src: idx}
            t.r = {}

    def emit(self, order):
        for o in order:
            eng = o.eng
            self._waits(eng, o.reads, o.writes)
            if o.kind == "finish":
                continue
            ins = None
            for (name, a, k) in o.calls:
                ins = getattr(eng.h, name)(*a, **k)
                eng.nins += 1
            if o.kind == "op":
                eng.count += 1
                ins.then_inc(eng.sem, 1)
                eng.snap[eng.count] = dict(eng.know)
                self._record(eng, eng.count, o.reads, o.writes)
            else:
                ch = o.ch
                ch.count += 1
                ins.then_inc(ch.sem, ch.step)
                ch.snap[ch.count] = dict(eng.know)
                self._record(ch, ch.count, o.reads, o.writes)


class Rot:
    def __init__(self, tiles):
        self.tiles = tiles
        self.i = 0

    def next(self):
        t = self.tiles[self.i % len(self.tiles)]
        self.i += 1
        return t


def build_program():
    nc = bass.Bass("TRN2", target_bir_lowering=False)
    fw = FW(nc)
    PE, ACT, DVE, POOL, SP = fw.pe, fw.act, fw.dve, fw.pool, fw.sp

    def din(name, shape, dt):
        return nc.dram_tensor(name, shape, dt, kind="ExternalInput").ap()

    x_d = din("x", [SEQ, D_MODEL], F32)
    meta_d = din("meta", [N_META, D_MODEL], F32)
    xres_d = din("xres", [SEQ // 2, D_MODEL], F32)
    win_d = din("w_in", [D_MODEL, 2560], F32)
    wout_d = din("w_out", [2048, D_MODEL], F32)
    wg_d = din("w_g", [128, 2, 4, 128], F32)
    vecs_d = din("vecs", [128, NV], F32)
    fgain_d = din("fgain", [128, D_MODEL], F32)
    ident_d = din("ident", [128, 128], BF16)
    cs_d = din("cs", [128, NCHUNK, 64], F32)
    dec_d = din("dec", [128, NDEC], F32)
    mask_d = din("maskT", [128, 128], BF16)
    out_d = nc.dram_tensor("out", [SEQ // 2, D_MODEL], F32, kind="ExternalOutput").ap()
    ysrc_d = [None] + [nc.dram_tensor(f"ysrc{s}", [1024, TILE_NT[s]], BF16) for s in range(1, NTILES)]
    ydst_d = [None] + [nc.dram_tensor(f"ydst{s}", [2048, TILE_NT[s]], BF16) for s in range(1, NTILES)]
    ysrc_t = [None] + [T(ysrc_d[s].ap(), f"ysrc{s}") for s in range(1, NTILES)]
    ydst_t = [None] + [T(ydst_d[s].ap(), f"ydst{s}") for s in range(1, NTILES)]
    out_t = T(out_d, "out")

    sb_bytes = [0]

    def sb(name, shape, dt):
        n = 1
        for d in shape[1:]:
            n *= d
        sb_bytes[0] += n * (2 if dt == BF16 else 4)
        return T(nc.alloc_sbuf_tensor(name, shape, dt).ap(), name)

    w_in_bf = sb("w_in_bf", [128, 8, 2560], BF16)
    w_in_cb = [T(w_in_bf.ap[:, :, cb * 512:(cb + 1) * 512], f"w_in_cb{cb}") for cb in range(5)]
    w_out_bf = sb("w_out_bf", [128, 16, 1024], BF16)
    wg_bf = sb("wg_bf", [128, 2, 4, 128], BF16)
    diag = sb("diag", [128, 4, 4, 128], BF16)
    fgain = sb("fgain_sb", [128, D_MODEL], F32)
    cs_rot = Rot([sb(f"cs_sb{i}", [128, 4, 64], F32) for i in range(2)])
    cs_of_tile = {}
    ident = sb("ident_sb", [128, 128], BF16)
    maskT = sb("mask_sb", [128, 128], BF16)
    vecs = sb("vecs_sb", [128, NV], F32)
    dec = sb("dec_sb", [128, NDEC], F32)
    negh = sb("negh", [128, 16], F32)
    posh = sb("posh", [128, 1], F32)
    hv = sb("hv", [128, 24], F32)
    xs_rot = Rot([sb(f"xs{i}", [128, D_MODEL], F32) for i in range(2)])
    xn_rot = Rot([sb(f"xn{i}", [128, D_MODEL], BF16) for i in range(2)])
    xnT_rot = [sb(f"xnT{i}", [128, 8, 512], BF16) for i in range(2)]
    ssq_rot = Rot([sb(f"ssq{i}", [128, 4], F32) for i in range(2)])
    rstd_rot = Rot([sb(f"rstd{i}", [128, 4], F32) for i in range(2)])
    lx = [sb(f"lx{i}", [128, 4, 515], BF16) for i in range(2)]
    thg = sb("thg", [128, 512], BF16)
    ghalf = sb("ghalf", [128, 512], BF16)
    sg_all = [[sb(f"sg{i}_{h}", [128, 512], BF16) for h in range(4)] for i in range(2)]
    NSET = 2
    xcb = Rot([sb(f"xcb{i}", [128, 512], BF16) for i in range(NSET)])
    xch = Rot([sb(f"xch{i}", [128, 512], F32) for i in range(NSET)])
    thr = Rot([sb(f"thr{i}", [128, 512], F32) for i in range(NSET)])
    thi = Rot([sb(f"thi{i}", [128, 512], F32) for i in range(NSET)])
    a_t = Rot([sb(f"a{i}", [128, 512], F32) for i in range(NSET)])
    hst = [sb(f"hst{h}", [128, 1], F32) for h in range(4)]
    yT_all = nc.alloc_sbuf_tensor("yT_all", [128, 8, 512], BF16).ap()
    yT = [T(yT_all[:, k, :], f"yT{k}") for k in range(8)]
    qk_sb = Rot([sb(f"qk_sb{i}", [128, 512], F32) for i in range(1)])
    tmpA = sb("tmpA", [128, 256], F32)
    tmpB = sb("tmpB", [128, 256], F32)
    qk_rot = sb("qk_rot", [128, 512], F32)
    qd = Rot([sb(f"qd{i}", [128, 4, 64], BF16) for i in range(2)])
    kd = Rot([sb(f"kd{i}", [128, 4, 64], BF16) for i in range(2)])
    kdec = Rot([sb(f"kdec{i}", [128, 4, 64], BF16) for i in range(2)])
    v_bf = Rot([sb(f"v_bf{i}", [128, 512], BF16) for i in range(2)])
    thrg = sb("thrg", [128, 512], BF16)
    rghalf = sb("rghalf", [128, 512], BF16)
    sgr_rot = Rot([sb(f"sgr{j}", [128, 512], BF16) for j in range(2)])
    qkT = Rot([sb(f"qkT{i}", [64, 8, 128], BF16) for i in range(2)])
    sm = Rot([sb(f"sm{i}", [128, 4, 128], BF16) for i in range(2)])
    o_sb_rot = Rot([sb(f"o_sb{j}", [128, 4, 128], F32) for j in range(2)])
    bst = sb("bst", [128, 4, 6], F32)
    bst_h = [T(bst.ap[:, h, :], f"bst{h}") for h in range(4)]
    mv_rot = Rot([sb(f"mv{i}", [128, 4, 2], F32) for i in range(2)])
    for t_ in mv_rot.tiles:
        t_.parts = [T(t_.ap[:, h, :], t_.name + f"_{h}") for h in range(4)]
    rs_rot = Rot([sb(f"rs{i}", [128, 4], F32) for i in range(2)])
    on = sb("on", [128, 4, 128], BF16)
    yr = Rot([sb(f"yr{i}", [128, 4, 128], BF16) for i in range(2)])
    R = sb("R", [64, 4, 128], F32)
    R_h = [T(R.ap[:, h, :], f"R{h}") for h in range(4)]
    Rbf = sb("Rbf", [64, 4, 128], BF16)
    yfull = sb("yfull", [128, 16, 256], BF16)
    xr_rot = Rot([sb(f"xr{i}", [128, 2, D_MODEL], F32) for i in range(1)])
    fss4 = sb("fss4", [128, 4], F32)
    frs = sb("frs", [128, 2], F32)

    banks = [T(nc.alloc_psum_tensor(f"pb{i}", [128, 512], F32).ap(), f"pb{i}") for i in range(8)]
    p_in = Rot(banks[0:2])
    p_out = Rot(banks[2:3])
    p_tr = Rot(banks[3:4])
    p_lru = Rot(banks[4:6])
    p_ret = Rot(banks[6:8])

    def bfv(bank):
        return bank.ap.bitcast(BF16)

    c_const = fw.chan("const")
    c_stage = [fw.chan("stage0"), fw.chan("stage1")]
    c_x = [fw.chan("x0"), fw.chan("x1")]
    c_cs = [fw.chan("cs0"), fw.chan("cs1")]
    c_ysrc = fw.chan("ysrc")
    c_cc = fw.chan("cc", step=1)
    c_yfull = fw.chan("yfull")
    c_xr = [fw.chan("xr0"), fw.chan("xr1")]
    c_out = fw.chan("out")
    for t_, i_ in zip(xs_rot.tiles, range(2)):
        t_.chan = c_x[i_]
    for t_, i_ in zip(xr_rot.tiles, range(2)):
        t_.chan = c_xr[i_]
    class _Stg:
        pass
    stg = []
    for i_, (t_, v_) in enumerate(((xr_rot.tiles[0], xr_rot.tiles[0].ap.rearrange("p a d -> p (a d)")),
                                   (yfull, yfull.ap.rearrange("p k t -> p (k t)").bitcast(F32)))):
        g_ = _Stg()
        g_.t, g_.v, g_.schan = t_, v_, c_stage[i_]
        stg.append(g_)
    stg_rot = Rot(stg)

    for dst, src in ((ident, ident_d), (maskT, mask_d), (vecs, vecs_d), (dec, dec_d), (fgain, fgain_d)):
        fw.dma(SP, fw.chan("const_" + dst.name), dst[:], src, writes=[dst])

    fw.op(DVE, lambda e: e.tensor_scalar_mul(out=hv[:, 0:12], in0=vecs[:, V_CB:V_CB + 12], scalar1=0.5), reads=[vecs], writes=[hv])
    fw.op(ACT, lambda e: e.activation(out=hv[:, 20:24], in_=vecs[:, V_LAM:V_LAM + 4], func=AF.Exp, scale=-1.0), reads=[vecs, hv], writes=[hv])
    fw.op(ACT, lambda e: e.activation(out=hv[:, 20:24], in_=hv[:, 20:24], func=AF.Ln, bias=1.0), reads=[hv], writes=[hv])
    fw.op(DVE, lambda e: e.tensor_scalar_mul(out=hv[:, 12:16], in0=hv[:, 20:24], scalar1=-4.0), reads=[hv], writes=[hv])
    fw.op(DVE, lambda e: e.tensor_scalar_mul(out=hv[:, 16:20], in0=hv[:, 20:24], scalar1=-8.0), reads=[hv], writes=[hv])
    for h in range(4):
        for k in range(4):
            fw.op(DVE, lambda e: e.tensor_scalar_mul(out=diag[:, h, k, :], in0=ident[:], scalar1=vecs[:, V_CW + h * 4 + k:V_CW + h * 4 + k + 1]),
                  reads=[ident, vecs], writes=[diag])
    fw.op(POOL, lambda e: e.memset(negh[:], -0.5), writes=[negh])
    fw.op(POOL, lambda e: e.memset(posh[:], 0.5), writes=[posh])
    fw.op(DVE, lambda e: e.memset(R[:], 0.0), writes=R_h)
    fw.op(DVE, lambda e: e.memset(Rbf[:], 0.0), writes=[Rbf])
    fw.op(DVE, lambda e: e.memset(lx[0][:], 0.0), writes=[lx[0]])
    fw.op(DVE, lambda e: e.memset(lx[1][:], 0.0), writes=[lx[1]])
    for h in range(4):
        fw.op(DVE, lambda e: e.memset(hst[h][:], 0.0), writes=[hst[h]])

    cast_rr = [0]

    def cast(out_ap, in_ap, scalar_ap, reads, writes):
        k = cast_rr[0] % 2
        cast_rr[0] += 1
        if k == 0:
            if scalar_ap is None:
                fw.op(ACT, lambda e: e.activation(out=out_ap, in_=in_ap, func=AF.Copy), reads=reads, writes=writes)
            else:
                fw.op(ACT, lambda e: e.activation(out=out_ap, in_=in_ap, func=AF.Copy, scale=scalar_ap), reads=reads, writes=writes)
        else:
            eng = DVE if k == 1 else POOL
            if scalar_ap is None:
                fw.op(eng, lambda e: e.tensor_copy(out=out_ap, in_=in_ap), reads=reads, writes=writes)
            else:
                fw.op(eng, lambda e: e.tensor_scalar_mul(out=out_ap, in0=in_ap, scalar1=scalar_ap), reads=reads, writes=writes)

    win_v = win_d.rearrange("(dc p) c -> p dc c", p=128)
    for cb in (2, 3, 4, 0, 1):
        for half in range(2):
            sg_ = stg_rot.next()
            st, stv = sg_.t, sg_.v
            fw.dma(SP, sg_.schan, stv.rearrange("p (a c) -> p a c", a=4), win_v[:, 4 * half:4 * half + 4, cb * 512:(cb + 1) * 512], writes=[st])
            for dl in range(4):
                dc = 4 * half + dl
                cast(w_in_bf[:, dc, cb * 512:(cb + 1) * 512], stv[:, dl * 512:(dl + 1) * 512], vecs[:, V_NG + dc:V_NG + dc + 1], [st, vecs], [w_in_cb[cb]])
    sg_ = stg_rot.next()
    st, stv = sg_.t, sg_.v
    fw.dma(SP, sg_.schan, stv[:, 0:1024].rearrange("p (a h j) -> p a h j", a=2, h=4), wg_d, writes=[st])
    fw.op(DVE, lambda e: e.tensor_copy(out=wg_bf[:].rearrange("p a h j -> p (a h j)"), in_=stv[:, 0:1024]), reads=[st], writes=[wg_bf])
    wout_v = wout_d.rearrange("(kc p) n -> p kc n", p=128)

    def load_w_out(g):
        sg_ = stg_rot.next()
        st, stv = sg_.t, sg_.v
        fw.dma(SP, sg_.schan, stv.rearrange("p (l n) -> p l n", l=2), wout_v[:, 2 * g:2 * g + 2, :], writes=[st])
        for l in range(2):
            kc = 2 * g + l
            if kc % 8 >= 4:
                hglob = (kc // 8) * 4 + (kc % 8 - 4)
                sc_ap = vecs[:, V_RG + hglob:V_RG + hglob + 1]
            else:
                sc_ap = None
            for nh in range(2):
                cast(w_out_bf[:, kc, nh * 512:(nh + 1) * 512], stv[:, l * 1024 + nh * 512:l * 1024 + (nh + 1) * 512], sc_ap, [st, vecs], [w_out_bf])

    pid = nc.partition_id([mybir.EngineType.Pool])
    par = pid % 2

    def stage_A(s):
        xnT = xnT_rot[s % 2]
        fw.cur_prio = 2
        chunks = TILE_CHUNKS[s]
        nch = len(chunks)
        ssq = ssq_rot.next()
        rstd = rstd_rot.next()
        xs_l = []
        cst = cs_rot.next()
        cs_of_tile[s] = cst
        fw.dma(SP, c_cs[s % 2], cst[:, 0:nch, :], cs_d[:, chunks[0]:chunks[0] + nch, :], writes=[cst])
        for j, n in enumerate(chunks):
            xs = xs_rot.next()
            xs_l.append(xs)
            if n == 0:
                fw.op(POOL, lambda e: e.memset(xs[:], 0.0), writes=[xs])
                fw.dma(SP, xs.chan, xs[128 - N_META:128, :], meta_d, writes=[xs])
            else:
                fw.dma(SP, xs.chan, xs[:], x_d[(n - 1) * 128:n * 128, :], writes=[xs])
            xnj = xn_rot.tiles[(xn_rot.i + (j % 2 if nch > 1 else 0)) % 2]
            fw.op(ACT, lambda e: e.activation(out=xnj[:], in_=xs[:], func=AF.Square, accum_out=ssq[:, j:j + 1]), reads=[xs], writes=[xnj, ssq])
            if (nch > 1 and j % 2 == 1) or nch == 1:
                j0 = j - 1 if nch > 1 else 0
                fw.op(POOL, lambda e: e.tensor_scalar(out=rstd[:, j0:j + 1], in0=ssq[:, j0:j + 1], scalar1=1.0 / D_MODEL, scalar2=EPS, op0=ALU.mult, op1=ALU.add), reads=[ssq], writes=[rstd])
                fw.op(POOL, lambda e: e.tensor_tensor(out=rstd[:, j0:j + 1], in0=rstd[:, j0:j + 1], in1=negh[:, j0:j + 1], op=ALU.pow), reads=[rstd, negh], writes=[rstd])
                for jj in range(j0, j + 1):
                    xs2 = xs_l[jj]
                    xn = xn_rot.next()
                    fw.op(DVE, lambda e: e.tensor_scalar_mul(out=xn[:], in0=xs2[:], scalar1=rstd[:, jj:jj + 1]), reads=[xs2, rstd], writes=[xn])
                    ptr = p_tr.next()
                    pv = bfv(ptr)
                    for dc in range(8):
                        fw.op(PE, lambda e: e.transpose(out=pv[:, dc * 128:(dc + 1) * 128], in_=xn[:, dc * 128:(dc + 1) * 128], identity=ident[:]),
                              reads=[xn, ident], writes=[ptr], inc=(dc == 7))
                    fw.op(DVE, lambda e: e.tensor_copy(out=xnT[:, :, jj * 128:(jj + 1) * 128], in_=pv.rearrange("p (a t) -> p a t", a=8)),
                          reads=[ptr], writes=[xnT])

    def stage_B1(s):
        sg = sg_all[s % 2]
        xnT = xnT_rot[s % 2]
        fw.cur_prio = 0
        NT = 128 * len(TILE_CHUNKS[s])
        lxc = lx[s % 2]
        for ec in (4, 0, 5, 1, 6, 2, 7, 3):
            if ec >= 4 and s == 0:
                continue
            pb = p_in.next()
            for dc in range(8):
                fw.op(PE, lambda e: e.matmul(pb[:, 0:NT], lhsT=w_in_bf[:, dc, ec * 128:(ec + 1) * 128], rhs=xnT[:, dc, 0:NT], start=(dc == 0), stop=(dc == 7)),
                      reads=[w_in_cb[ec // 4], xnT], writes=[pb], inc=(dc == 7))
            if ec < 4:
                h = ec
                fw.op(DVE, lambda e: e.tensor_copy(out=lxc[:, h, 3:3 + NT], in_=pb[:, 0:NT]), reads=[pb], writes=[lxc])
            else:
                h = ec - 4
                fw.op(ACT, lambda e: e.activation(out=thg[:, 0:NT], in_=pb[:, 0:NT], func=AF.Tanh, scale=0.5), reads=[pb], writes=[thg])
                fw.op(ACT, lambda e: e.activation(out=ghalf[:, 0:NT], in_=pb[:, 0:NT], func=AF.Copy, scale=0.5), reads=[pb], writes=[ghalf])
                fw.op(DVE, lambda e: e.scalar_tensor_tensor(out=sg[h][:, 0:NT], in0=thg[:, 0:NT], scalar=1.0, in1=ghalf[:, 0:NT], op0=ALU.add, op1=ALU.mult),
                      reads=[thg, ghalf], writes=[sg[h]])

    def stage_halo(s):
        NT = 128 * len(TILE_CHUNKS[s])
        lxc = lx[s % 2]
        nxt = lx[(s + 1) % 2]
        fw.op(POOL, lambda e: e.tensor_copy(out=nxt[:, :, 0:3], in_=lxc[:, :, NT:NT + 3]), reads=[lxc], writes=[nxt])

    def stage_C(s, h):
        sg = sg_all[s % 2]
        fw.cur_prio = 0
        NT = 128 * len(TILE_CHUNKS[s])
        lxc = lx[s % 2]
        pc = p_lru.next()
        for k in range(4):
            fw.op(PE, lambda e: e.matmul(pc[:, 0:NT], lhsT=diag[:, h, k, :], rhs=lxc[:, h, k:k + NT], start=(k == 0), stop=(k == 3)),
                  reads=[diag, lxc], writes=[pc], inc=(k == 3))
        xcb_, xch_, thr_, thi_, a_ = xcb.next(), xch.next(), thr.next(), thi.next(), a_t.next()
        a2_ = thr_
        fw.op(ACT, lambda e: e.activation(out=xcb_[:, 0:NT], in_=pc[:, 0:NT], func=AF.Identity, bias=vecs[:, V_CB + h:V_CB + h + 1], scale=1.0),
              reads=[pc, vecs], writes=[xcb_])
        fw.op(ACT, lambda e: e.activation(out=xch_[:, 0:NT], in_=pc[:, 0:NT], func=AF.Identity, bias=hv[:, h:h + 1], scale=0.5),
              reads=[pc, hv], writes=[xch_])
        pr = p_lru.next()
        fw.op(PE, lambda e: e.matmul(pr[:, 0:NT], lhsT=wg_bf[:, 0, h, :], rhs=xcb_[:, 0:NT], start=True, stop=True), reads=[wg_bf, xcb_], writes=[pr])
        pi = p_lru.next()
        fw.op(PE, lambda e: e.matmul(pi[:, 0:NT], lhsT=wg_bf[:, 1, h, :], rhs=xcb_[:, 0:NT], start=True, stop=True), reads=[wg_bf, xcb_], writes=[pi])
        fw.op(ACT, lambda e: e.activation(out=thr_[:, 0:NT], in_=pr[:, 0:NT], func=AF.Tanh, bias=hv[:, 4 + h:5 + h], scale=0.5), reads=[pr, hv], writes=[thr_])
        fw.op(ACT, lambda e: e.activation(out=thi_[:, 0:NT], in_=pi[:, 0:NT], func=AF.Tanh, bias=hv[:, 8 + h:9 + h], scale=0.5), reads=[pi, hv], writes=[thi_])
        fw.op(ACT, lambda e: e.activation(out=a_[:, 0:NT], in_=thr_[:, 0:NT], func=AF.Exp, bias=hv[:, 12 + h:13 + h], scale=hv[:, 12 + h:13 + h]), reads=[thr_, hv], writes=[a_])
        fw.op(ACT, lambda e: e.activation(out=a2_[:, 0:NT], in_=thr_[:, 0:NT], func=AF.Exp, bias=hv[:, 16 + h:17 + h], scale=hv[:, 16 + h:17 + h]), reads=[thr_, hv], writes=[a2_])
        fw.op(ACT, lambda e: e.activation(out=a2_[:, 0:NT], in_=a2_[:, 0:NT], func=AF.Relu, bias=1.0, scale=-1.0), reads=[a2_], writes=[a2_])
        fw.op(ACT, lambda e: e.activation(out=a2_[:, 0:NT], in_=a2_[:, 0:NT], func=AF.Sqrt), reads=[a2_], writes=[a2_])
        fw.op(DVE, lambda e: e.scalar_tensor_tensor(out=thi_[:, 0:NT], in0=thi_[:, 0:NT], scalar=1.0, in1=xch_[:, 0:NT], op0=ALU.add, op1=ALU.mult),
              reads=[thi_, xch_], writes=[thi_])
        fw.op(DVE, lambda e: e.tensor_tensor(out=a2_[:, 0:NT], in0=a2_[:, 0:NT], in1=thi_[:, 0:NT], op=ALU.mult), reads=[a2_, thi_], writes=[a2_])
        c0 = 128 - N_META if s == 0 else 0
        fw.op(DVE, lambda e: e.tensor_tensor_scan(out=xch_[:, c0:NT], data0=a_[:, c0:NT], data1=a2_[:, c0:NT], initial=hst[h][:, 0:1], op0=ALU.mult, op1=ALU.add),
              reads=[a_, a2_, hst[h], xch_], writes=[xch_])
        fw.op(POOL, lambda e: e.tensor_copy(out=hst[h][:, 0:1], in_=xch_[:, NT - 1:NT]), reads=[xch_], writes=[hst[h]])
        if s > 0:
            fw.op(DVE, lambda e: e.tensor_tensor(out=yT[h][:, 0:NT], in0=sg[h][:, 0:NT], in1=xch_[:, 0:NT], op=ALU.mult), reads=[sg[h], xch_], writes=[yT[h]])

    def stage_D(s, j):
        xnT = xnT_rot[s % 2]
        fw.cur_prio = 0
        n = TILE_CHUNKS[s][j]
        tsl = slice(j * 128, (j + 1) * 128)

        def proj(c0):
            pb = p_in.next()
            for dc in range(8):
                fw.op(PE, lambda e: e.matmul(pb[:, :], lhsT=xnT[:, dc, tsl], rhs=w_in_bf[:, dc, c0:c0 + 512], start=(dc == 0), stop=(dc == 7)),
                      reads=[xnT, w_in_cb[c0 // 512]], writes=[pb], inc=(dc == 7))
            return pb

        p_qk = proj(1024)
        qks = qk_sb.next()
        fw.op(ACT, lambda e: e.activation(out=qks[:], in_=p_qk[:, :], func=AF.Copy), reads=[p_qk], writes=[qks])
        p_v = proj(1536)
        vb = v_bf.next()
        fw.op(DVE, lambda e: e.tensor_copy(out=vb[:], in_=p_v[:, :]), reads=[p_v], writes=[vb])
        if s > 0:
            p_g = proj(2048)
            fw.op(ACT, lambda e: e.activation(out=thrg[:], in_=p_g[:, :], func=AF.Tanh, scale=0.5), reads=[p_g], writes=[thrg])
            fw.op(ACT, lambda e: e.activation(out=rghalf[:], in_=p_g[:, :], func=AF.Copy, scale=0.5), reads=[p_g], writes=[rghalf])
            sgr_ = sgr_rot.next()
            fw.op(DVE, lambda e: e.scalar_tensor_tensor(out=sgr_[:], in0=thrg[:], scalar=1.0, in1=rghalf[:], op0=ALU.add, op1=ALU.mult),
                  reads=[thrg, rghalf], writes=[sgr_])
        qv = qks.ap.rearrange("p (g t f) -> p g t f", g=8, t=2)
        rv = qk_rot.ap.rearrange("p (g t f) -> p g t f", g=8, t=2)
        t1, t2 = qv[:, :, 0, :], qv[:, :, 1, :]
        cs = cs_of_tile[s]
        cosb = cs[:, j, 0:32].unsqueeze(1).to_broadcast([128, 8, 32])
        sinb = cs[:, j, 32:64].unsqueeze(1).to_broadcast([128, 8, 32])
        tA = tmpA.ap.rearrange("p (g f) -> p g f", g=8)
        tB = tmpB.ap.rearrange("p (g f) -> p g f", g=8)
        fw.op(POOL, lambda e: e.tensor_tensor(out=tA, in0=t1, in1=cosb, op=ALU.mult), reads=[qks, cs], writes=[tmpA])
        fw.op(POOL, lambda e: e.tensor_tensor(out=tB, in0=t2, in1=sinb, op=ALU.mult), reads=[qks, cs], writes=[tmpB])
        fw.op(POOL, lambda e: e.tensor_tensor(out=rv[:, :, 0, :], in0=tA, in1=tB, op=ALU.subtract), reads=[tmpA, tmpB], writes=[qk_rot])
        fw.op(POOL, lambda e: e.tensor_tensor(out=tA, in0=t1, in1=sinb, op=ALU.mult), reads=[qks, cs], writes=[tmpA])
        fw.op(POOL, lambda e: e.tensor_tensor(out=tB, in0=t2, in1=cosb, op=ALU.mult), reads=[qks, cs], writes=[tmpB])
        fw.op(POOL, lambda e: e.tensor_tensor(out=rv[:, :, 1, :], in0=tA, in1=tB, op=ALU.add), reads=[tmpA, tmpB], writes=[qk_rot])
        qd_, kd_, kdec_ = qd.next(), kd.next(), kdec.next()
        qr = qk_rot.ap[:, 0:256].rearrange("p (h f) -> p h f", h=4)
        kr = qk_rot.ap[:, 256:512].rearrange("p (h f) -> p h f", h=4)

        def dcol(c):
            return dec[:, c:c + 4].unsqueeze(2).to_broadcast([128, 4, 64])

        fw.op(POOL, lambda e: e.tensor_tensor(out=qd_[:], in0=qr, in1=dcol(DC_Q), op=ALU.mult), reads=[qk_rot, dec], writes=[qd_])
        fw.op(POOL, lambda e: e.tensor_tensor(out=kd_[:], in0=kr, in1=dcol(DC_KI), op=ALU.mult), reads=[qk_rot, dec], writes=[kd_])
        fw.op(POOL, lambda e: e.tensor_tensor(out=kdec_[:], in0=kr, in1=dcol(DC_KD), op=ALU.mult), reads=[qk_rot, dec], writes=[kdec_])
        ptr = p_tr.next()
        pv = bfv(ptr)[0:64, :].rearrange("p (a t) -> p a t", a=8)
        for h in range(4):
            fw.op(PE, lambda e: e.transpose(out=pv[:, h, :], in_=qd_[:, h, :], identity=ident[:]), reads=[qd_, ident], writes=[ptr], inc=False)
        for h in range(4):
            fw.op(PE, lambda e: e.transpose(out=pv[:, 4 + h, :], in_=kd_[:, h, :], identity=ident[:]), reads=[kd_, ident], writes=[ptr], inc=(h == 3))
        qkT_ = qkT.next()
        fw.op(DVE, lambda e: e.tensor_copy(out=qkT_[:], in_=pv), reads=[ptr], writes=[qkT_])
        ps = p_ret.next()
        psv = ps.ap.rearrange("p (h c) -> p h c", h=4)
        for h in range(4):
            fw.op(PE, lambda e: e.matmul(psv[:, h, :], lhsT=qkT_[:, 4 + h, :], rhs=qkT_[:, h, :], start=True, stop=True), reads=[qkT_], writes=[ps], inc=(h == 3))
        sm_ = sm.next()
        fw.op(DVE, lambda e: e.tensor_tensor(out=sm_[:], in0=psv, in1=maskT[:].unsqueeze(1).to_broadcast([128, 4, 128]), op=ALU.mult), reads=[ps, maskT], writes=[sm_])
        po = p_ret.next()
        pov = po.ap.rearrange("p (h c) -> p h c", h=4)
        for h in range(4):
            fw.op(PE, lambda e: e.matmul(pov[:, h, :], lhsT=qkT_[:, h, :], rhs=Rbf[:, h, :], start=True, stop=False), reads=[qkT_, Rbf], writes=[po], inc=False)
            fw.op(PE, lambda e: e.matmul(pov[:, h, :], lhsT=sm_[:, h, :], rhs=vb[:, h * 128:(h + 1) * 128], start=False, stop=True), reads=[sm_, vb], writes=[po], inc=(h == 3))
        pk = p_ret.next()
        pkv = pk.ap[0:64, :].rearrange("p (h c) -> p h c", h=4)
        for h in range(4):
            fw.op(PE, lambda e: e.matmul(pkv[:, h, :], lhsT=kdec_[:, h, :], rhs=vb[:, h * 128:(h + 1) * 128], start=True, stop=True), reads=[kdec_, vb], writes=[pk], inc=(h == 3))
        if s > 0:
            o_ = o_sb_rot.next()
            mv_ = mv_rot.next()
            rs_ = rs_rot.next()
            fw.op(DVE, lambda e: e.tensor_copy(out=o_[:], in_=pov), reads=[po], writes=[o_])
            for h in range(4):
                fw.op(DVE, lambda e: e.bn_stats(out=bst[:, h, :], in_=o_[:, h, :]), reads=[o_], writes=[bst_h[h]])
            for h in range(4):
                fw.op(DVE, lambda e: e.bn_aggr(out=mv_[:, h, :], in_=bst[:, h, :]), reads=[bst_h[h]], writes=[mv_.parts[h]])
        for h in range(4):
            fw.op(DVE, lambda e: e.scalar_tensor_tensor(out=R[:, h, :], in0=R[:, h, :], scalar=dec[0:64, DC_G128 + h:DC_G128 + h + 1], in1=pkv[:, h, :], op0=ALU.mult, op1=ALU.add),
                  reads=[R_h[h], dec, pk], writes=[R_h[h]])
        fw.op(DVE, lambda e: e.tensor_copy(out=Rbf[:], in_=R[:]), reads=R_h, writes=[Rbf])
        if s > 0:
            fw.op(POOL, lambda e: e.tensor_scalar(out=rs_[:], in0=mv_[:, :, 1], scalar1=1.0, scalar2=EPS, op0=ALU.mult, op1=ALU.add), reads=mv_.parts, writes=[rs_])
            fw.op(POOL, lambda e: e.tensor_tensor(out=rs_[:], in0=rs_[:], in1=negh[:, 0:4], op=ALU.pow), reads=[rs_, negh], writes=[rs_])
            mean_b = mv_[:, :, 0:1].to_broadcast([128, 4, 128])
            rs_b = rs_[:].unsqueeze(2).to_broadcast([128, 4, 128])
            fw.op(POOL, lambda e: e.tensor_tensor(out=o_[:], in0=o_[:], in1=mean_b, op=ALU.subtract), reads=[o_] + mv_.parts, writes=[o_])
            fw.op(POOL, lambda e: e.tensor_tensor(out=on[:], in0=o_[:], in1=rs_b, op=ALU.mult), reads=[o_, rs_], writes=[on])
            yr_ = yr.next()
            fw.op(DVE, lambda e: e.tensor_tensor(out=yr_[:].rearrange("p h c -> p (h c)"), in0=on[:].rearrange("p h c -> p (h c)"), in1=sgr_[:], op=ALU.mult),
                  reads=[on, sgr_], writes=[yr_])
            ptr2 = p_tr.next()
            pv2 = bfv(ptr2)[:, 0:512].rearrange("p (h t) -> p h t", h=4)
            for h in range(4):
                fw.op(PE, lambda e: e.transpose(out=pv2[:, h, :], in_=yr_[:, h, :], identity=ident[:]), reads=[yr_, ident], writes=[ptr2], inc=(h == 3))
            fw.op(DVE, lambda e: e.tensor_copy(out=yT_all[:, 4:8, j * 128:(j + 1) * 128], in_=pv2), reads=[ptr2], writes=yT[4:8])

    def stage_E1(s):
        fw.cur_prio = 0
        fw.dma(SP, c_ysrc, ysrc_d[s].ap().rearrange("(k p) t -> p k t", p=128), yT_all[:, :, 0:TILE_NT[s]], reads=yT, writes=[ysrc_t[s]])
        fw.custom(POOL, c_cc, lambda e: e.collective_compute("AllGather", ALU.bypass, replica_groups=[[0, 1], [2, 3], [4, 5], [6, 7]],
                                                              ins=[ysrc_d[s].ap().opt()], outs=[ydst_d[s].ap().opt()]),
                  reads=[ysrc_t[s]], writes=[ydst_t[s]])

    def stage_E2(s):
        NH = TILE_NT[s] // 2
        nj2 = NH // 128
        off = TILE_T0[s] // 2
        fw.dma(POOL, c_yfull, yfull[:, :, 0:NH], ydst_d[s].ap().rearrange("(k p) t -> p k t", p=128)[:, :, bass.ds(par * NH, NH)], reads=[ydst_t[s]], writes=[yfull])
        xr = xr_rot.next()
        fw.dma(SP, xr.chan, xr[:, 0:nj2, :], xres_d[off:off + NH, :].rearrange("(j p) d -> p j d", p=128), writes=[xr])
        for j2 in range(nj2):
            for nh in range(2):
                pb = p_out.next()
                for kc in range(16):
                    fw.op(PE, lambda e: e.matmul(pb[:, :], lhsT=yfull[:, kc, j2 * 128:(j2 + 1) * 128], rhs=w_out_bf[:, kc, nh * 512:(nh + 1) * 512], start=(kc == 0), stop=(kc == 15)),
                          reads=[yfull, w_out_bf], writes=[pb], inc=(kc == 15))
                fw.op(DVE, lambda e: e.tensor_tensor(out=xr[:, j2, nh * 512:(nh + 1) * 512], in0=xr[:, j2, nh * 512:(nh + 1) * 512], in1=pb[:, :], op=ALU.add),
                      reads=[xr, pb], writes=[xr])
                fw.op(ACT, lambda e: e.activation(out=pb[:, :], in_=xr[:, j2, nh * 512:(nh + 1) * 512], func=AF.Square, accum_out=fss4[:, j2 * 2 + nh:j2 * 2 + nh + 1]),
                      reads=[xr], writes=[pb, fss4])
        f4 = fss4.ap.rearrange("p (j n) -> p j n", n=2)
        fw.op(POOL, lambda e: e.tensor_tensor(out=frs[:, 0:nj2], in0=f4[:, 0:nj2, 0], in1=f4[:, 0:nj2, 1], op=ALU.add), reads=[fss4], writes=[frs])
        fw.op(POOL, lambda e: e.tensor_scalar(out=frs[:, 0:nj2], in0=frs[:, 0:nj2], scalar1=1.0 / D_MODEL, scalar2=EPS, op0=ALU.mult, op1=ALU.add), reads=[frs], writes=[frs])
        fw.op(POOL, lambda e: e.tensor_tensor(out=frs[:, 0:nj2], in0=frs[:, 0:nj2], in1=negh[:, 0:nj2], op=ALU.pow), reads=[frs, negh], writes=[frs])
        for j2 in range(nj2):
            fw.op(DVE, lambda e: e.scalar_tensor_tensor(out=xr[:, j2, :], in0=xr[:, j2, :], scalar=frs[:, j2:j2 + 1], in1=fgain[:], op0=ALU.mult, op1=ALU.mult),
                  reads=[xr, frs, fgain], writes=[xr])
        fw.dma(SP, c_out, out_d[off:off + NH, :].rearrange("(j p) d -> p j d", p=128), xr[:, 0:nj2, :], reads=[xr], writes=[out_t])

    marks = {}
    stage_A(0)
    for g in range(8):
        load_w_out(g)
    marks["A0"] = fw.ops[-1]
    for s in range(NTILES):
        stage_B1(s)
        if s + 1 < NTILES:
            stage_A(s + 1)
        nch = len(TILE_CHUNKS[s])
        hpc = 4 // nch
        for j in range(nch):
            stage_D(s, j)
            for h in range(j * hpc, (j + 1) * hpc):
                stage_C(s, h)
        stage_halo(s)
        if s > 0:
            stage_E1(s)
        if s > 0:
            stage_E2(s)
        marks[f"t{s}"] = fw.ops[-1]
    fw.finish(POOL, [out_t])
    fw.finish(SP, [out_t])
    best = None
    for trial in (range(N_SCHED_TRIALS) if FORCE_TRIAL is None else [FORCE_TRIAL]):
        if trial == 0:
            order = fw.schedule()
        else:
            order = fw.schedule(seed=trial, noise=0.02 * (1 + trial % 5), win=(0.1e-6, 0.3e-6, 0.6e-6)[trial % 3])
        if best is None or fw.sim_end < best[0]:
            best = (fw.sim_end, list(order), fw.n_switch, trial)
    fw.sim_end, order, fw.n_switch, best_trial = best
    fw.emit(order)
    stats = {e.name: (e.nins, e.nwaits) for e in (PE, ACT, DVE, POOL, SP)}
    stats["fw"] = fw
    stats["order"] = order
    stats["sim_end_us"] = fw.sim_end * 1e6
    stats["n_switch"] = fw.n_switch
    stats["best_trial"] = best_trial
    stats["sbuf_bytes"] = sb_bytes[0] + 8192
    stats["marks"] = {k: round(v.fin * 1e6, 1) for k, v in marks.items()}
    stats["sim_busy_us"] = {k: round(v * 1e6, 1) for k, v in fw.sim_busy.items()}
    return nc, stats


def _consts():
    half = 32
    inv = np.power(np.float64(10000.0), -(np.arange(half, dtype=np.float64) / np.float64(half)))
    c = np.arange(128)[:, None]
    n = np.arange(NCHUNK)[None, :]
    pos = np.maximum(n * 128 + c - (128 - N_META), 0).astype(np.float64)
    ang = pos[:, :, None] * inv[None, None, :]
    cs = np.concatenate([np.cos(ang), np.sin(ang)], axis=-1).astype(np.float32)
    ident = np.eye(128, dtype=np.float32).astype(ml_dtypes.bfloat16)
    m = np.arange(128)[:, None]
    cc = np.arange(128)[None, :]
    maskT = (cc >= m).astype(np.float32).astype(ml_dtypes.bfloat16)
    return cs, ident, maskT


def _dec_table(heads):
    log_g = np.log1p(-np.exp2(-5.0 - np.arange(8, dtype=np.float64)))
    c = np.arange(128, dtype=np.float64)[:, None]
    lg = log_g[heads][None, :]
    dec = np.zeros((128, NDEC), np.float32)
    dec[:, DC_Q:DC_Q + 4] = np.exp((c + 1.0) * lg)
    dec[:, DC_KI:DC_KI + 4] = np.exp(-(c + 1.0) * lg) * 0.125
    dec[:, DC_KD:DC_KD + 4] = np.exp((127.0 - c) * lg) * 0.125
    dec[:, DC_G128:DC_G128 + 4] = np.exp(128.0 * lg)
    return dec


_PROG = None


def kernel(x, meta_tokens, norm_gain, w_in, conv_w, conv_b, w_rg, b_rg, w_ig, b_ig,
           lru_lambda, ret_norm_gain, w_out, final_norm_gain):
    global _PROG
    f = lambda a: np.ascontiguousarray(np.asarray(a), dtype=np.float32)
    x, meta_tokens, norm_gain, w_in = f(x), f(meta_tokens), f(norm_gain), f(w_in)
    conv_w, conv_b, w_rg, b_rg, w_ig, b_ig = f(conv_w), f(conv_b), f(w_rg), f(b_rg), f(w_ig), f(b_ig)
    lru_lambda, ret_norm_gain, w_out, final_norm_gain = f(lru_lambda), f(ret_norm_gain), f(w_out), f(final_norm_gain)
    if _PROG is None:
        _PROG = build_program()
    nc = _PROG[0]
    cs, ident, maskT = _consts()
    w_in0 = w_in[0]
    perm = np.concatenate([np.arange(0, 512), np.arange(1024, 1536), np.arange(512, 1024), np.arange(1536, 2048)])
    w_out_p = np.ascontiguousarray(w_out[0][perm])
    fgain = np.ascontiguousarray(np.broadcast_to(final_norm_gain[None, :], (128, D_MODEL)))
    in_maps = []
    for c in range(8):
        b, p = c // 2, c % 2
        lch = np.arange(512 * p, 512 * p + 512)
        heads = np.arange(4 * p, 4 * p + 4)
        qk_cols = (heads[:, None] * 64 + np.arange(64)[None, :]).reshape(-1)
        v_cols = (heads[:, None] * 128 + np.arange(128)[None, :]).reshape(-1)
        cols = np.concatenate([lch, 1024 + lch, 2048 + qk_cols, 2560 + qk_cols, 3072 + v_cols, 4096 + v_cols])
        w_in_c = np.ascontiguousarray(w_in0[:, cols])
        wg = np.stack([w_rg[0, 4 * p:4 * p + 4], w_ig[0, 4 * p:4 * p + 4]], 0)
        wg = np.ascontiguousarray(wg.transpose(2, 0, 1, 3))
        vecs = np.zeros((128, NV), np.float32)
        for h in range(4):
            ch = 512 * p + h * 128 + np.arange(128)
            for k in range(4):
                vecs[:, V_CW + h * 4 + k] = conv_w[0, k, ch]
            vecs[:, V_CB + h] = conv_b[0, ch]
            vecs[:, V_BRG + h] = b_rg[0, ch]
            vecs[:, V_BIG + h] = b_ig[0, ch]
            vecs[:, V_LAM + h] = lru_lambda[0, ch]
        for dc in range(8):
            vecs[:, V_NG + dc] = norm_gain[0, dc * 128:(dc + 1) * 128]
        for h in range(8):
            vecs[:, V_RG + h] = ret_norm_gain[0, h * 128:(h + 1) * 128]
        xb = x[b]
        xres = np.ascontiguousarray(np.concatenate([xb[TILE_T0[t_] + p * (TILE_NT[t_] // 2):TILE_T0[t_] + (p + 1) * (TILE_NT[t_] // 2)] for t_ in range(1, NTILES)], 0))
        in_maps.append({
            "x": xb, "meta": meta_tokens, "xres": xres, "w_in": w_in_c, "w_out": w_out_p, "w_g": wg,
            "vecs": vecs, "fgain": fgain, "ident": ident, "cs": cs, "dec": _dec_table(heads), "maskT": maskT,
        })
    res = run_bass_kernel_spmd(nc, in_maps, core_ids=list(range(8)))
    out = np.zeros((BATCH, SEQ, D_MODEL), np.float32)
    for c in range(8):
        b, p = c // 2, c % 2
        o = np.asarray(res.results[c]["out"])
        for t_ in range(1, NTILES):
            nh_ = TILE_NT[t_] // 2
            out[b, TILE_T0[t_] + p * nh_:TILE_T0[t_] + (p + 1) * nh_] = o[TILE_T0[t_] // 2:TILE_T0[t_] // 2 + nh_]
    return out
```
